# Optimizing a Trainium2 kernel written in Bass

```python
import jax, jax.numpy as jnp
from jax import lax
import numpy as np

D_MODEL = 1024
BATCH = 4
SEQ = 8192
DEPTH = 2

D_FF = 2816
HG_DK = 128
HG_HEADS = D_MODEL // HG_DK
HG_DV = D_MODEL // HG_HEADS
HG_WK = HG_HEADS * HG_DK
HG_WV = HG_HEADS * HG_DV
HG_CHUNK = 64
ATT_PATTERNS = ((128, 1), (512, 4), (2048, 16))
ATT_GROUPS = 3
ATT_HEADS = 4
ATT_DH = 128
ATT_W = ATT_GROUPS * ATT_HEADS * ATT_DH
ATT_OUT = ATT_HEADS * ATT_DH
ROPE_THETA = 10000.0
EPS = 1e-6
SPLIT_SIZES = (HG_WK, HG_WK, HG_WV, HG_WV, ATT_W, ATT_W, ATT_W, D_MODEL, D_MODEL)
P_IN = sum(SPLIT_SIZES)

kernel_name = "hybrid_hgrn2_dilated_attn_macaron"


def rms(x):
    xf = x.astype(jnp.float32)
    return xf * lax.rsqrt(jnp.mean(xf * xf, axis=-1, keepdims=True) + EPS)


def rmsnorm(x, g):
    return (rms(x) * g.astype(jnp.float32)).astype(x.dtype)


def swiglu(h, w_in, w_out):
    a, b = jnp.split(h @ w_in, 2, axis=-1)
    return (jax.nn.silu(a) * b) @ w_out


def rope_tables(t):
    pos = jnp.arange(t, dtype=jnp.float32)
    inv = ROPE_THETA ** (-jnp.arange(0, ATT_DH, 2, dtype=jnp.float32) / ATT_DH)
    ang = pos[:, None] * inv[None, :]
    ang = jnp.concatenate([ang, ang], axis=-1)
    return jnp.cos(ang), jnp.sin(ang)


def apply_rope(x, cos, sin):
    x1, x2 = jnp.split(x, 2, axis=-1)
    return x * cos + jnp.concatenate([-x2, x1], axis=-1) * sin


def hgrn2_chunk_scan(q, k, v, log_f):
    b, t, h, dk = q.shape
    dv = v.shape[-1]
    n = t // HG_CHUNK

    def chunks(a):
        return a.reshape(b, n, HG_CHUNK, h, a.shape[-1]).transpose(1, 0, 3, 2, 4)

    causal = jnp.tril(jnp.ones((HG_CHUNK, HG_CHUNK), dtype=bool))[:, :, None]

    def step(state, inp):
        qc, kc, vc, gc = inp
        gcum = jnp.cumsum(gc, axis=2)
        diff = gcum[:, :, :, None, :] - gcum[:, :, None, :, :]
        decay = jnp.exp(jnp.where(causal, diff, -jnp.inf))
        attn = jnp.einsum('bhtk,bhsk,bhtsk->bhts', qc, kc, decay)
        o = jnp.einsum('bhts,bhsv->bhtv', attn, vc) + jnp.einsum(
            'bhtk,bhkv->bhtv', qc * jnp.exp(gcum), state)
        g_last = gcum[:, :, -1:, :]
        state = jnp.exp(g_last[:, :, 0, :, None]) * state + jnp.einsum(
            'bhsk,bhsv->bhkv', kc * jnp.exp(g_last - gcum), vc)
        return state, o

    s0 = jnp.zeros((b, h, dk, dv), jnp.float32)
    _, o = lax.scan(step, s0, (chunks(q), chunks(k), chunks(v), chunks(log_f)))
    return o.transpose(1, 0, 3, 2, 4).reshape(b, t, h * dv)


def dilated_window_attention(q, k, v, window, dilation):
    b, h, t, dh = q.shape
    back = window // dilation
    blk = back
    L = t // dilation
    nb = -(-L // blk)
    Lp = nb * blk

    def to_res(a):
        a = a.reshape(b, h, L, dilation, dh).transpose(0, 1, 3, 2, 4)
        return jnp.pad(a, ((0, 0), (0, 0), (0, 0), (0, Lp - L), (0, 0)))

    def kv_blocks(a):
        a = jnp.pad(a, ((0, 0), (0, 0), (0, 0), (blk, 0), (0, 0)))
        a = a.reshape(b, h, dilation, nb + 1, blk, dh)
        return jnp.concatenate([a[:, :, :, :-1], a[:, :, :, 1:]], axis=4)

    qb = to_res(q).reshape(b, h, dilation, nb, blk, dh)
    kb = kv_blocks(to_res(k))
    vb = kv_blocks(to_res(v))
    s = jnp.einsum('bhrnqd,bhrnkd->bhrnqk', qb, kb) * (dh ** -0.5)
    qi = jnp.arange(blk)[:, None]
    ki = jnp.arange(2 * blk)[None, :]
    band = (ki >= qi) & (ki <= qi + back)
    valid = (ki >= blk)[None] | (jnp.arange(nb)[:, None, None] > 0)
    mask = band[None] & valid
    s = jnp.where(mask, s, -jnp.inf)
    lse = jax.nn.logsumexp(s, axis=-1)
    p = jnp.exp(s - lse[..., None])
    o = jnp.einsum('bhrnqk,bhrnkd->bhrnqd', p, vb)
    o = o.reshape(b, h, dilation, Lp, dh)[:, :, :, :L].transpose(0, 1, 3, 2, 4)
    lse = lse.reshape(b, h, dilation, Lp)[:, :, :, :L].transpose(0, 1, 3, 2)
    return o.reshape(b, h, t, dh), lse.reshape(b, h, t)


def setup_inputs(seed: int = 0) -> dict:
    key = jax.random.key(seed)
    ks = jax.random.split(key, 16)
    f32 = jnp.float32

    def nrm(k, shape, fan_in):
        return jax.random.normal(k, shape, f32) * (fan_in ** -0.5)

    def gain(k, shape):
        return 1.0 + 0.05 * jax.random.normal(k, shape, f32)

    return {
        "x": jax.random.normal(ks[0], (BATCH, SEQ, D_MODEL), f32),
        "ffn1_norm": gain(ks[1], (DEPTH, D_MODEL)),
        "ffn1_w_in": nrm(ks[2], (DEPTH, D_MODEL, 2 * D_FF), D_MODEL),
        "ffn1_w_out": nrm(ks[3], (DEPTH, D_FF, D_MODEL), D_FF),
        "mix_norm": gain(ks[4], (DEPTH, D_MODEL)),
        "w_in": nrm(ks[5], (DEPTH, D_MODEL, P_IN), D_MODEL),
        "hgrn_lb_logits": 0.5 * jax.random.normal(ks[6], (DEPTH, HG_WK), f32),
        "hgrn_out_norm": gain(ks[7], (DEPTH, HG_WV)),
        "attn_q_norm": gain(ks[8], (DEPTH, ATT_GROUPS, ATT_DH)),
        "attn_k_norm": gain(ks[9], (DEPTH, ATT_GROUPS, ATT_DH)),
        "w_branch_a": nrm(ks[10], (DEPTH, HG_WV, D_MODEL), HG_WV),
        "w_branch_b": nrm(ks[11], (DEPTH, ATT_OUT, D_MODEL), ATT_OUT),
        "w_out": nrm(ks[12], (DEPTH, D_MODEL, D_MODEL), D_MODEL),
        "ffn2_norm": gain(ks[13], (DEPTH, D_MODEL)),
        "ffn2_w_in": nrm(ks[14], (DEPTH, D_MODEL, 2 * D_FF), D_MODEL),
        "ffn2_w_out": nrm(ks[15], (DEPTH, D_FF, D_MODEL), D_FF),
    }


def reference(x, ffn1_norm, ffn1_w_in, ffn1_w_out, mix_norm, w_in, hgrn_lb_logits,
              hgrn_out_norm, attn_q_norm, attn_k_norm, w_branch_a, w_branch_b, w_out,
              ffn2_norm, ffn2_w_in, ffn2_w_out):
    b, t, _ = x.shape
    f32 = jnp.float32
    cos, sin = rope_tables(t)
    lb_all = jnp.cumsum(jax.nn.softmax(hgrn_lb_logits.astype(f32), axis=0), axis=0)
    lb_all = lb_all - lb_all[0:1]
    split_idx = [int(s) for s in np.cumsum(SPLIT_SIZES)[:-1]]

    for l in range(DEPTH):
        x = x + 0.5 * swiglu(rmsnorm(x, ffn1_norm[l]), ffn1_w_in[l], ffn1_w_out[l])

        h = rmsnorm(x, mix_norm[l])
        hq, hf, hi, hg, aq, ak, av, ga, gb = jnp.split(h @ w_in[l], split_idx, axis=-1)

        lb = lb_all[l]
        f = lb + (1.0 - lb) * jax.nn.sigmoid(hf.astype(f32))
        q_a = jax.nn.silu(hq.astype(f32)).reshape(b, t, HG_HEADS, HG_DK)
        k_a = (1.0 - f).reshape(b, t, HG_HEADS, HG_DK)
        v_a = hi.astype(f32).reshape(b, t, HG_HEADS, HG_DV)
        log_f = jnp.log(f).reshape(b, t, HG_HEADS, HG_DK)
        o_a = hgrn2_chunk_scan(q_a, k_a, v_a, log_f)
        o_a = rms(o_a.reshape(b, t, HG_HEADS, HG_DV)).reshape(b, t, HG_WV)
        o_a = o_a * hgrn_out_norm[l].astype(f32) * jax.nn.silu(hg.astype(f32))
        y_a = o_a.astype(x.dtype) @ w_branch_a[l]

        def heads(a):
            return a.reshape(b, t, ATT_GROUPS, ATT_HEADS, ATT_DH).transpose(2, 0, 3, 1, 4).astype(f32)

        qn = attn_q_norm[l].astype(f32)[:, None, None, None, :]
        kn = attn_k_norm[l].astype(f32)[:, None, None, None, :]
        q_b = apply_rope(rms(heads(aq)) * qn, cos, sin)
        k_b = apply_rope(rms(heads(ak)) * kn, cos, sin)
        v_b = heads(av)
        outs, lses = [], []
        for g, (window, dilation) in enumerate(ATT_PATTERNS):
            o_g, lse_g = dilated_window_attention(q_b[g], k_b[g], v_b[g], window, dilation)
            outs.append(o_g)
            lses.append(lse_g)
        alpha = jax.nn.softmax(jnp.stack(lses, axis=0), axis=0)
        o_b = jnp.einsum('gbht,gbhtd->bthd', alpha, jnp.stack(outs, axis=0)).reshape(b, t, ATT_OUT)
        y_b = o_b.astype(x.dtype) @ w_branch_b[l]

        merged = jax.nn.sigmoid(ga) * y_a + jax.nn.sigmoid(gb) * y_b
        x = x + merged @ w_out[l]

        x = x + 0.5 * swiglu(rmsnorm(x, ffn2_norm[l]), ffn2_w_in[l], ffn2_w_out[l])
    return x
```

```python
import contextlib
import numpy as np
import concourse.bass as bass
import concourse.mybir as mybir
from concourse.bass_utils import run_bass_kernel_spmd

F32 = mybir.dt.float32
BF16 = mybir.dt.bfloat16
AF = mybir.ActivationFunctionType
ALU = mybir.AluOpType
AX = mybir.AxisListType

D = 1024
DFF = 2816
P_IN = 10752
EPS = 1e-6
NCORES = 8


class Buf:
    __slots__ = ("w", "r", "name")

    def __init__(self, name=""):
        self.w = None
        self.r = {}
        self.name = name


class Sched:
    def __init__(self, nc, stack, ndma=6):
        self.nc = nc
        self.h = {"pe": nc.tensor, "act": nc.scalar, "dve": nc.vector,
                  "pool": nc.gpsimd, "sp": nc.sync}
        self.sems = {}
        self.cnt = {}
        for e in ("pe", "act", "dve", "pool"):
            self.sems[e] = stack.enter_context(nc.semaphore("s_" + e))
            self.cnt[e] = 0
        self.seen = {e: {} for e in self.h}
        self.dq = {}
        for q in ("sp", "pool", "act"):
            names = ["d_%s%d" % (q, i) for i in range(ndma)]
            for n in names:
                self.sems[n] = stack.enter_context(nc.semaphore(n))
                self.cnt[n] = 0
            self.dq[q] = [names, 0]
        self.ninst = 0

    def _wait(self, e, deps):
        need = {}
        for d in deps:
            if d is None:
                continue
            k, v, src = d
            if src == e and e == "pe":
                continue
            if self.seen[e].get(k, 0) >= v:
                continue
            if need.get(k, 0) < v:
                need[k] = v
        for k, v in need.items():
            self.h[e].wait_ge(self.sems[k], v)
            self.seen[e][k] = v
            self.ninst += 1

    def _deps(self, e, reads, writes):
        deps = []
        for b in reads:
            deps.append(b.w)
        for b in writes:
            deps.append(b.w)
            for r in b.r.values():
                if r[2] == e and e != "pool":
                    continue
                deps.append(r)
        return deps

    def _record(self, ev, reads, writes):
        for b in reads:
            old = b.r.get(ev[0])
            if old is None or old[1] < ev[1]:
                b.r[ev[0]] = ev
        for b in writes:
            b.w = ev
            b.r = {}

    def op(self, e, fn, reads=(), writes=(), inc=True):
        self._wait(e, self._deps(e, reads, writes))
        ins = fn(self.h[e])
        if inc:
            self.cnt[e] += 1
            ins.then_inc(self.sems[e], 1)
            ev = (e, self.cnt[e], e)
        else:
            ev = (e, self.cnt[e] + 1, e)
        self.ninst += 1
        self._record(ev, reads, writes)
        return ev

    def dma(self, q, out, in_, reads=(), writes=(), **kw):
        names, idx = self.dq[q]
        k = names[idx % len(names)]
        self.dq[q][1] = idx + 1
        deps = self._deps("dma_issue", reads, writes)
        if self.cnt[k] > 0:
            deps.append((k, self.cnt[k], "dma"))
        self._wait(q, deps)
        ins = self.h[q].dma_start(out=out, in_=in_, **kw)
        self.cnt[k] += 16
        ins.then_inc(self.sems[k], 16)
        ev = (k, self.cnt[k], "dma")
        self.ninst += 1
        self._record(ev, reads, writes)
        return ev

    def finish(self, e="sp"):
        for k, c in self.cnt.items():
            if c > 0 and self.seen[e].get(k, 0) < c:
                self.h[e].wait_ge(self.sems[k], c)
                self.seen[e][k] = c


class Ctx:
    pass


def sb(cx, name, shape, dt):
    t = cx.stack.enter_context(cx.nc.sbuf_tensor("sb_" + name, list(shape), dt))
    return t


def ps(cx, name, shape, dt):
    t = cx.stack.enter_context(cx.nc.psum_tensor("ps_" + name, list(shape), dt))
    return t


def emit_norm_T(cx, S, xt, xb, gain, hT, hTb, col0, tag, fixed=None):
    nc = cx.nc
    i = cx.rr % 2 if fixed is None else fixed
    cx.rr += 1
    junk, junkb = cx.junk[i], cx.junkb[i]
    ss, ssb = cx.ss[i], cx.ssb[i]
    hb, hbb = cx.hbf[i], cx.hbfb[i]
    pT, pTb = cx.psT[i], cx.psTb[i]
    S.op("act", lambda e: e.activation(out=junk[:], in_=xt[:], func=AF.Square,
                                       accum_out=ss[:, 0:1]),
         reads=[xb], writes=[junkb, ssb])
    S.op("act", lambda e: e.activation(out=ss[:, 1:2], in_=ss[:, 0:1], func=AF.Sqrt,
                                       scale=1.0 / D, bias=EPS),
         reads=[ssb], writes=[ssb])
    S.op("dve", lambda e: e.reciprocal(out=ss[:, 2:3], in_=ss[:, 1:2]),
         reads=[ssb], writes=[ssb])
    S.op("dve", lambda e: e.scalar_tensor_tensor(out=hb[:], in0=xt[:], scalar=ss[:, 2:3],
                                                 in1=gain[:], op0=ALU.mult, op1=ALU.mult),
         reads=[xb, ssb, cx.constb], writes=[hbb])
    for k in range(8):
        S.op("pe", lambda e, k=k: e.transpose(pT[:, k * 128:(k + 1) * 128],
                                              hb[:, k * 128:(k + 1) * 128], cx.ident[:]),
             reads=[hbb, cx.constb], writes=[pTb], inc=(k == 7))
    S.op("act", lambda e: e.copy(out=hT[:, :, col0:col0 + 128],
                                 in_=pT[:].rearrange("p (k t) -> p k t", k=8)),
         reads=[pTb], writes=[hTb])


def load_weight_bf16(cx, S, dst, dstb, src_ap, nk, ncols, c0=0, c1=None, cstep=1408):
    c1 = ncols if c1 is None else c1
    v = src_ap.rearrange("(k p) c -> p k c", p=128)
    c = c0
    while c < c1:
        ce = min(c1, c + cstep)
        for k in range(nk):
            S.dma("pool", dst[:, k, c - c0:ce - c0], v[:, k, c:ce], writes=[dstb])
        c = ce


def phase_ffn(cx, S, xin, xout, g_ap, w_in_ap, w_out_ap, T, tag):
    nc = cx.nc
    G = 256
    with contextlib.ExitStack() as st:
        old = cx.stack
        cx.stack = st
        w1 = sb(cx, tag + "w1", [128, 8, 2 * DFF], BF16)
        w2 = sb(cx, tag + "w2", [128, 22, D], BF16)
        gain = sb(cx, tag + "g", [128, D], F32)
        xts = [sb(cx, tag + "xt%d" % i, [128, D], F32) for i in range(4)]
        hT = [sb(cx, tag + "hT%d" % i, [128, 8, G], BF16) for i in range(2)]
        uT = sb(cx, tag + "uT", [128, 22, G], BF16)
        sa = [sb(cx, tag + "sa%d" % i, [128, G], F32) for i in range(2)]
        w1b, w2b, uTb = Buf(), Buf(), Buf()
        xtb = [Buf() for _ in range(4)]
        hTb = [Buf() for _ in range(2)]
        sab = [Buf() for _ in range(2)]
        S.dma("sp", gain[:], g_ap.partition_broadcast(128), writes=[cx.constb])
        load_weight_bf16(cx, S, w1, w1b, w_in_ap, 8, 2 * DFF)
        load_weight_bf16(cx, S, w2, w2b, w_out_ap, 22, D, cstep=1024)
        ng = T // G
        for gi in range(ng):
            hTg, hTgb = hT[gi % 2], hTb[gi % 2]
            for i in range(2):
                xt, xb = xts[(gi % 2) * 2 + i], xtb[(gi % 2) * 2 + i]
                r0 = gi * G + i * 128
                S.dma("sp", xt[:], xin[r0:r0 + 128, :], writes=[xb])
                emit_norm_T(cx, S, xt, xb, gain, hTg, hTgb, i * 128, tag)
            for j in range(22):
                pA, pAb = cx.psA[j % 2], cx.psAb[j % 2]
                pB, pBb = cx.psB[j % 2], cx.psBb[j % 2]
                for k in range(8):
                    S.op("pe", lambda e, k=k: e.matmul(pA[:, 0:G], w1[:, k, j * 128:(j + 1) * 128],
                                                       hTg[:, k, :], start=(k == 0), stop=(k == 7)),
                         reads=[w1b, hTgb], writes=[pAb], inc=(k == 7))
                for k in range(8):
                    S.op("pe", lambda e, k=k: e.matmul(pB[:, 0:G],
                                                       w1[:, k, DFF + j * 128:DFF + (j + 1) * 128],
                                                       hTg[:, k, :], start=(k == 0), stop=(k == 7)),
                         reads=[w1b, hTgb], writes=[pBb], inc=(k == 7))
                s_, s_b = sa[j % 2], sab[j % 2]
                S.op("act", lambda e: e.activation(out=s_[:], in_=pA[:, 0:G], func=AF.Silu),
                     reads=[pAb], writes=[s_b])
                S.op("dve", lambda e: e.tensor_tensor(out=uT[:, j, :], in0=s_[:], in1=pB[:, 0:G],
                                                      op=ALU.mult),
                     reads=[s_b, pBb], writes=[uTb])
            for i in range(2):
                xt, xb = xts[(gi % 2) * 2 + i], xtb[(gi % 2) * 2 + i]
                r0 = gi * G + i * 128
                for n in range(2):
                    pY, pYb = cx.psY[n], cx.psYb[n]
                    for j in range(22):
                        S.op("pe", lambda e, j=j: e.matmul(pY[:], uT[:, j, i * 128:(i + 1) * 128],
                                                           w2[:, j, n * 512:(n + 1) * 512],
                                                           start=(j == 0), stop=(j == 21)),
                             reads=[uTb, w2b], writes=[pYb], inc=(j == 21))
                    S.op("dve", lambda e: e.scalar_tensor_tensor(
                        out=xt[:, n * 512:(n + 1) * 512], in0=pY[:], scalar=0.5,
                        in1=xt[:, n * 512:(n + 1) * 512], op0=ALU.mult, op1=ALU.add),
                        reads=[pYb, xb], writes=[xb])
                S.dma("sp", xout[r0:r0 + 128, :], xt[:], reads=[xb])
        cx.stack = old
        S.finish("sp")
        barrier(cx, S)


def barrier(cx, S):
    for e in ("pe", "act", "dve", "pool", "sp"):
        S.finish(e)


def make_ctx(nc, stack):
    cx = Ctx()
    cx.nc = nc
    cx.stack = stack
    cx.rr = 0
    S = Sched(nc, stack)
    cx.ident = sb(cx, "ident", [128, 128], BF16)
    cx.constb = Buf()
    cx.junk = [sb(cx, "junk%d" % i, [128, D], BF16) for i in range(2)]
    cx.junkb = [Buf() for _ in range(2)]
    cx.ss = [sb(cx, "ss%d" % i, [128, 4], F32) for i in range(2)]
    cx.ssb = [Buf() for _ in range(2)]
    cx.hbf = [sb(cx, "hbf%d" % i, [128, D], BF16) for i in range(2)]
    cx.hbfb = [Buf() for _ in range(2)]
    cx.psT = [ps(cx, "psT%d" % i, [128, 1024], BF16) for i in range(2)]
    cx.psTb = [Buf() for _ in range(2)]
    cx.psA = [ps(cx, "psA%d" % i, [128, 512], F32) for i in range(2)]
    cx.psAb = [Buf() for _ in range(2)]
    cx.psB = [ps(cx, "psB%d" % i, [128, 512], F32) for i in range(2)]
    cx.psBb = [Buf() for _ in range(2)]
    cx.psY = [ps(cx, "psY%d" % i, [128, 512], F32) for i in range(2)]
    cx.psYb = [Buf() for _ in range(2)]
    return cx, S


def load_consts(cx, S, ident_ap):
    S.dma("pool", cx.ident[:], ident_ap, writes=[cx.constb])


def load_consts(cx, S, cst_ap):
    S.dma("pool", cx.cbf[:], cst_ap[:, 0:640], writes=[cx.constb])
    S.dma("sp", cx.rmask[:], cst_ap[:, 640:768], writes=[cx.constb])
    S.dma("sp", cx.cmask[:], cst_ap[:, 768:772], writes=[cx.constb])


def host_consts():
    c = np.zeros((128, 772), np.float32)
    i = np.arange(128)
    c[:, 0:128] = np.eye(128)
    c[:, 128:256] = 1.0
    s_, t_ = i[:, None], i[None, :]
    c[:, 256:384] = ((s_ // 32 == t_ // 32) & (s_ <= t_))
    c[:, 384:512] = (s_ >= t_)
    c[:, 512:640] = (s_ <= t_)
    c[:, 640:768] = (t_ % 32 != 0) * np.ones((128, 1))
    for j in range(4):
        c[:, 768 + j] = (i // 32 == j)
    return c


def phase_hgrn(cx, S, xin, oat, g_ap, w_in_ap, lgT_ap, gnT_ap, T, layer, tag):
    nc = cx.nc
    ident = cx.cbf[:, 0:128]
    ones = cx.cbf[:, 128:256]
    bmask = cx.cbf[:, 256:384]
    with contextlib.ExitStack() as st:
        old = cx.stack
        cx.stack = st
        W = sb(cx, tag + "W", [128, 8, 4096], BF16)
        Wb = Buf()
        gain = sb(cx, tag + "g", [128, D], F32)
        lg = sb(cx, tag + "lg", [128, lgT_ap.shape[1], 8], F32)
        lb = sb(cx, tag + "lb", [128, 8], F32)
        oml = sb(cx, tag + "oml", [128, 8], F32)
        gn = sb(cx, tag + "gn", [128, 8], F32)
        parb = Buf()
        xts = [sb(cx, tag + "xt%d" % i, [128, D], F32) for i in range(2)]
        xtb = [Buf() for _ in range(2)]
        hT = [sb(cx, tag + "hT%d" % i, [128, 8, 128], BF16) for i in range(2)]
        hTb = [Buf() for _ in range(2)]
        v = [sb(cx, tag + "v%d" % i, [128, D], BF16) for i in range(2)]
        vb = [Buf() for _ in range(2)]
        vm = [sb(cx, tag + "vm%d" % i, [128, 4, D], BF16) for i in range(2)]
        vmb = [Buf() for _ in range(2)]
        S32 = sb(cx, tag + "S32", [128, 8, 128], F32)
        S32e = sb(cx, tag + "S32e", [128, 8, 128], F32)
        Sbf = sb(cx, tag + "Sbf", [128, 8, 128], BF16)
        S32b = [Buf() for _ in range(8)]
        S32eb = [Buf() for _ in range(8)]
        Sbfb = [Buf() for _ in range(8)]
        NB = 2

        def mk(name, dt, n=NB):
            return ([sb(cx, tag + name + str(i), [128, 128], dt) for i in range(n)],
                    [Buf() for _ in range(n)])
        q32, q32b = mk("q", F32)
        f32t, f32b = mk("f", F32)
        gt, gtb = mk("gt", F32)
        lf, lfb = mk("lf", F32)
        kk, kkb = mk("kk", F32)
        gc, gcb = mk("gc", F32)
        eq, eqb = mk("eq", F32)
        ek, ekb = mk("ek", F32)
        qg, qgb = mk("qg", BF16)
        kg, kgb = mk("kg", BF16)
        kgT, kgTb = mk("kgT", BF16)
        AT, ATb = mk("AT", BF16)
        sq, sqb = mk("sq", BF16)
        rs, rsb = mk("rs", F32)
        on, onb = mk("on", F32)
        oa, oab = mk("oa", BF16, 4)

        S.dma("sp", gain[:], g_ap.partition_broadcast(128), writes=[cx.constb])
        S.dma("sp", lg[:], lgT_ap, writes=[parb])
        S.dma("sp", gn[:], gnT_ap, writes=[parb])
        if layer == 0:
            S.op("dve", lambda e: e.memset(lb[:], 0.0), writes=[parb])
        else:
            S.op("dve", lambda e: e.tensor_tensor(out=lb[:], in0=lg[:, 1, :], in1=lg[:, 0, :],
                                                  op=ALU.subtract), reads=[parb], writes=[parb])
            S.op("act", lambda e: e.activation(out=lb[:], in_=lb[:], func=AF.Sigmoid),
                 reads=[parb], writes=[parb])
        S.op("dve", lambda e: e.tensor_scalar(out=oml[:], in0=lb[:], scalar1=-1.0, scalar2=1.0,
                                              op0=ALU.mult, op1=ALU.add), reads=[parb], writes=[parb])
        S.op("dve", lambda e: e.memset(S32[:], 0.0), writes=S32b)
        S.op("pool", lambda e: e.memset(Sbf[:], 0.0), writes=Sbfb)
        load_weight_bf16(cx, S, W, Wb, w_in_ap, 8, P_IN, c0=0, c1=4096, cstep=2048)

        psQ, psQb = cx.psA[0], cx.psAb[0]
        psF, psFb = cx.psA[1], cx.psAb[1]
        psG, psGb = cx.psB[0], cx.psBb[0]
        psAT, psATb = cx.psB[1], cx.psBb[1]
        psV, psVb = cx.psY[0], cx.psYb[0]
        psO, psOb = cx.psY[1], cx.psYb[1]
        psK, psKb = cx.psT[1], cx.psTb[1]
        nt = T // 128
        hc = 0
        for ti in range(nt):
            t0 = ti * 128
            xt, xb = xts[ti % 2], xtb[ti % 2]
            hTt, hTtb = hT[ti % 2], hTb[ti % 2]
            vt, vtb = v[ti % 2], vb[ti % 2]
            S.dma("sp", xt[:], xin[t0:t0 + 128, :], writes=[xb])
            emit_norm_T(cx, S, xt, xb, gain, hTt, hTtb, 0, tag, fixed=0)
            for half in range(2):
                for k in range(8):
                    S.op("pe", lambda e, k=k: e.matmul(psV[:], hTt[:, k, :],
                                                       W[:, k, 2048 + half * 512:2048 + (half + 1) * 512],
                                                       start=(k == 0), stop=(k == 7)),
                         reads=[hTtb, Wb], writes=[psVb], inc=(k == 7))
                S.op("act", lambda e: e.copy(out=vt[:, half * 512:(half + 1) * 512], in_=psV[:]),
                     reads=[psVb], writes=[vtb])
            vmt, vmtb = vm[ti % 2], vmb[ti % 2]
            for j in range(4):
                S.op("pool", lambda e: e.tensor_scalar(out=vmt[:, j, :], in0=vt[:], scalar1=cx.cmask[:, j:j + 1],
                                                       scalar2=None, op0=ALU.mult),
                     reads=[vtb, cx.constb], writes=[vmtb])
            for hd in range(8):
                b = hc % NB
                hc += 1
                c0 = hd * 128
                for (pt, ptb, base) in ((psQ, psQb, 0), (psF, psFb, 1024), (psG, psGb, 3072)):
                    for k in range(8):
                        S.op("pe", lambda e, k=k: e.matmul(pt[:, 0:128], W[:, k, base + c0:base + c0 + 128],
                                                           hTt[:, k, :], start=(k == 0), stop=(k == 7)),
                             reads=[hTtb, Wb], writes=[ptb], inc=(k == 7))
                S.op("act", lambda e: e.activation(out=q32[b][:], in_=psQ[:, 0:128], func=AF.Silu),
                     reads=[psQb], writes=[q32b[b]])
                S.op("act", lambda e: e.activation(out=gt[b][:], in_=psG[:, 0:128], func=AF.Silu),
                     reads=[psGb], writes=[gtb[b]])
                S.op("act", lambda e: e.activation(out=f32t[b][:], in_=psF[:, 0:128], func=AF.Sigmoid),
                     reads=[psFb], writes=[f32b[b]])
                S.op("dve", lambda e: e.tensor_scalar(out=f32t[b][:], in0=f32t[b][:],
                                                      scalar1=oml[:, hd:hd + 1], scalar2=lb[:, hd:hd + 1],
                                                      op0=ALU.mult, op1=ALU.add),
                     reads=[f32b[b], parb], writes=[f32b[b]])
                S.op("act", lambda e: e.activation(out=lf[b][:], in_=f32t[b][:], func=AF.Ln),
                     reads=[f32b[b]], writes=[lfb[b]])
                S.op("pool", lambda e: e.tensor_scalar(out=kk[b][:], in0=f32t[b][:], scalar1=-1.0,
                                                       scalar2=1.0, op0=ALU.mult, op1=ALU.add),
                     reads=[f32b[b]], writes=[kkb[b]])
                S.op("dve", lambda e: e.tensor_tensor_scan(out=gc[b][:], data0=cx.rmask[:], data1=lf[b][:],
                                                           initial=0.0, op0=ALU.mult, op1=ALU.add),
                     reads=[lfb[b], cx.constb], writes=[gcb[b]])
                S.op("act", lambda e: e.activation(out=eq[b][:], in_=gc[b][:], func=AF.Exp),
                     reads=[gcb[b]], writes=[eqb[b]])
                S.op("act", lambda e: e.activation(out=ek[b][:], in_=gc[b][:], func=AF.Exp, scale=-1.0),
                     reads=[gcb[b]], writes=[ekb[b]])
                S.op("pool", lambda e: e.tensor_tensor(out=qg[b][:], in0=q32[b][:], in1=eq[b][:], op=ALU.mult),
                     reads=[q32b[b], eqb[b]], writes=[qgb[b]])
                S.op("pool", lambda e: e.tensor_tensor(out=kg[b][:], in0=kk[b][:], in1=ek[b][:], op=ALU.mult),
                     reads=[kkb[b], ekb[b]], writes=[kgb[b]])
                S.op("pe", lambda e: e.transpose(psK[:, 0:128], kg[b][:], ident),
                     reads=[kgb[b], cx.constb], writes=[psKb])
                S.op("act", lambda e: e.copy(out=kgT[b][:], in_=psK[:, 0:128]),
                     reads=[psKb], writes=[kgTb[b]])
                S.op("pe", lambda e: e.matmul(psAT[:, 0:128], kg[b][:], qg[b][:], start=True, stop=True),
                     reads=[kgb[b], qgb[b]], writes=[psATb])
                S.op("dve", lambda e: e.tensor_tensor(out=AT[b][:], in0=psAT[:, 0:128], in1=bmask, op=ALU.mult),
                     reads=[psATb, cx.constb], writes=[ATb[b]])
                S.op("pe", lambda e: e.matmul(psO[:, 0:128], vt[:, c0:c0 + 128], AT[b][:], start=True, stop=False),
                     reads=[vtb, ATb[b]], writes=[psOb])
                for j in range(4):
                    js = slice(32 * j, 32 * j + 32)
                    S.op("pe", lambda e: e.matmul(psO[:, js], Sbf[:, hd, :], qg[b][:, js],
                                                  start=False, stop=(j == 3)),
                         reads=[Sbfb[hd], qgb[b]], writes=[psOb], inc=False)
                    S.op("pe", lambda e: e.matmul(psV[:, 0:128], kgT[b][:], vmt[:, j, c0:c0 + 128],
                                                  start=True, stop=True),
                         reads=[kgTb[b], vmtb], writes=[psVb])
                    egl = eq[b][:, 32 * j + 31:32 * j + 32]
                    S.op("pool", lambda e: e.tensor_scalar(out=S32e[:, hd, :], in0=S32[:, hd, :], scalar1=egl,
                                                           scalar2=None, op0=ALU.mult),
                         reads=[S32b[hd], eqb[b]], writes=[S32eb[hd]])
                    S.op("dve", lambda e: e.scalar_tensor_tensor(out=S32[:, hd, :], in0=psV[:, 0:128], scalar=egl,
                                                                 in1=S32e[:, hd, :], op0=ALU.mult, op1=ALU.add),
                         reads=[psVb, eqb[b], S32eb[hd]], writes=[S32b[hd]])
                    S.op("act", lambda e: e.copy(out=Sbf[:, hd, :], in_=S32[:, hd, :]),
                         reads=[S32b[hd]], writes=[Sbfb[hd]])
                S.op("act", lambda e: e.activation(out=sq[b][:], in_=psO[:, 0:128], func=AF.Square),
                     reads=[psOb], writes=[sqb[b]])
                S.op("pe", lambda e: e.matmul(psAT[:, 128:256], ones, sq[b][:], start=True, stop=True),
                     reads=[sqb[b], cx.constb], writes=[psATb])
                S.op("act", lambda e: e.activation(out=rs[b][:], in_=psAT[:, 128:256], func=AF.Sqrt,
                                                   scale=1.0 / 128, bias=EPS),
                     reads=[psATb], writes=[rsb[b]])
                S.op("dve", lambda e: e.reciprocal(out=rs[b][:], in_=rs[b][:]),
                     reads=[rsb[b]], writes=[rsb[b]])
                S.op("dve", lambda e: e.tensor_tensor(out=on[b][:], in0=psO[:, 0:128], in1=rs[b][:], op=ALU.mult),
                     reads=[psOb, rsb[b]], writes=[onb[b]])
                ob = (ti * 8 + hd) % 4
                S.op("dve", lambda e: e.scalar_tensor_tensor(out=oa[ob][:], in0=on[b][:], scalar=gn[:, hd:hd + 1],
                                                              in1=gt[b][:], op0=ALU.mult, op1=ALU.mult),
                     reads=[onb[b], gtb[b], parb], writes=[oab[ob]])
                S.dma("sp", oat[c0:c0 + 128, t0:t0 + 128], oa[ob][:], reads=[oab[ob]])
        cx.stack = old
        barrier(cx, S)


def phase_attn_proj(cx, S, xin, qkv, g_ap, w_in_ap, qn_ap, kn_ap, cos_ap, sin_ap, T, tag):
    nc = cx.nc
    with contextlib.ExitStack() as st:
        old = cx.stack
        cx.stack = st
        W = sb(cx, tag + "W", [128, 8, 4608], BF16)
        Wb = Buf()
        gain = sb(cx, tag + "g", [128, D], F32)
        gq = sb(cx, tag + "gq", [128, 2, 3, 128], F32)
        parb = Buf()
        xts = [sb(cx, tag + "xt%d" % i, [128, D], F32) for i in range(2)]
        xtb = [Buf() for _ in range(2)]
        hT = [sb(cx, tag + "hT%d" % i, [128, 8, 128], BF16) for i in range(2)]
        hTb = [Buf() for _ in range(2)]
        cs = [sb(cx, tag + "cs%d" % i, [128, 2, 512], F32) for i in range(2)]
        csb = [Buf() for _ in range(2)]
        ob = [sb(cx, tag + "ob%d" % i, [128, 3, 1536], BF16) for i in range(2)]
        obb = [Buf() for _ in range(2)]
        NB = 2
        ssq = [sb(cx, tag + "ssq%d" % i, [128, 8], F32) for i in range(NB)]
        ssqb = [Buf() for _ in range(NB)]
        jk = [sb(cx, tag + "jk%d" % i, [128, 128], BF16) for i in range(NB)]
        jkb = [Buf() for _ in range(NB)]
        qn = [sb(cx, tag + "qn%d" % i, [128, 512], F32) for i in range(NB)]
        qnb = [Buf() for _ in range(NB)]
        t1 = [sb(cx, tag + "t1%d" % i, [128, 512], F32) for i in range(NB)]
        t1b = [Buf() for _ in range(NB)]
        t2 = [sb(cx, tag + "t2%d" % i, [128, 512], F32) for i in range(NB)]
        t2b = [Buf() for _ in range(NB)]
        S.dma("sp", gain[:], g_ap.partition_broadcast(128), writes=[cx.constb])
        for g in range(3):
            S.dma("sp", gq[:, 0, g, :], qn_ap[g:g + 1, :].partition_broadcast(128), writes=[parb])
            S.dma("sp", gq[:, 1, g, :], kn_ap[g:g + 1, :].partition_broadcast(128), writes=[parb])
        load_weight_bf16(cx, S, W, Wb, w_in_ap, 8, P_IN, c0=4096, c1=8704, cstep=1536)
        pss = [(cx.psA[0], cx.psAb[0]), (cx.psA[1], cx.psAb[1]), (cx.psB[0], cx.psBb[0]), (cx.psB[1], cx.psBb[1])]
        nt = T // 128
        cnt = 0
        for ti in range(nt):
            t0 = ti * 128
            xt, xb = xts[ti % 2], xtb[ti % 2]
            hTt, hTtb = hT[ti % 2], hTb[ti % 2]
            c_, c_b = cs[ti % 2], csb[ti % 2]
            o_, o_b = ob[ti % 2], obb[ti % 2]
            S.dma("sp", xt[:], xin[t0:t0 + 128, :], writes=[xb])
            S.dma("sp", c_[:, 0, :], cos_ap[t0:t0 + 128, :], writes=[c_b])
            S.dma("sp", c_[:, 1, :], sin_ap[t0:t0 + 128, :], writes=[c_b])
            emit_norm_T(cx, S, xt, xb, gain, hTt, hTtb, 0, tag)
            for which in range(3):
                for g in range(3):
                    pt, ptb = pss[cnt % 4]
                    b = cnt % NB
                    cnt += 1
                    cb = which * 1536 + g * 512
                    for k in range(8):
                        S.op("pe", lambda e, k=k: e.matmul(pt[:], hTt[:, k, :], W[:, k, cb:cb + 512],
                                                           start=(k == 0), stop=(k == 7)),
                             reads=[hTtb, Wb], writes=[ptb], inc=(k == 7))
                    if which == 2:
                        S.op("act", lambda e: e.copy(out=o_[:, 2, g * 512:(g + 1) * 512], in_=pt[:]),
                             reads=[ptb], writes=[o_b])
                        continue
                    for hh in range(4):
                        S.op("act", lambda e: e.activation(out=jk[b][:], in_=pt[:, hh * 128:(hh + 1) * 128],
                                                           func=AF.Square, accum_out=ssq[b][:, hh:hh + 1]),
                             reads=[ptb], writes=[jkb[b], ssqb[b]])
                    S.op("act", lambda e: e.activation(out=ssq[b][:, 4:8], in_=ssq[b][:, 0:4], func=AF.Sqrt,
                                                       scale=1.0 / 128, bias=EPS),
                         reads=[ssqb[b]], writes=[ssqb[b]])
                    S.op("dve", lambda e: e.reciprocal(out=ssq[b][:, 4:8], in_=ssq[b][:, 4:8]),
                         reads=[ssqb[b]], writes=[ssqb[b]])
                    for hh in range(4):
                        S.op("dve", lambda e: e.scalar_tensor_tensor(
                            out=qn[b][:, hh * 128:(hh + 1) * 128], in0=pt[:, hh * 128:(hh + 1) * 128],
                            scalar=ssq[b][:, 4 + hh:5 + hh], in1=gq[:, which, g, :], op0=ALU.mult, op1=ALU.mult),
                            reads=[ptb, ssqb[b], parb], writes=[qnb[b]])
                    S.op("pool", lambda e: e.tensor_tensor(out=t1[b][:], in0=qn[b][:], in1=c_[:, 0, :], op=ALU.mult),
                         reads=[qnb[b], c_b], writes=[t1b[b]])
                    qv = qn[b][:].rearrange("p (h two d) -> p h two d", h=4, two=2)
                    sv = c_[:, 1, :].rearrange("p (h two d) -> p h two d", h=4, two=2)
                    tv = t2[b][:].rearrange("p (h two d) -> p h two d", h=4, two=2)
                    S.op("pool", lambda e: e.tensor_tensor(out=tv[:, :, 0, :], in0=qv[:, :, 1, :], in1=sv[:, :, 0, :],
                                                           op=ALU.mult),
                         reads=[qnb[b], c_b], writes=[t2b[b]])
                    S.op("pool", lambda e: e.tensor_tensor(out=tv[:, :, 1, :], in0=qv[:, :, 0, :], in1=sv[:, :, 1, :],
                                                           op=ALU.mult),
                         reads=[qnb[b], c_b], writes=[t2b[b]])
                    S.op("dve", lambda e: e.tensor_tensor(out=o_[:, which, g * 512:(g + 1) * 512], in0=t1[b][:],
                                                          in1=t2[b][:], op=ALU.add),
                         reads=[t1b[b], t2b[b]], writes=[o_b])
            S.dma("sp", qkv[t0:t0 + 128, :, :], o_[:], reads=[o_b])
        cx.stack = old
        barrier(cx, S)


ATT_PAT = ((128, 1), (512, 4), (2048, 16))
AC_STAGE = 4


def phase_attn_core(cx, S, qkv, obt, T, tag):
    nc = cx.nc
    ident = cx.cbf[:, 0:128]
    ones = cx.cbf[:, 128:256]
    amask = cx.cbf[:, 384:640]
    SB = 2048
    scale = 128 ** -0.5
    with contextlib.ExitStack() as st:
        old = cx.stack
        cx.stack = st
        num = sb(cx, tag + "num", [128, 4, SB], F32)
        den = sb(cx, tag + "den", [128, 4, SB], F32)
        numb = [Buf() for _ in range(4)]
        denb = [Buf() for _ in range(4)]
        NB = 2
        qt = [sb(cx, tag + "q%d" % i, [128, 512], BF16) for i in range(NB)]
        kp = [sb(cx, tag + "kp%d" % i, [128, 512], BF16) for i in range(NB)]
        ko = [sb(cx, tag + "ko%d" % i, [128, 512], BF16) for i in range(NB)]
        vp = [sb(cx, tag + "vp%d" % i, [128, 512], BF16) for i in range(NB)]
        vo = [sb(cx, tag + "vo%d" % i, [128, 512], BF16) for i in range(NB)]
        qtb, kpb, kob, vpb, vob = ([Buf() for _ in range(NB)] for _ in range(5))
        qT = [sb(cx, tag + "qT%d" % i, [128, 512], BF16) for i in range(NB)]
        kpT = [sb(cx, tag + "kpT%d" % i, [128, 512], BF16) for i in range(NB)]
        koT = [sb(cx, tag + "koT%d" % i, [128, 512], BF16) for i in range(NB)]
        qTb, kpTb, koTb = ([Buf() for _ in range(NB)] for _ in range(3))
        PT = [sb(cx, tag + "PT%d" % i, [128, 256], BF16) for i in range(4)]
        PTb = [Buf() for _ in range(4)]
        obf = [sb(cx, tag + "obf%d" % i, [128, SB], BF16) for i in range(2)]
        obfb = [Buf() for _ in range(2)]
        blk = 0
        hc = 0
        tc = 0
        oc = 0
        for sbi in range(T // SB):
            for g, (_, d) in enumerate(ATT_PAT):
                nper = SB // (128 * d)
                for r in range(d):
                    for nn in range(nper):
                        n = sbi * nper + nn
                        b = blk % NB
                        blk += 1
                        ts = n * 128 * d + r
                        te = ts + 127 * d + 1
                        off = ts - sbi * SB
                        has_prev = n > 0
                        c0, c1 = g * 512, (g + 1) * 512
                        S.dma("sp", qt[b][:], qkv[ts:te:d, 0, c0:c1], writes=[qtb[b]])
                        S.dma("sp", ko[b][:], qkv[ts:te:d, 1, c0:c1], writes=[kob[b]])
                        S.dma("sp", vo[b][:], qkv[ts:te:d, 2, c0:c1], writes=[vob[b]])
                        if has_prev:
                            ps_ = ts - 128 * d
                            S.dma("sp", kp[b][:], qkv[ps_:ps_ + 127 * d + 1:d, 1, c0:c1], writes=[kpb[b]])
                            S.dma("sp", vp[b][:], qkv[ps_:ps_ + 127 * d + 1:d, 2, c0:c1], writes=[vpb[b]])
                        todo = [(qt[b], qtb[b], qT[b], qTb[b]), (ko[b], kob[b], koT[b], koTb[b])]
                        if has_prev:
                            todo.append((kp[b], kpb[b], kpT[b], kpTb[b]))
                        for (src, srcb, dst, dstb) in todo:
                            pT, pTb = cx.psT[tc % 2], cx.psTb[tc % 2]
                            tc += 1
                            for hh in range(4):
                                S.op("pe", lambda e: e.transpose(pT[:, hh * 128:(hh + 1) * 128],
                                                                 src[:, hh * 128:(hh + 1) * 128], ident),
                                     reads=[srcb, cx.constb], writes=[pTb], inc=(hh == 3))
                            S.op("act", lambda e: e.copy(out=dst[:], in_=pT[:, 0:512]),
                                 reads=[pTb], writes=[dstb])
                        for hh in range(4 if AC_STAGE >= 2 else 0):
                            hs = slice(hh * 128, (hh + 1) * 128)
                            psS, psSb = cx.psA[hc % 2], cx.psAb[hc % 2]
                            psN, psNb = cx.psB[hc % 2], cx.psBb[hc % 2]
                            psD, psDb = cx.psY[hc % 2], cx.psYb[hc % 2]
                            P, Pb = PT[hc % 4], PTb[hc % 4]
                            hc += 1
                            lo = 0 if has_prev else 128
                            if has_prev:
                                S.op("pe", lambda e: e.matmul(psS[:, 0:128], kpT[b][:, hs], qT[b][:, hs],
                                                              start=True, stop=True),
                                     reads=[kpTb[b], qTb[b]], writes=[psSb], inc=False)
                            S.op("pe", lambda e: e.matmul(psS[:, 128:256], koT[b][:, hs], qT[b][:, hs],
                                                          start=True, stop=True),
                                 reads=[koTb[b], qTb[b]], writes=[psSb])
                            S.op("act", lambda e: e.activation(out=P[:, lo:256], in_=psS[:, lo:256], func=AF.Exp,
                                                               scale=scale),
                                 reads=[psSb], writes=[Pb])
                            S.op("pool", lambda e: e.tensor_tensor(out=P[:, lo:256], in0=P[:, lo:256],
                                                                   in1=amask[:, lo:256], op=ALU.mult),
                                 reads=[Pb, cx.constb], writes=[Pb])
                            if AC_STAGE < 3:
                                continue
                            if has_prev:
                                S.op("pe", lambda e: e.matmul(psN[:, 0:128], vp[b][:, hs], P[:, 0:128],
                                                              start=True, stop=False),
                                     reads=[vpb[b], Pb], writes=[psNb], inc=False)
                            S.op("pe", lambda e: e.matmul(psN[:, 0:128], vo[b][:, hs], P[:, 128:256],
                                                          start=(not has_prev), stop=True),
                                 reads=[vob[b], Pb], writes=[psNb], inc=False)
                            if has_prev:
                                S.op("pe", lambda e: e.matmul(psD[:, 0:128], ones, P[:, 0:128],
                                                              start=True, stop=False),
                                     reads=[cx.constb, Pb], writes=[psDb], inc=False)
                            S.op("pe", lambda e: e.matmul(psD[:, 0:128], ones, P[:, 128:256],
                                                          start=(not has_prev), stop=True),
                                 reads=[cx.constb, Pb], writes=[psDb])
                            if AC_STAGE < 3.2 or (AC_STAGE in (3.25, 3.27) and g > 0):
                                S.op("pe", lambda e: e.matmul(psN[:, 256:384], ones, P[:, 128:256], start=True, stop=True),
                                     reads=[cx.constb, Pb], writes=[psNb])
                                continue
                            if AC_STAGE < 3.4 and g > 0:
                                S.op("pe", lambda e: e.matmul(psN[:, 256:384], ones, P[:, 128:256], start=True, stop=True),
                                     reads=[cx.constb, Pb], writes=[psNb])
                                continue
                            nv = num[:, hh, off:off + 127 * d + 1:d]
                            dv = den[:, hh, off:off + 127 * d + 1:d]
                            if g == 0:
                                if AC_STAGE != 3.27:
                                    S.op("act", lambda e: e.copy(out=nv, in_=psN[:, 0:128]),
                                         reads=[psNb], writes=[numb[hh]])
                                if AC_STAGE != 3.25:
                                    S.op("act", lambda e: e.copy(out=dv, in_=psD[:, 0:128]),
                                         reads=[psDb], writes=[denb[hh]])
                            else:
                                S.op("dve", lambda e: e.tensor_tensor(out=nv, in0=nv, in1=psN[:, 0:128], op=ALU.add),
                                     reads=[psNb, numb[hh]], writes=[numb[hh]])
                                S.op("dve", lambda e: e.tensor_tensor(out=dv, in0=dv, in1=psD[:, 0:128], op=ALU.add),
                                     reads=[psDb, denb[hh]], writes=[denb[hh]])
            for hh in range(4 if AC_STAGE >= 4 else 0):
                o_, o_b = obf[oc % 2], obfb[oc % 2]
                oc += 1
                S.op("dve", lambda e: e.reciprocal(out=den[:, hh, :], in_=den[:, hh, :]),
                     reads=[denb[hh]], writes=[denb[hh]])
                S.op("pool", lambda e: e.tensor_tensor(out=o_[:], in0=num[:, hh, :], in1=den[:, hh, :], op=ALU.mult),
                     reads=[numb[hh], denb[hh]], writes=[o_b])
                S.dma("sp", obt[hh * 128:(hh + 1) * 128, sbi * SB:(sbi + 1) * SB], o_[:], reads=[o_b])
        cx.stack = old
        barrier(cx, S)


def phase_out(cx, S, xin, xout, oat, obt, g_ap, w_in_ap, wa_ap, wb_ap, wo_ap, T, tag):
    nc = cx.nc
    with contextlib.ExitStack() as st:
        old = cx.stack
        cx.stack = st
        WG = sb(cx, tag + "WG", [128, 8, 2048], BF16)
        WA = sb(cx, tag + "WA", [128, 8, D], BF16)
        WB = sb(cx, tag + "WB", [128, 4, D], BF16)
        WO = sb(cx, tag + "WO", [128, 8, D], BF16)
        Wb = Buf()
        gain = sb(cx, tag + "g", [128, D], F32)
        parb = Buf()
        xts = [sb(cx, tag + "xt%d" % i, [128, D], F32) for i in range(2)]
        xtb = [Buf() for _ in range(2)]
        hT = [sb(cx, tag + "hT%d" % i, [128, 8, 128], BF16) for i in range(2)]
        hTb = [Buf() for _ in range(2)]
        oaT = [sb(cx, tag + "oaT%d" % i, [128, 8, 128], BF16) for i in range(2)]
        oaTb = [Buf() for _ in range(2)]
        obT = [sb(cx, tag + "obT%d" % i, [128, 4, 128], BF16) for i in range(2)]
        obTb = [Buf() for _ in range(2)]
        sga = [sb(cx, tag + "sga%d" % i, [128, D], F32) for i in range(2)]
        sgb = [sb(cx, tag + "sgb%d" % i, [128, D], F32) for i in range(2)]
        sgab = [Buf() for _ in range(2)]
        sgbb = [Buf() for _ in range(2)]
        m1 = [sb(cx, tag + "m1%d" % i, [128, 512], F32) for i in range(2)]
        m2 = [sb(cx, tag + "m2%d" % i, [128, 512], F32) for i in range(2)]
        m1b = [Buf() for _ in range(2)]
        m2b = [Buf() for _ in range(2)]
        mg = [sb(cx, tag + "mg%d" % i, [128, D], BF16) for i in range(2)]
        mgb = [Buf() for _ in range(2)]
        mT = [sb(cx, tag + "mT%d" % i, [128, 8, 128], BF16) for i in range(2)]
        mTb = [Buf() for _ in range(2)]
        S.dma("sp", gain[:], g_ap.partition_broadcast(128), writes=[cx.constb])
        load_weight_bf16(cx, S, WG, Wb, w_in_ap, 8, P_IN, c0=8704, c1=10752, cstep=2048)
        load_weight_bf16(cx, S, WA, Wb, wa_ap, 8, D, cstep=1024)
        load_weight_bf16(cx, S, WB, Wb, wb_ap, 4, D, cstep=1024)
        load_weight_bf16(cx, S, WO, Wb, wo_ap, 8, D, cstep=1024)
        oat_v = oat.rearrange("(c p) t -> p c t", p=128)
        obt_v = obt.rearrange("(c p) t -> p c t", p=128)
        nt = T // 128
        for ti in range(nt):
            t0 = ti * 128
            i2 = ti % 2
            xt, xb = xts[i2], xtb[i2]
            S.dma("sp", xt[:], xin[t0:t0 + 128, :], writes=[xb])
            S.dma("sp", oaT[i2][:], oat_v[:, :, t0:t0 + 128], writes=[oaTb[i2]])
            S.dma("sp", obT[i2][:], obt_v[:, :, t0:t0 + 128], writes=[obTb[i2]])
            emit_norm_T(cx, S, xt, xb, gain, hT[i2], hTb[i2], 0, tag)
            for n in range(2):
                for (pt, ptb, base, dst, dstb) in ((cx.psA[n], cx.psAb[n], 0, sga[i2], sgab[i2]),
                                                   (cx.psB[n], cx.psBb[n], 1024, sgb[i2], sgbb[i2])):
                    for k in range(8):
                        S.op("pe", lambda e, k=k: e.matmul(pt[:], hT[i2][:, k, :],
                                                           WG[:, k, base + n * 512:base + (n + 1) * 512],
                                                           start=(k == 0), stop=(k == 7)),
                             reads=[hTb[i2], Wb], writes=[ptb], inc=(k == 7))
                    S.op("act", lambda e: e.activation(out=dst[:, n * 512:(n + 1) * 512], in_=pt[:],
                                                       func=AF.Sigmoid),
                         reads=[ptb], writes=[dstb])
            for n in range(2):
                ns = slice(n * 512, (n + 1) * 512)
                pa, pab = cx.psY[0], cx.psYb[0]
                pb_, pbb = cx.psY[1], cx.psYb[1]
                for c in range(8):
                    S.op("pe", lambda e, c=c: e.matmul(pa[:], oaT[i2][:, c, :], WA[:, c, ns],
                                                       start=(c == 0), stop=(c == 7)),
                         reads=[oaTb[i2], Wb], writes=[pab], inc=(c == 7))
                for c in range(4):
                    S.op("pe", lambda e, c=c: e.matmul(pb_[:], obT[i2][:, c, :], WB[:, c, ns],
                                                       start=(c == 0), stop=(c == 3)),
                         reads=[obTb[i2], Wb], writes=[pbb], inc=(c == 3))
                S.op("dve", lambda e: e.tensor_tensor(out=m1[n][:], in0=sga[i2][:, ns], in1=pa[:], op=ALU.mult),
                     reads=[sgab[i2], pab], writes=[m1b[n]])
                S.op("dve", lambda e: e.tensor_tensor(out=m2[n][:], in0=sgb[i2][:, ns], in1=pb_[:], op=ALU.mult),
                     reads=[sgbb[i2], pbb], writes=[m2b[n]])
                S.op("pool", lambda e: e.tensor_tensor(out=mg[i2][:, ns], in0=m1[n][:], in1=m2[n][:], op=ALU.add),
                     reads=[m1b[n], m2b[n]], writes=[mgb[i2]])
            pT, pTb = cx.psT[ti % 2], cx.psTb[ti % 2]
            for c in range(8):
                S.op("pe", lambda e, c=c: e.transpose(pT[:, c * 128:(c + 1) * 128],
                                                      mg[i2][:, c * 128:(c + 1) * 128], cx.cbf[:, 0:128]),
                     reads=[mgb[i2], cx.constb], writes=[pTb], inc=(c == 7))
            S.op("act", lambda e: e.copy(out=mT[i2][:], in_=pT[:].rearrange("p (k t) -> p k t", k=8)),
                 reads=[pTb], writes=[mTb[i2]])
            for n in range(2):
                ns = slice(n * 512, (n + 1) * 512)
                pt, ptb = cx.psA[n], cx.psAb[n]
                for c in range(8):
                    S.op("pe", lambda e, c=c: e.matmul(pt[:], mT[i2][:, c, :], WO[:, c, ns],
                                                       start=(c == 0), stop=(c == 7)),
                         reads=[mTb[i2], Wb], writes=[ptb], inc=(c == 7))
                S.op("dve", lambda e: e.tensor_tensor(out=xt[:, ns], in0=xt[:, ns], in1=pt[:], op=ALU.add),
                     reads=[ptb, xb], writes=[xb])
            S.dma("sp", xout[t0:t0 + 128, :], xt[:], reads=[xb])
        cx.stack = old
        barrier(cx, S)


def make_ctx(nc, stack):
    cx = Ctx()
    cx.nc = nc
    cx.stack = stack
    cx.rr = 0
    S = Sched(nc, stack)
    cx.cbf = sb(cx, "cbf", [128, 640], BF16)
    cx.rmask = sb(cx, "rmask", [128, 128], F32)
    cx.cmask = sb(cx, "cmask", [128, 4], F32)
    cx.ident = cx.cbf[:, 0:128]
    cx.constb = Buf()
    cx.junk = [sb(cx, "junk%d" % i, [128, D], BF16) for i in range(2)]
    cx.junkb = [Buf() for _ in range(2)]
    cx.ss = [sb(cx, "ss%d" % i, [128, 4], F32) for i in range(2)]
    cx.ssb = [Buf() for _ in range(2)]
    cx.hbf = [sb(cx, "hbf%d" % i, [128, D], BF16) for i in range(2)]
    cx.hbfb = [Buf() for _ in range(2)]
    cx.psT = [ps(cx, "psT%d" % i, [128, 1024], BF16) for i in range(2)]
    cx.psTb = [Buf() for _ in range(2)]
    cx.psA = [ps(cx, "psA%d" % i, [128, 512], F32) for i in range(2)]
    cx.psAb = [Buf() for _ in range(2)]
    cx.psB = [ps(cx, "psB%d" % i, [128, 512], F32) for i in range(2)]
    cx.psBb = [Buf() for _ in range(2)]
    cx.psY = [ps(cx, "psY%d" % i, [128, 512], F32) for i in range(2)]
    cx.psYb = [Buf() for _ in range(2)]
    return cx, S


def build_nc(T, depth=2, phases=None, debug=False):
    nc = bass.Bass("TRN2", target_bir_lowering=False)
    dt = nc.dram_tensor
    x = dt("x", [T, D], F32, kind="ExternalInput").ap()
    cst = dt("cst", [128, 772], F32, kind="ExternalInput").ap()
    cos4 = dt("cos4", [T, 512], F32, kind="ExternalInput").ap()
    sin4 = dt("sin4", [T, 512], F32, kind="ExternalInput").ap()
    f1n = dt("ffn1_norm", [depth, D], F32, kind="ExternalInput").ap()
    f1i = dt("ffn1_w_in", [depth, D, 2 * DFF], F32, kind="ExternalInput").ap()
    f1o = dt("ffn1_w_out", [depth, DFF, D], F32, kind="ExternalInput").ap()
    mn = dt("mix_norm", [depth, D], F32, kind="ExternalInput").ap()
    win = dt("w_in", [depth, D, P_IN], F32, kind="ExternalInput").ap()
    lgT = dt("lgT", [128, depth, 8], F32, kind="ExternalInput").ap()
    gnT = dt("gnT", [depth, 128, 8], F32, kind="ExternalInput").ap()
    aqn = dt("attn_q_norm", [depth, 3, 128], F32, kind="ExternalInput").ap()
    akn = dt("attn_k_norm", [depth, 3, 128], F32, kind="ExternalInput").ap()
    wa = dt("w_branch_a", [depth, D, D], F32, kind="ExternalInput").ap()
    wb = dt("w_branch_b", [depth, 512, D], F32, kind="ExternalInput").ap()
    wo = dt("w_out", [depth, D, D], F32, kind="ExternalInput").ap()
    f2n = dt("ffn2_norm", [depth, D], F32, kind="ExternalInput").ap()
    f2i = dt("ffn2_w_in", [depth, D, 2 * DFF], F32, kind="ExternalInput").ap()
    f2o = dt("ffn2_w_out", [depth, DFF, D], F32, kind="ExternalInput").ap()
    y = dt("y", [T, D], F32, kind="ExternalOutput").ap()
    kind = "ExternalOutput" if debug else "Internal"
    xa = dt("xa", [T, D], F32, kind=kind).ap()
    xb_ = dt("xb", [T, D], F32, kind=kind).ap()
    oat = dt("oat", [D, T], BF16, kind=kind).ap()
    obt = dt("obt", [512, T], BF16, kind=kind).ap()
    qkv = dt("qkv", [T, 3, 1536], BF16, kind=kind).ap()
    with contextlib.ExitStack() as stack:
        cx, S = make_ctx(nc, stack)
        load_consts(cx, S, cst)
        cur = x
        for l in range(depth):
            last = (l == depth - 1)
            tg = "L%d" % l
            P = phases
            if P is None or "f1" in P:
                phase_ffn(cx, S, cur, xa, f1n[l:l + 1, :], f1i[l], f1o[l], T, tg + "f1")
            if P is None or "hg" in P:
                phase_hgrn(cx, S, xa, oat, mn[l:l + 1, :], win[l], lgT, gnT[l], T, l, tg + "hg")
            if P is None or "ap" in P:
                phase_attn_proj(cx, S, xa, qkv, mn[l:l + 1, :], win[l], aqn[l], akn[l], cos4, sin4, T, tg + "ap")
            if P is None or "ac" in P:
                phase_attn_core(cx, S, qkv, obt, T, tg + "ac")
            if P is None or "po" in P:
                phase_out(cx, S, xa, xb_, oat, obt, mn[l:l + 1, :], win[l], wa[l], wb[l], wo[l], T, tg + "po")
            if P is None or "f2" in P:
                phase_ffn(cx, S, xb_, y if last else xa, f2n[l:l + 1, :], f2i[l], f2o[l], T, tg + "f2")
            cur = xa
        barrier(cx, S)
        cx.ninst = S.ninst
        print("ninst", S.ninst, {k: v for k, v in S.cnt.items()})
    return nc


def rope_host(T):
    pos = np.arange(T, dtype=np.float32)
    inv = (np.float32(10000.0) ** (-np.arange(0, 128, 2, dtype=np.float32) / np.float32(128))).astype(np.float32)
    ang = pos[:, None] * inv[None, :]
    ang = np.concatenate([ang, ang], axis=-1).astype(np.float32)
    cos = np.cos(ang).astype(np.float32)
    sin = np.sin(ang).astype(np.float32)
    sin[:, :64] = -sin[:, :64]
    return np.tile(cos, (1, 4)), np.tile(sin, (1, 4))


def make_in_maps(inputs, T, depth=2):
    cos4, sin4 = rope_host(T)
    base = {k: np.ascontiguousarray(v, dtype=np.float32) for k, v in inputs.items() if k not in ("x", "hgrn_lb_logits", "hgrn_out_norm")}
    base["cst"] = host_consts()
    base["cos4"] = np.ascontiguousarray(cos4)
    base["sin4"] = np.ascontiguousarray(sin4)
    lg = np.asarray(inputs["hgrn_lb_logits"], np.float32)
    base["lgT"] = np.ascontiguousarray(lg.reshape(depth, 8, 128).transpose(2, 0, 1))
    gn = np.asarray(inputs["hgrn_out_norm"], np.float32)
    base["gnT"] = np.ascontiguousarray(gn.reshape(depth, 8, 128).transpose(0, 2, 1))
    x = np.asarray(inputs["x"], np.float32)
    maps = []
    for c in range(NCORES):
        m = dict(base)
        m["x"] = np.ascontiguousarray(x[c % x.shape[0], :T])
        maps.append(m)
    return maps


def kernel(**inputs):
    x = np.asarray(inputs["x"])
    B, T, _ = x.shape
    nc = build_nc(T, depth=2)
    maps = make_in_maps(inputs, T, depth=2)
    res = run_bass_kernel_spmd(nc, maps, core_ids=list(range(NCORES)))
    out = np.stack([np.asarray(res.results[b]["y"], dtype=np.float32) for b in range(B)], axis=0)
    return out


def build_ac_only(T):
    nc = bass.Bass("TRN2", target_bir_lowering=False)
    dt = nc.dram_tensor
    cst = dt("cst", [128, 772], F32, kind="ExternalInput").ap()
    qkv = dt("qkv", [T, 3, 1536], BF16, kind="ExternalInput").ap()
    obt = dt("obt", [512, T], BF16, kind="ExternalOutput").ap()
    with contextlib.ExitStack() as stack:
        cx, S = make_ctx(nc, stack)
        load_consts(cx, S, cst)
        phase_attn_core(cx, S, qkv, obt, T, "ac")
        barrier(cx, S)
    return nc
```

```python
import contextlib
import numpy as np
import concourse.bass as bass
import concourse.mybir as mybir
from concourse.bass_utils import run_bass_kernel_spmd

F32 = mybir.dt.float32
BF16 = mybir.dt.bfloat16
AF = mybir.ActivationFunctionType
ALU = mybir.AluOpType
AX = mybir.AxisListType

D = 1024
DFF = 2816
P_IN = 10752
EPS = 1e-6
NCORES = 8


class Buf:
    __slots__ = ("w", "r", "name")

    def __init__(self, name=""):
        self.w = None
        self.r = {}
        self.name = name


class Sched:
    def __init__(self, nc, stack, ndma=6):
        self.nc = nc
        self.h = {"pe": nc.tensor, "act": nc.scalar, "dve": nc.vector,
                  "pool": nc.gpsimd, "sp": nc.sync}
        self.sems = {}
        self.cnt = {}
        for e in ("pe", "act", "dve", "pool"):
            self.sems[e] = stack.enter_context(nc.semaphore("s_" + e))
            self.cnt[e] = 0
        self.seen = {e: {} for e in self.h}
        self.dq = {}
        for q in ("sp", "pool", "act"):
            names = ["d_%s%d" % (q, i) for i in range(ndma)]
            for n in names:
                self.sems[n] = stack.enter_context(nc.semaphore(n))
                self.cnt[n] = 0
            self.dq[q] = [names, 0]
        self.ninst = 0

    def _wait(self, e, deps):
        need = {}
        for d in deps:
            if d is None:
                continue
            k, v, src = d
            if src == e and e == "pe":
                continue
            if self.seen[e].get(k, 0) >= v:
                continue
            if need.get(k, 0) < v:
                need[k] = v
        for k, v in need.items():
            self.h[e].wait_ge(self.sems[k], v)
            self.seen[e][k] = v
            self.ninst += 1

    def _deps(self, e, reads, writes):
        deps = []
        for b in reads:
            deps.append(b.w)
        for b in writes:
            deps.append(b.w)
            for r in b.r.values():
                if r[2] == e and e != "pool":
                    continue
                deps.append(r)
        return deps

    def _record(self, ev, reads, writes):
        for b in reads:
            old = b.r.get(ev[0])
            if old is None or old[1] < ev[1]:
                b.r[ev[0]] = ev
        for b in writes:
            b.w = ev
            b.r = {}

    def op(self, e, fn, reads=(), writes=(), inc=True):
        self._wait(e, self._deps(e, reads, writes))
        ins = fn(self.h[e])
        if inc:
            self.cnt[e] += 1
            ins.then_inc(self.sems[e], 1)
            ev = (e, self.cnt[e], e)
        else:
            ev = (e, self.cnt[e] + 1, e)
        self.ninst += 1
        self._record(ev, reads, writes)
        return ev

    def dma(self, q, out, in_, reads=(), writes=(), **kw):
        names, idx = self.dq[q]
        k = names[idx % len(names)]
        self.dq[q][1] = idx + 1
        deps = self._deps("dma_issue", reads, writes)
        if self.cnt[k] > 0:
            deps.append((k, self.cnt[k], "dma"))
        self._wait(q, deps)
        ins = self.h[q].dma_start(out=out, in_=in_, **kw)
        self.cnt[k] += 16
        ins.then_inc(self.sems[k], 16)
        ev = (k, self.cnt[k], "dma")
        self.ninst += 1
        self._record(ev, reads, writes)
        return ev

    def finish(self, e="sp"):
        for k, c in self.cnt.items():
            if c > 0 and self.seen[e].get(k, 0) < c:
                self.h[e].wait_ge(self.sems[k], c)
                self.seen[e][k] = c


class Ctx:
    pass


def sb(cx, name, shape, dt):
    t = cx.stack.enter_context(cx.nc.sbuf_tensor("sb_" + name, list(shape), dt))
    return t


def ps(cx, name, shape, dt):
    t = cx.stack.enter_context(cx.nc.psum_tensor("ps_" + name, list(shape), dt))
    return t


def emit_norm_T(cx, S, xt, xb, gain, hT, hTb, col0, tag, fixed=None):
    nc = cx.nc
    i = cx.rr % 2 if fixed is None else fixed
    cx.rr += 1
    junk, junkb = cx.junk[i], cx.junkb[i]
    ss, ssb = cx.ss[i], cx.ssb[i]
    hb, hbb = cx.hbf[i], cx.hbfb[i]
    pT, pTb = cx.psT[i], cx.psTb[i]
    S.op("act", lambda e: e.activation(out=junk[:], in_=xt[:], func=AF.Square,
                                       accum_out=ss[:, 0:1]),
         reads=[xb], writes=[junkb, ssb])
    S.op("act", lambda e: e.activation(out=ss[:, 1:2], in_=ss[:, 0:1], func=AF.Sqrt,
                                       scale=1.0 / D, bias=EPS),
         reads=[ssb], writes=[ssb])
    S.op("dve", lambda e: e.reciprocal(out=ss[:, 2:3], in_=ss[:, 1:2]),
         reads=[ssb], writes=[ssb])
    S.op("dve", lambda e: e.scalar_tensor_tensor(out=hb[:], in0=xt[:], scalar=ss[:, 2:3],
                                                 in1=gain[:], op0=ALU.mult, op1=ALU.mult),
         reads=[xb, ssb, cx.constb], writes=[hbb])
    for k in range(8):
        S.op("pe", lambda e, k=k: e.transpose(pT[:, k * 128:(k + 1) * 128],
                                              hb[:, k * 128:(k + 1) * 128], cx.ident[:]),
             reads=[hbb, cx.constb], writes=[pTb], inc=(k == 7))
    S.op("act", lambda e: e.copy(out=hT[:, :, col0:col0 + 128],
                                 in_=pT[:].rearrange("p (k t) -> p k t", k=8)),
         reads=[pTb], writes=[hTb])


def load_weight_bf16(cx, S, dst, dstb, src_ap, nk, ncols, c0=0, c1=None, cstep=1408):
    c1 = ncols if c1 is None else c1
    v = src_ap.rearrange("(k p) c -> p k c", p=128)
    c = c0
    while c < c1:
        ce = min(c1, c + cstep)
        for k in range(nk):
            S.dma("pool", dst[:, k, c - c0:ce - c0], v[:, k, c:ce], writes=[dstb])
        c = ce


def phase_ffn(cx, S, xin, xout, g_ap, w_in_ap, w_out_ap, T, tag):
    nc = cx.nc
    G = 256
    with contextlib.ExitStack() as st:
        old = cx.stack
        cx.stack = st
        w1 = sb(cx, tag + "w1", [128, 8, 2 * DFF], BF16)
        w2 = sb(cx, tag + "w2", [128, 22, D], BF16)
        gain = sb(cx, tag + "g", [128, D], F32)
        xts = [sb(cx, tag + "xt%d" % i, [128, D], F32) for i in range(4)]
        hT = [sb(cx, tag + "hT%d" % i, [128, 8, G], BF16) for i in range(2)]
        uT = sb(cx, tag + "uT", [128, 22, G], BF16)
        sa = [sb(cx, tag + "sa%d" % i, [128, G], F32) for i in range(2)]
        w1b, w2b, uTb = Buf(), Buf(), Buf()
        xtb = [Buf() for _ in range(4)]
        hTb = [Buf() for _ in range(2)]
        sab = [Buf() for _ in range(2)]
        S.dma("sp", gain[:], g_ap.partition_broadcast(128), writes=[cx.constb])
        load_weight_bf16(cx, S, w1, w1b, w_in_ap, 8, 2 * DFF)
        load_weight_bf16(cx, S, w2, w2b, w_out_ap, 22, D, cstep=1024)
        ng = T // G
        for gi in range(ng):
            hTg, hTgb = hT[gi % 2], hTb[gi % 2]
            for i in range(2):
                xt, xb = xts[(gi % 2) * 2 + i], xtb[(gi % 2) * 2 + i]
                r0 = gi * G + i * 128
                S.dma("sp", xt[:], xin[r0:r0 + 128, :], writes=[xb])
                emit_norm_T(cx, S, xt, xb, gain, hTg, hTgb, i * 128, tag)
            for j in range(22):
                pA, pAb = cx.psA[j % 2], cx.psAb[j % 2]
                pB, pBb = cx.psB[j % 2], cx.psBb[j % 2]
                for k in range(8):
                    S.op("pe", lambda e, k=k: e.matmul(pA[:, 0:G], w1[:, k, j * 128:(j + 1) * 128],
                                                       hTg[:, k, :], start=(k == 0), stop=(k == 7)),
                         reads=[w1b, hTgb], writes=[pAb], inc=(k == 7))
                for k in range(8):
                    S.op("pe", lambda e, k=k: e.matmul(pB[:, 0:G],
                                                       w1[:, k, DFF + j * 128:DFF + (j + 1) * 128],
                                                       hTg[:, k, :], start=(k == 0), stop=(k == 7)),
                         reads=[w1b, hTgb], writes=[pBb], inc=(k == 7))
                s_, s_b = sa[j % 2], sab[j % 2]
                S.op("act", lambda e: e.activation(out=s_[:], in_=pA[:, 0:G], func=AF.Silu),
                     reads=[pAb], writes=[s_b])
                S.op("dve", lambda e: e.tensor_tensor(out=uT[:, j, :], in0=s_[:], in1=pB[:, 0:G],
                                                      op=ALU.mult),
                     reads=[s_b, pBb], writes=[uTb])
            for i in range(2):
                xt, xb = xts[(gi % 2) * 2 + i], xtb[(gi % 2) * 2 + i]
                r0 = gi * G + i * 128
                for n in range(2):
                    pY, pYb = cx.psY[n], cx.psYb[n]
                    for j in range(22):
                        S.op("pe", lambda e, j=j: e.matmul(pY[:], uT[:, j, i * 128:(i + 1) * 128],
                                                           w2[:, j, n * 512:(n + 1) * 512],
                                                           start=(j == 0), stop=(j == 21)),
                             reads=[uTb, w2b], writes=[pYb], inc=(j == 21))
                    S.op("dve", lambda e: e.scalar_tensor_tensor(
                        out=xt[:, n * 512:(n + 1) * 512], in0=pY[:], scalar=0.5,
                        in1=xt[:, n * 512:(n + 1) * 512], op0=ALU.mult, op1=ALU.add),
                        reads=[pYb, xb], writes=[xb])
                S.dma("sp", xout[r0:r0 + 128, :], xt[:], reads=[xb])
        cx.stack = old
        S.finish("sp")
        barrier(cx, S)


def barrier(cx, S):
    for e in ("pe", "act", "dve", "pool", "sp"):
        S.finish(e)


def make_ctx(nc, stack):
    cx = Ctx()
    cx.nc = nc
    cx.stack = stack
    cx.rr = 0
    S = Sched(nc, stack)
    cx.ident = sb(cx, "ident", [128, 128], BF16)
    cx.constb = Buf()
    cx.junk = [sb(cx, "junk%d" % i, [128, D], BF16) for i in range(2)]
    cx.junkb = [Buf() for _ in range(2)]
    cx.ss = [sb(cx, "ss%d" % i, [128, 4], F32) for i in range(2)]
    cx.ssb = [Buf() for _ in range(2)]
    cx.hbf = [sb(cx, "hbf%d" % i, [128, D], BF16) for i in range(2)]
    cx.hbfb = [Buf() for _ in range(2)]
    cx.psT = [ps(cx, "psT%d" % i, [128, 1024], BF16) for i in range(2)]
    cx.psTb = [Buf() for _ in range(2)]
    cx.psA = [ps(cx, "psA%d" % i, [128, 512], F32) for i in range(2)]
    cx.psAb = [Buf() for _ in range(2)]
    cx.psB = [ps(cx, "psB%d" % i, [128, 512], F32) for i in range(2)]
    cx.psBb = [Buf() for _ in range(2)]
    cx.psY = [ps(cx, "psY%d" % i, [128, 512], F32) for i in range(2)]
    cx.psYb = [Buf() for _ in range(2)]
    return cx, S


def load_consts(cx, S, ident_ap):
    S.dma("pool", cx.ident[:], ident_ap, writes=[cx.constb])


def load_consts(cx, S, cst_ap):
    S.dma("pool", cx.cbf[:], cst_ap[:, 0:640], writes=[cx.constb])
    S.dma("sp", cx.rmask[:], cst_ap[:, 640:768], writes=[cx.constb])
    S.dma("sp", cx.cmask[:], cst_ap[:, 768:772], writes=[cx.constb])
    S.dma("pool", cx.bmask4[:], cst_ap[:, 772:1284], writes=[cx.constb])
    S.dma("sp", cx.rmask4[:], cst_ap[:, 1284:1796], writes=[cx.constb])


def host_consts():
    c = np.zeros((128, 1796), np.float32)
    i = np.arange(128)
    c[:, 0:128] = np.eye(128)
    c[:, 128:256] = 1.0
    s_, t_ = i[:, None], i[None, :]
    c[:, 256:384] = ((s_ // 32 == t_ // 32) & (s_ <= t_))
    c[:, 384:512] = (s_ >= t_)
    c[:, 512:640] = (s_ <= t_)
    c[:, 640:768] = (t_ % 32 != 0) * np.ones((128, 1))
    for j in range(4):
        c[:, 768 + j] = (i // 32 == j)
        c[:, 772 + j * 128:772 + (j + 1) * 128] = c[:, 256:384]
        c[:, 1284 + j * 128:1284 + (j + 1) * 128] = c[:, 640:768]
    return c


def phase_hgrn(cx, S, xin, oat, g_ap, w_in_ap, lgT_ap, gnT_ap, T, layer, tag):
    nc = cx.nc
    ident = cx.cbf[:, 0:128]
    ones = cx.cbf[:, 128:256]
    with contextlib.ExitStack() as st:
        old = cx.stack
        cx.stack = st
        W = sb(cx, tag + "W", [128, 8, 4096], BF16)
        Wb = Buf()
        gain = sb(cx, tag + "g", [128, D], F32)
        lg = sb(cx, tag + "lg", [128, lgT_ap.shape[1], 8], F32)
        lb = sb(cx, tag + "lb", [128, 8], F32)
        oml = sb(cx, tag + "oml", [128, 8], F32)
        gn = sb(cx, tag + "gn", [128, 8], F32)
        zer = sb(cx, tag + "zer", [128, 128], F32)
        lbf = sb(cx, tag + "lbf", [128, 8, 128], F32)
        omlf = sb(cx, tag + "omlf", [128, 8, 128], F32)
        gnf = sb(cx, tag + "gnf", [128, 8, 128], F32)
        parb = Buf()
        xts = [sb(cx, tag + "xt%d" % i, [128, D], F32) for i in range(2)]
        xtb = [Buf() for _ in range(2)]
        hT = [sb(cx, tag + "hT%d" % i, [128, 8, 128], BF16) for i in range(2)]
        hTb = [Buf() for _ in range(2)]
        v = [sb(cx, tag + "v%d" % i, [128, D], BF16) for i in range(2)]
        vb = [Buf() for _ in range(2)]
        vm = [sb(cx, tag + "vm%d" % i, [128, 4, D], BF16) for i in range(2)]
        vmb = [Buf() for _ in range(2)]
        S32 = [sb(cx, tag + "S32_%d" % i, [128, 512], F32) for i in range(2)]
        S32b = [Buf() for _ in range(2)]
        Sbf = [sb(cx, tag + "Sbf_%d" % i, [128, 8, 512], BF16) for i in range(2)]
        Sbfb = [[Buf() for _ in range(8)] for _ in range(2)]
        NB = 2

        def mk(name, dt, w=512, n=NB):
            return ([sb(cx, tag + name + str(i), [128, w], dt) for i in range(n)],
                    [Buf() for _ in range(n)])
        q32, q32b = mk("q", F32)
        f32t, f32b = mk("f", F32)
        gt, gtb = mk("gt", F32)
        lf, lfb = mk("lf", F32)
        kk, kkb = mk("kk", F32)
        gc, gcb = mk("gc", F32)
        eq, eqb = mk("eq", F32)
        ek, ekb = mk("ek", F32)
        qg, qgb = mk("qg", BF16)
        kg, kgb = mk("kg", BF16)
        kgT, kgTb = mk("kgT", BF16)
        AT, ATb = mk("AT", BF16)
        sq, sqb = mk("sq", BF16)
        rs, rsb = mk("rs", F32)
        on, onb = mk("on", F32)
        tmp, tmpb = mk("tmp", F32)
        oa, oab = mk("oa", BF16, n=4)

        S.dma("sp", gain[:], g_ap.partition_broadcast(128), writes=[cx.constb])
        S.dma("sp", lg[:], lgT_ap, writes=[parb])
        S.dma("sp", gn[:], gnT_ap, writes=[parb])
        if layer == 0:
            S.op("dve", lambda e: e.memset(lb[:], 0.0), writes=[parb])
        else:
            S.op("dve", lambda e: e.tensor_tensor(out=lb[:], in0=lg[:, 1, :], in1=lg[:, 0, :],
                                                  op=ALU.subtract), reads=[parb], writes=[parb])
            S.op("act", lambda e: e.activation(out=lb[:], in_=lb[:], func=AF.Sigmoid),
                 reads=[parb], writes=[parb])
        S.op("dve", lambda e: e.tensor_scalar(out=oml[:], in0=lb[:], scalar1=-1.0, scalar2=1.0,
                                              op0=ALU.mult, op1=ALU.add), reads=[parb], writes=[parb])
        S.op("dve", lambda e: e.memset(zer[:], 0.0), writes=[parb])
        for hd in range(8):
            for (dst, src) in ((lbf, lb), (omlf, oml), (gnf, gn)):
                S.op("dve", lambda e: e.tensor_scalar(out=dst[:, hd, :], in0=zer[:], scalar1=src[:, hd:hd + 1],
                                                      scalar2=None, op0=ALU.add),
                     reads=[parb], writes=[parb])
        for gi in range(2):
            S.op("dve", lambda e: e.memset(S32[gi][:], 0.0), writes=[S32b[gi]])
            S.op("pool", lambda e: e.memset(Sbf[gi][:], 0.0), writes=Sbfb[gi])
        load_weight_bf16(cx, S, W, Wb, w_in_ap, 8, P_IN, c0=0, c1=4096, cstep=2048)

        pQ, pQb = cx.psA[0], cx.psAb[0]
        pF, pFb = cx.psA[1], cx.psAb[1]
        pG, pGb = cx.psB[0], cx.psBb[0]
        pAT, pATb = cx.psB[1], cx.psBb[1]
        pS, pSb = cx.psY[0], cx.psYb[0]
        pO, pOb = cx.psY[1], cx.psYb[1]
        pK, pKb = cx.psT[1], cx.psTb[1]
        oat_v = oat.rearrange("(h p) t -> p h t", p=128)
        nt = T // 128
        gcount = 0
        for ti in range(nt):
            t0 = ti * 128
            xt, xb = xts[ti % 2], xtb[ti % 2]
            hTt, hTtb = hT[ti % 2], hTb[ti % 2]
            vt, vtb = v[ti % 2], vb[ti % 2]
            vmt, vmtb = vm[ti % 2], vmb[ti % 2]
            S.dma("sp", xt[:], xin[t0:t0 + 128, :], writes=[xb])
            emit_norm_T(cx, S, xt, xb, gain, hTt, hTtb, 0, tag, fixed=0)
            for half in range(2):
                for k in range(8):
                    S.op("pe", lambda e, k=k: e.matmul(pS[:], hTt[:, k, :],
                                                       W[:, k, 2048 + half * 512:2048 + (half + 1) * 512],
                                                       start=(k == 0), stop=(k == 7)),
                         reads=[hTtb, Wb], writes=[pSb], inc=(k == 7))
                S.op("act", lambda e: e.copy(out=vt[:, half * 512:(half + 1) * 512], in_=pS[:]),
                     reads=[pSb], writes=[vtb])
            for j in range(4):
                S.op("pool", lambda e: e.tensor_scalar(out=vmt[:, j, :], in0=vt[:], scalar1=cx.cmask[:, j:j + 1],
                                                       scalar2=None, op0=ALU.mult),
                     reads=[vtb, cx.constb], writes=[vmtb])
            for gi in range(2):
                b = gcount % NB
                gcount += 1
                c0 = gi * 512
                hsl = slice(gi * 4, gi * 4 + 4)
                for (pt, ptb, base) in ((pQ, pQb, 0), (pF, pFb, 1024), (pG, pGb, 3072)):
                    for hh in range(4):
                        for k in range(8):
                            S.op("pe", lambda e, k=k: e.matmul(
                                pt[:, hh * 128:(hh + 1) * 128],
                                W[:, k, base + c0 + hh * 128:base + c0 + (hh + 1) * 128],
                                hTt[:, k, :], start=(k == 0), stop=(k == 7)),
                                reads=[hTtb, Wb], writes=[ptb], inc=(k == 7 and hh == 3))
                S.op("act", lambda e: e.activation(out=q32[b][:], in_=pQ[:], func=AF.Silu),
                     reads=[pQb], writes=[q32b[b]])
                S.op("act", lambda e: e.activation(out=gt[b][:], in_=pG[:], func=AF.Silu),
                     reads=[pGb], writes=[gtb[b]])
                S.op("act", lambda e: e.activation(out=f32t[b][:], in_=pF[:], func=AF.Sigmoid),
                     reads=[pFb], writes=[f32b[b]])
                S.op("pool", lambda e: e.tensor_tensor(out=gt[b][:], in0=gt[b][:],
                                                       in1=gnf[:, hsl, :].rearrange("p h t -> p (h t)"), op=ALU.mult),
                     reads=[gtb[b], parb], writes=[gtb[b]])
                if layer != 0:
                    S.op("dve", lambda e: e.tensor_tensor(out=f32t[b][:], in0=f32t[b][:],
                                                          in1=omlf[:, hsl, :].rearrange("p h t -> p (h t)"),
                                                          op=ALU.mult),
                         reads=[f32b[b], parb], writes=[f32b[b]])
                    S.op("dve", lambda e: e.tensor_tensor(out=f32t[b][:], in0=f32t[b][:],
                                                          in1=lbf[:, hsl, :].rearrange("p h t -> p (h t)"),
                                                          op=ALU.add),
                         reads=[f32b[b], parb], writes=[f32b[b]])
                S.op("act", lambda e: e.activation(out=lf[b][:], in_=f32t[b][:], func=AF.Ln),
                     reads=[f32b[b]], writes=[lfb[b]])
                S.op("pool", lambda e: e.tensor_scalar(out=kk[b][:], in0=f32t[b][:], scalar1=-1.0,
                                                       scalar2=1.0, op0=ALU.mult, op1=ALU.add),
                     reads=[f32b[b]], writes=[kkb[b]])
                S.op("dve", lambda e: e.tensor_tensor_scan(out=gc[b][:], data0=cx.rmask4[:], data1=lf[b][:],
                                                           initial=0.0, op0=ALU.mult, op1=ALU.add),
                     reads=[lfb[b], cx.constb], writes=[gcb[b]])
                S.op("act", lambda e: e.activation(out=eq[b][:], in_=gc[b][:], func=AF.Exp),
                     reads=[gcb[b]], writes=[eqb[b]])
                S.op("act", lambda e: e.activation(out=ek[b][:], in_=gc[b][:], func=AF.Exp, scale=-1.0),
                     reads=[gcb[b]], writes=[ekb[b]])
                S.op("pool", lambda e: e.tensor_tensor(out=kg[b][:], in0=kk[b][:], in1=ek[b][:], op=ALU.mult),
                     reads=[kkb[b], ekb[b]], writes=[kgb[b]])
                S.op("dve", lambda e: e.tensor_tensor(out=qg[b][:], in0=q32[b][:], in1=eq[b][:], op=ALU.mult),
                     reads=[q32b[b], eqb[b]], writes=[qgb[b]])
                for hh in range(4):
                    S.op("pe", lambda e: e.transpose(pK[:, hh * 128:(hh + 1) * 128],
                                                     kg[b][:, hh * 128:(hh + 1) * 128], ident),
                         reads=[kgb[b], cx.constb], writes=[pKb], inc=(hh == 3))
                S.op("act", lambda e: e.copy(out=kgT[b][:], in_=pK[:, 0:512]),
                     reads=[pKb], writes=[kgTb[b]])
                for hh in range(4):
                    hs = slice(hh * 128, (hh + 1) * 128)
                    S.op("pe", lambda e: e.matmul(pAT[:, hs], kg[b][:, hs], qg[b][:, hs], start=True, stop=True),
                         reads=[kgb[b], qgb[b]], writes=[pATb], inc=(hh == 3))
                S.op("dve", lambda e: e.tensor_tensor(out=AT[b][:], in0=pAT[:], in1=cx.bmask4[:], op=ALU.mult),
                     reads=[pATb, cx.constb], writes=[ATb[b]])
                eqv = eq[b][:].rearrange("p (h t) -> p h t", h=4)
                for j in range(4):
                    for hh in range(4):
                        hs = slice(hh * 128, (hh + 1) * 128)
                        S.op("pe", lambda e: e.matmul(pS[:, hs], kgT[b][:, hs],
                                                      vmt[:, j, c0 + hh * 128:c0 + (hh + 1) * 128],
                                                      start=True, stop=True),
                             reads=[kgTb[b], vmtb], writes=[pSb], inc=(hh == 3))
                    eglb = eqv[:, :, 32 * j + 31:32 * j + 32].to_broadcast([128, 4, 128])
                    S.op("dve", lambda e: e.tensor_tensor(out=tmp[b][:], in0=S32[gi][:], in1=pS[:], op=ALU.add),
                         reads=[S32b[gi], pSb], writes=[tmpb[b]])
                    S.op("dve", lambda e: e.tensor_tensor(out=S32[gi][:].rearrange("p (h t) -> p h t", h=4),
                                                          in0=tmp[b][:].rearrange("p (h t) -> p h t", h=4),
                                                          in1=eglb, op=ALU.mult),
                         reads=[tmpb[b], eqb[b]], writes=[S32b[gi]])
                    sl = ((ti % 2) * 4 + j + 1) % 8
                    S.op("act", lambda e: e.copy(out=Sbf[gi][:, sl, :], in_=S32[gi][:]),
                         reads=[S32b[gi]], writes=[Sbfb[gi][sl]])
                for hh in range(4):
                    hs = slice(hh * 128, (hh + 1) * 128)
                    S.op("pe", lambda e: e.matmul(pO[:, hs], vt[:, c0 + hh * 128:c0 + (hh + 1) * 128], AT[b][:, hs],
                                                  start=True, stop=False),
                         reads=[vtb, ATb[b]], writes=[pOb], inc=False)
                    for j in range(4):
                        sl = ((ti % 2) * 4 + j) % 8
                        S.op("pe", lambda e: e.matmul(pO[:, hh * 128 + 32 * j:hh * 128 + 32 * j + 32],
                                                      Sbf[gi][:, sl, hs], qg[b][:, hh * 128 + 32 * j:hh * 128 + 32 * j + 32],
                                                      start=False, stop=(j == 3)),
                             reads=[Sbfb[gi][sl], qgb[b]], writes=[pOb], inc=(j == 3 and hh == 3))
                S.op("act", lambda e: e.activation(out=sq[b][:], in_=pO[:], func=AF.Square),
                     reads=[pOb], writes=[sqb[b]])
                S.op("pe", lambda e: e.matmul(pAT[:], ones, sq[b][:], start=True, stop=True),
                     reads=[sqb[b], cx.constb], writes=[pATb])
                S.op("act", lambda e: e.activation(out=rs[b][:], in_=pAT[:], func=AF.Sqrt,
                                                   scale=1.0 / 128, bias=EPS),
                     reads=[pATb], writes=[rsb[b]])
                S.op("dve", lambda e: e.reciprocal(out=rs[b][:], in_=rs[b][:]),
                     reads=[rsb[b]], writes=[rsb[b]])
                S.op("dve", lambda e: e.tensor_tensor(out=on[b][:], in0=pO[:], in1=rs[b][:], op=ALU.mult),
                     reads=[pOb, rsb[b]], writes=[onb[b]])
                ob = gcount % 4
                S.op("pool", lambda e: e.tensor_tensor(out=oa[ob][:], in0=on[b][:], in1=gt[b][:], op=ALU.mult),
                     reads=[onb[b], gtb[b]], writes=[oab[ob]])
                S.dma("sp", oat_v[:, gi * 4:gi * 4 + 4, t0:t0 + 128],
                      oa[ob][:].rearrange("p (h t) -> p h t", h=4), reads=[oab[ob]])
        cx.stack = old
        barrier(cx, S)


def phase_attn_proj(cx, S, xin, qkv, g_ap, w_in_ap, qn_ap, kn_ap, cos_ap, sin_ap, T, tag):
    nc = cx.nc
    with contextlib.ExitStack() as st:
        old = cx.stack
        cx.stack = st
        W = sb(cx, tag + "W", [128, 8, 4608], BF16)
        Wb = Buf()
        gain = sb(cx, tag + "g", [128, D], F32)
        gq = sb(cx, tag + "gq", [128, 2, 3, 128], F32)
        parb = Buf()
        xts = [sb(cx, tag + "xt%d" % i, [128, D], F32) for i in range(2)]
        xtb = [Buf() for _ in range(2)]
        hT = [sb(cx, tag + "hT%d" % i, [128, 8, 128], BF16) for i in range(2)]
        hTb = [Buf() for _ in range(2)]
        cs = [sb(cx, tag + "cs%d" % i, [128, 2, 512], F32) for i in range(2)]
        csb = [Buf() for _ in range(2)]
        ob = [sb(cx, tag + "ob%d" % i, [128, 3, 1536], BF16) for i in range(2)]
        obb = [Buf() for _ in range(2)]
        NB = 2
        ssq = [sb(cx, tag + "ssq%d" % i, [128, 8], F32) for i in range(NB)]
        ssqb = [Buf() for _ in range(NB)]
        jk = [sb(cx, tag + "jk%d" % i, [128, 128], BF16) for i in range(NB)]
        jkb = [Buf() for _ in range(NB)]
        qn = [sb(cx, tag + "qn%d" % i, [128, 512], F32) for i in range(NB)]
        qnb = [Buf() for _ in range(NB)]
        t1 = [sb(cx, tag + "t1%d" % i, [128, 512], F32) for i in range(NB)]
        t1b = [Buf() for _ in range(NB)]
        t2 = [sb(cx, tag + "t2%d" % i, [128, 512], F32) for i in range(NB)]
        t2b = [Buf() for _ in range(NB)]
        S.dma("sp", gain[:], g_ap.partition_broadcast(128), writes=[cx.constb])
        for g in range(3):
            S.dma("sp", gq[:, 0, g, :], qn_ap[g:g + 1, :].partition_broadcast(128), writes=[parb])
            S.dma("sp", gq[:, 1, g, :], kn_ap[g:g + 1, :].partition_broadcast(128), writes=[parb])
        load_weight_bf16(cx, S, W, Wb, w_in_ap, 8, P_IN, c0=4096, c1=8704, cstep=1536)
        pss = [(cx.psA[0], cx.psAb[0]), (cx.psA[1], cx.psAb[1]), (cx.psB[0], cx.psBb[0]), (cx.psB[1], cx.psBb[1])]
        nt = T // 128
        cnt = 0
        for ti in range(nt):
            t0 = ti * 128
            xt, xb = xts[ti % 2], xtb[ti % 2]
            hTt, hTtb = hT[ti % 2], hTb[ti % 2]
            c_, c_b = cs[ti % 2], csb[ti % 2]
            o_, o_b = ob[ti % 2], obb[ti % 2]
            S.dma("sp", xt[:], xin[t0:t0 + 128, :], writes=[xb])
            S.dma("sp", c_[:, 0, :], cos_ap[t0:t0 + 128, :], writes=[c_b])
            S.dma("sp", c_[:, 1, :], sin_ap[t0:t0 + 128, :], writes=[c_b])
            emit_norm_T(cx, S, xt, xb, gain, hTt, hTtb, 0, tag)
            for which in range(3):
                for g in range(3):
                    pt, ptb = pss[cnt % 4]
                    b = cnt % NB
                    cnt += 1
                    cb = which * 1536 + g * 512
                    for k in range(8):
                        S.op("pe", lambda e, k=k: e.matmul(pt[:], hTt[:, k, :], W[:, k, cb:cb + 512],
                                                           start=(k == 0), stop=(k == 7)),
                             reads=[hTtb, Wb], writes=[ptb], inc=(k == 7))
                    if which == 2:
                        S.op("act", lambda e: e.copy(out=o_[:, 2, g * 512:(g + 1) * 512], in_=pt[:]),
                             reads=[ptb], writes=[o_b])
                        continue
                    for hh in range(4):
                        S.op("act", lambda e: e.activation(out=jk[b][:], in_=pt[:, hh * 128:(hh + 1) * 128],
                                                           func=AF.Square, accum_out=ssq[b][:, hh:hh + 1]),
                             reads=[ptb], writes=[jkb[b], ssqb[b]])
                    S.op("act", lambda e: e.activation(out=ssq[b][:, 4:8], in_=ssq[b][:, 0:4], func=AF.Sqrt,
                                                       scale=1.0 / 128, bias=EPS),
                         reads=[ssqb[b]], writes=[ssqb[b]])
                    S.op("dve", lambda e: e.reciprocal(out=ssq[b][:, 4:8], in_=ssq[b][:, 4:8]),
                         reads=[ssqb[b]], writes=[ssqb[b]])
                    for hh in range(4):
                        S.op("dve", lambda e: e.scalar_tensor_tensor(
                            out=qn[b][:, hh * 128:(hh + 1) * 128], in0=pt[:, hh * 128:(hh + 1) * 128],
                            scalar=ssq[b][:, 4 + hh:5 + hh], in1=gq[:, which, g, :], op0=ALU.mult, op1=ALU.mult),
                            reads=[ptb, ssqb[b], parb], writes=[qnb[b]])
                    S.op("pool", lambda e: e.tensor_tensor(out=t1[b][:], in0=qn[b][:], in1=c_[:, 0, :], op=ALU.mult),
                         reads=[qnb[b], c_b], writes=[t1b[b]])
                    qv = qn[b][:].rearrange("p (h two d) -> p h two d", h=4, two=2)
                    sv = c_[:, 1, :].rearrange("p (h two d) -> p h two d", h=4, two=2)
                    tv = t2[b][:].rearrange("p (h two d) -> p h two d", h=4, two=2)
                    S.op("pool", lambda e: e.tensor_tensor(out=tv[:, :, 0, :], in0=qv[:, :, 1, :], in1=sv[:, :, 0, :],
                                                           op=ALU.mult),
                         reads=[qnb[b], c_b], writes=[t2b[b]])
                    S.op("pool", lambda e: e.tensor_tensor(out=tv[:, :, 1, :], in0=qv[:, :, 0, :], in1=sv[:, :, 1, :],
                                                           op=ALU.mult),
                         reads=[qnb[b], c_b], writes=[t2b[b]])
                    S.op("dve", lambda e: e.tensor_tensor(out=o_[:, which, g * 512:(g + 1) * 512], in0=t1[b][:],
                                                          in1=t2[b][:], op=ALU.add),
                         reads=[t1b[b], t2b[b]], writes=[o_b])
            S.dma("sp", qkv[t0:t0 + 128, :, :], o_[:], reads=[o_b])
        cx.stack = old
        barrier(cx, S)


ATT_PAT = ((128, 1), (512, 4), (2048, 16))
AC_STAGE = 4


def phase_attn_core(cx, S, qkv, obt, T, tag):
    nc = cx.nc
    ident = cx.cbf[:, 0:128]
    ones = cx.cbf[:, 128:256]
    amask = cx.cbf[:, 384:640]
    SB = 2048
    scale = 128 ** -0.5
    with contextlib.ExitStack() as st:
        old = cx.stack
        cx.stack = st
        num = sb(cx, tag + "num", [128, 4, SB], F32)
        den = sb(cx, tag + "den", [128, 4, SB], F32)
        numb = [Buf() for _ in range(4)]
        denb = [Buf() for _ in range(4)]
        NB = 2
        qt = [sb(cx, tag + "q%d" % i, [128, 512], BF16) for i in range(NB)]
        kp = [sb(cx, tag + "kp%d" % i, [128, 512], BF16) for i in range(NB)]
        ko = [sb(cx, tag + "ko%d" % i, [128, 512], BF16) for i in range(NB)]
        vp = [sb(cx, tag + "vp%d" % i, [128, 512], BF16) for i in range(NB)]
        vo = [sb(cx, tag + "vo%d" % i, [128, 512], BF16) for i in range(NB)]
        qtb, kpb, kob, vpb, vob = ([Buf() for _ in range(NB)] for _ in range(5))
        qT = [sb(cx, tag + "qT%d" % i, [128, 512], BF16) for i in range(NB)]
        kpT = [sb(cx, tag + "kpT%d" % i, [128, 512], BF16) for i in range(NB)]
        koT = [sb(cx, tag + "koT%d" % i, [128, 512], BF16) for i in range(NB)]
        qTb, kpTb, koTb = ([Buf() for _ in range(NB)] for _ in range(3))
        PT = [sb(cx, tag + "PT%d" % i, [128, 256], BF16) for i in range(4)]
        PTb = [Buf() for _ in range(4)]
        obf = [sb(cx, tag + "obf%d" % i, [128, SB], BF16) for i in range(2)]
        obfb = [Buf() for _ in range(2)]
        blk = 0
        hc = 0
        tc = 0
        oc = 0
        for sbi in range(T // SB):
            for g, (_, d) in enumerate(ATT_PAT):
                nper = SB // (128 * d)
                for r in range(d):
                    for nn in range(nper):
                        n = sbi * nper + nn
                        b = blk % NB
                        blk += 1
                        ts = n * 128 * d + r
                        te = ts + 127 * d + 1
                        off = ts - sbi * SB
                        has_prev = n > 0
                        c0, c1 = g * 512, (g + 1) * 512
                        S.dma("sp", qt[b][:], qkv[ts:te:d, 0, c0:c1], writes=[qtb[b]])
                        S.dma("sp", ko[b][:], qkv[ts:te:d, 1, c0:c1], writes=[kob[b]])
                        S.dma("sp", vo[b][:], qkv[ts:te:d, 2, c0:c1], writes=[vob[b]])
                        if has_prev:
                            ps_ = ts - 128 * d
                            S.dma("sp", kp[b][:], qkv[ps_:ps_ + 127 * d + 1:d, 1, c0:c1], writes=[kpb[b]])
                            S.dma("sp", vp[b][:], qkv[ps_:ps_ + 127 * d + 1:d, 2, c0:c1], writes=[vpb[b]])
                        todo = [(qt[b], qtb[b], qT[b], qTb[b]), (ko[b], kob[b], koT[b], koTb[b])]
                        if has_prev:
                            todo.append((kp[b], kpb[b], kpT[b], kpTb[b]))
                        for (src, srcb, dst, dstb) in todo:
                            pT, pTb = cx.psT[tc % 2], cx.psTb[tc % 2]
                            tc += 1
                            for hh in range(4):
                                S.op("pe", lambda e: e.transpose(pT[:, hh * 128:(hh + 1) * 128],
                                                                 src[:, hh * 128:(hh + 1) * 128], ident),
                                     reads=[srcb, cx.constb], writes=[pTb], inc=(hh == 3))
                            S.op("act", lambda e: e.copy(out=dst[:], in_=pT[:, 0:512]),
                                 reads=[pTb], writes=[dstb])
                        for hh in range(4 if AC_STAGE >= 2 else 0):
                            hs = slice(hh * 128, (hh + 1) * 128)
                            psS, psSb = cx.psA[hc % 2], cx.psAb[hc % 2]
                            psN, psNb = cx.psB[hc % 2], cx.psBb[hc % 2]
                            psD, psDb = cx.psY[hc % 2], cx.psYb[hc % 2]
                            P, Pb = PT[hc % 4], PTb[hc % 4]
                            hc += 1
                            lo = 0 if has_prev else 128
                            if has_prev:
                                S.op("pe", lambda e: e.matmul(psS[:, 0:128], kpT[b][:, hs], qT[b][:, hs],
                                                              start=True, stop=True),
                                     reads=[kpTb[b], qTb[b]], writes=[psSb], inc=False)
                            S.op("pe", lambda e: e.matmul(psS[:, 128:256], koT[b][:, hs], qT[b][:, hs],
                                                          start=True, stop=True),
                                 reads=[koTb[b], qTb[b]], writes=[psSb])
                            S.op("act", lambda e: e.activation(out=P[:, lo:256], in_=psS[:, lo:256], func=AF.Exp,
                                                               scale=scale),
                                 reads=[psSb], writes=[Pb])
                            S.op("pool", lambda e: e.tensor_tensor(out=P[:, lo:256], in0=P[:, lo:256],
                                                                   in1=amask[:, lo:256], op=ALU.mult),
                                 reads=[Pb, cx.constb], writes=[Pb])
                            if AC_STAGE < 3:
                                continue
                            if has_prev:
                                S.op("pe", lambda e: e.matmul(psN[:, 0:128], vp[b][:, hs], P[:, 0:128],
                                                              start=True, stop=False),
                                     reads=[vpb[b], Pb], writes=[psNb], inc=False)
                            S.op("pe", lambda e: e.matmul(psN[:, 0:128], vo[b][:, hs], P[:, 128:256],
                                                          start=(not has_prev), stop=True),
                                 reads=[vob[b], Pb], writes=[psNb], inc=False)
                            if has_prev:
                                S.op("pe", lambda e: e.matmul(psD[:, 0:128], ones, P[:, 0:128],
                                                              start=True, stop=False),
                                     reads=[cx.constb, Pb], writes=[psDb], inc=False)
                            S.op("pe", lambda e: e.matmul(psD[:, 0:128], ones, P[:, 128:256],
                                                          start=(not has_prev), stop=True),
                                 reads=[cx.constb, Pb], writes=[psDb])
                            if AC_STAGE < 3.2 or (AC_STAGE in (3.25, 3.27) and g > 0):
                                S.op("pe", lambda e: e.matmul(psN[:, 256:384], ones, P[:, 128:256], start=True, stop=True),
                                     reads=[cx.constb, Pb], writes=[psNb])
                                continue
                            if AC_STAGE < 3.4 and g > 0:
                                S.op("pe", lambda e: e.matmul(psN[:, 256:384], ones, P[:, 128:256], start=True, stop=True),
                                     reads=[cx.constb, Pb], writes=[psNb])
                                continue
                            nv = num[:, hh, off:off + 127 * d + 1:d]
                            dv = den[:, hh, off:off + 127 * d + 1:d]
                            if g == 0:
                                if AC_STAGE != 3.27:
                                    S.op("act", lambda e: e.copy(out=nv, in_=psN[:, 0:128]),
                                         reads=[psNb], writes=[numb[hh]])
                                if AC_STAGE != 3.25:
                                    S.op("act", lambda e: e.copy(out=dv, in_=psD[:, 0:128]),
                                         reads=[psDb], writes=[denb[hh]])
                            else:
                                S.op("dve", lambda e: e.tensor_tensor(out=nv, in0=nv, in1=psN[:, 0:128], op=ALU.add),
                                     reads=[psNb, numb[hh]], writes=[numb[hh]])
                                S.op("dve", lambda e: e.tensor_tensor(out=dv, in0=dv, in1=psD[:, 0:128], op=ALU.add),
                                     reads=[psDb, denb[hh]], writes=[denb[hh]])
            for hh in range(4 if AC_STAGE >= 4 else 0):
                o_, o_b = obf[oc % 2], obfb[oc % 2]
                oc += 1
                S.op("dve", lambda e: e.reciprocal(out=den[:, hh, :], in_=den[:, hh, :]),
                     reads=[denb[hh]], writes=[denb[hh]])
                S.op("pool", lambda e: e.tensor_tensor(out=o_[:], in0=num[:, hh, :], in1=den[:, hh, :], op=ALU.mult),
                     reads=[numb[hh], denb[hh]], writes=[o_b])
                S.dma("sp", obt[hh * 128:(hh + 1) * 128, sbi * SB:(sbi + 1) * SB], o_[:], reads=[o_b])
        cx.stack = old
        barrier(cx, S)


def phase_out(cx, S, xin, xout, oat, obt, g_ap, w_in_ap, wa_ap, wb_ap, wo_ap, T, tag):
    nc = cx.nc
    with contextlib.ExitStack() as st:
        old = cx.stack
        cx.stack = st
        WG = sb(cx, tag + "WG", [128, 8, 2048], BF16)
        WA = sb(cx, tag + "WA", [128, 8, D], BF16)
        WB = sb(cx, tag + "WB", [128, 4, D], BF16)
        WO = sb(cx, tag + "WO", [128, 8, D], BF16)
        Wb = Buf()
        gain = sb(cx, tag + "g", [128, D], F32)
        parb = Buf()
        xts = [sb(cx, tag + "xt%d" % i, [128, D], F32) for i in range(2)]
        xtb = [Buf() for _ in range(2)]
        hT = [sb(cx, tag + "hT%d" % i, [128, 8, 128], BF16) for i in range(2)]
        hTb = [Buf() for _ in range(2)]
        oaT = [sb(cx, tag + "oaT%d" % i, [128, 8, 128], BF16) for i in range(2)]
        oaTb = [Buf() for _ in range(2)]
        obT = [sb(cx, tag + "obT%d" % i, [128, 4, 128], BF16) for i in range(2)]
        obTb = [Buf() for _ in range(2)]
        sga = [sb(cx, tag + "sga%d" % i, [128, D], F32) for i in range(2)]
        sgb = [sb(cx, tag + "sgb%d" % i, [128, D], F32) for i in range(2)]
        sgab = [Buf() for _ in range(2)]
        sgbb = [Buf() for _ in range(2)]
        m1 = [sb(cx, tag + "m1%d" % i, [128, 512], F32) for i in range(2)]
        m2 = [sb(cx, tag + "m2%d" % i, [128, 512], F32) for i in range(2)]
        m1b = [Buf() for _ in range(2)]
        m2b = [Buf() for _ in range(2)]
        mg = [sb(cx, tag + "mg%d" % i, [128, D], BF16) for i in range(2)]
        mgb = [Buf() for _ in range(2)]
        mT = [sb(cx, tag + "mT%d" % i, [128, 8, 128], BF16) for i in range(2)]
        mTb = [Buf() for _ in range(2)]
        S.dma("sp", gain[:], g_ap.partition_broadcast(128), writes=[cx.constb])
        load_weight_bf16(cx, S, WG, Wb, w_in_ap, 8, P_IN, c0=8704, c1=10752, cstep=2048)
        load_weight_bf16(cx, S, WA, Wb, wa_ap, 8, D, cstep=1024)
        load_weight_bf16(cx, S, WB, Wb, wb_ap, 4, D, cstep=1024)
        load_weight_bf16(cx, S, WO, Wb, wo_ap, 8, D, cstep=1024)
        oat_v = oat.rearrange("(c p) t -> p c t", p=128)
        obt_v = obt.rearrange("(c p) t -> p c t", p=128)
        nt = T // 128
        for ti in range(nt):
            t0 = ti * 128
            i2 = ti % 2
            xt, xb = xts[i2], xtb[i2]
            S.dma("sp", xt[:], xin[t0:t0 + 128, :], writes=[xb])
            S.dma("sp", oaT[i2][:], oat_v[:, :, t0:t0 + 128], writes=[oaTb[i2]])
            S.dma("sp", obT[i2][:], obt_v[:, :, t0:t0 + 128], writes=[obTb[i2]])
            emit_norm_T(cx, S, xt, xb, gain, hT[i2], hTb[i2], 0, tag)
            for n in range(2):
                for (pt, ptb, base, dst, dstb) in ((cx.psA[n], cx.psAb[n], 0, sga[i2], sgab[i2]),
                                                   (cx.psB[n], cx.psBb[n], 1024, sgb[i2], sgbb[i2])):
                    for k in range(8):
                        S.op("pe", lambda e, k=k: e.matmul(pt[:], hT[i2][:, k, :],
                                                           WG[:, k, base + n * 512:base + (n + 1) * 512],
                                                           start=(k == 0), stop=(k == 7)),
                             reads=[hTb[i2], Wb], writes=[ptb], inc=(k == 7))
                    S.op("act", lambda e: e.activation(out=dst[:, n * 512:(n + 1) * 512], in_=pt[:],
                                                       func=AF.Sigmoid),
                         reads=[ptb], writes=[dstb])
            for n in range(2):
                ns = slice(n * 512, (n + 1) * 512)
                pa, pab = cx.psY[0], cx.psYb[0]
                pb_, pbb = cx.psY[1], cx.psYb[1]
                for c in range(8):
                    S.op("pe", lambda e, c=c: e.matmul(pa[:], oaT[i2][:, c, :], WA[:, c, ns],
                                                       start=(c == 0), stop=(c == 7)),
                         reads=[oaTb[i2], Wb], writes=[pab], inc=(c == 7))
                for c in range(4):
                    S.op("pe", lambda e, c=c: e.matmul(pb_[:], obT[i2][:, c, :], WB[:, c, ns],
                                                       start=(c == 0), stop=(c == 3)),
                         reads=[obTb[i2], Wb], writes=[pbb], inc=(c == 3))
                S.op("dve", lambda e: e.tensor_tensor(out=m1[n][:], in0=sga[i2][:, ns], in1=pa[:], op=ALU.mult),
                     reads=[sgab[i2], pab], writes=[m1b[n]])
                S.op("dve", lambda e: e.tensor_tensor(out=m2[n][:], in0=sgb[i2][:, ns], in1=pb_[:], op=ALU.mult),
                     reads=[sgbb[i2], pbb], writes=[m2b[n]])
                S.op("pool", lambda e: e.tensor_tensor(out=mg[i2][:, ns], in0=m1[n][:], in1=m2[n][:], op=ALU.add),
                     reads=[m1b[n], m2b[n]], writes=[mgb[i2]])
            pT, pTb = cx.psT[ti % 2], cx.psTb[ti % 2]
            for c in range(8):
                S.op("pe", lambda e, c=c: e.transpose(pT[:, c * 128:(c + 1) * 128],
                                                      mg[i2][:, c * 128:(c + 1) * 128], cx.cbf[:, 0:128]),
                     reads=[mgb[i2], cx.constb], writes=[pTb], inc=(c == 7))
            S.op("act", lambda e: e.copy(out=mT[i2][:], in_=pT[:].rearrange("p (k t) -> p k t", k=8)),
                 reads=[pTb], writes=[mTb[i2]])
            for n in range(2):
                ns = slice(n * 512, (n + 1) * 512)
                pt, ptb = cx.psA[n], cx.psAb[n]
                for c in range(8):
                    S.op("pe", lambda e, c=c: e.matmul(pt[:], mT[i2][:, c, :], WO[:, c, ns],
                                                       start=(c == 0), stop=(c == 7)),
                         reads=[mTb[i2], Wb], writes=[ptb], inc=(c == 7))
                S.op("dve", lambda e: e.tensor_tensor(out=xt[:, ns], in0=xt[:, ns], in1=pt[:], op=ALU.add),
                     reads=[ptb, xb], writes=[xb])
            S.dma("sp", xout[t0:t0 + 128, :], xt[:], reads=[xb])
        cx.stack = old
        barrier(cx, S)


def make_ctx(nc, stack):
    cx = Ctx()
    cx.nc = nc
    cx.stack = stack
    cx.rr = 0
    S = Sched(nc, stack)
    cx.cbf = sb(cx, "cbf", [128, 640], BF16)
    cx.rmask = sb(cx, "rmask", [128, 128], F32)
    cx.cmask = sb(cx, "cmask", [128, 4], F32)
    cx.bmask4 = sb(cx, "bmask4", [128, 512], BF16)
    cx.rmask4 = sb(cx, "rmask4", [128, 512], F32)
    cx.ident = cx.cbf[:, 0:128]
    cx.constb = Buf()
    cx.junk = [sb(cx, "junk%d" % i, [128, D], BF16) for i in range(2)]
    cx.junkb = [Buf() for _ in range(2)]
    cx.ss = [sb(cx, "ss%d" % i, [128, 4], F32) for i in range(2)]
    cx.ssb = [Buf() for _ in range(2)]
    cx.hbf = [sb(cx, "hbf%d" % i, [128, D], BF16) for i in range(2)]
    cx.hbfb = [Buf() for _ in range(2)]
    cx.psT = [ps(cx, "psT%d" % i, [128, 1024], BF16) for i in range(2)]
    cx.psTb = [Buf() for _ in range(2)]
    cx.psA = [ps(cx, "psA%d" % i, [128, 512], F32) for i in range(2)]
    cx.psAb = [Buf() for _ in range(2)]
    cx.psB = [ps(cx, "psB%d" % i, [128, 512], F32) for i in range(2)]
    cx.psBb = [Buf() for _ in range(2)]
    cx.psY = [ps(cx, "psY%d" % i, [128, 512], F32) for i in range(2)]
    cx.psYb = [Buf() for _ in range(2)]
    return cx, S


def build_nc(T, depth=2, phases=None, debug=False):
    nc = bass.Bass("TRN2", target_bir_lowering=False)
    dt = nc.dram_tensor
    x = dt("x", [T, D], F32, kind="ExternalInput").ap()
    cst = dt("cst", [128, 1796], F32, kind="ExternalInput").ap()
    cos4 = dt("cos4", [T, 512], F32, kind="ExternalInput").ap()
    sin4 = dt("sin4", [T, 512], F32, kind="ExternalInput").ap()
    f1n = dt("ffn1_norm", [depth, D], F32, kind="ExternalInput").ap()
    f1i = dt("ffn1_w_in", [depth, D, 2 * DFF], F32, kind="ExternalInput").ap()
    f1o = dt("ffn1_w_out", [depth, DFF, D], F32, kind="ExternalInput").ap()
    mn = dt("mix_norm", [depth, D], F32, kind="ExternalInput").ap()
    win = dt("w_in", [depth, D, P_IN], F32, kind="ExternalInput").ap()
    lgT = dt("lgT", [128, depth, 8], F32, kind="ExternalInput").ap()
    gnT = dt("gnT", [depth, 128, 8], F32, kind="ExternalInput").ap()
    aqn = dt("attn_q_norm", [depth, 3, 128], F32, kind="ExternalInput").ap()
    akn = dt("attn_k_norm", [depth, 3, 128], F32, kind="ExternalInput").ap()
    wa = dt("w_branch_a", [depth, D, D], F32, kind="ExternalInput").ap()
    wb = dt("w_branch_b", [depth, 512, D], F32, kind="ExternalInput").ap()
    wo = dt("w_out", [depth, D, D], F32, kind="ExternalInput").ap()
    f2n = dt("ffn2_norm", [depth, D], F32, kind="ExternalInput").ap()
    f2i = dt("ffn2_w_in", [depth, D, 2 * DFF], F32, kind="ExternalInput").ap()
    f2o = dt("ffn2_w_out", [depth, DFF, D], F32, kind="ExternalInput").ap()
    y = dt("y", [T, D], F32, kind="ExternalOutput").ap()
    kind = "ExternalOutput" if debug else "Internal"
    xa = dt("xa", [T, D], F32, kind=kind).ap()
    xb_ = dt("xb", [T, D], F32, kind=kind).ap()
    oat = dt("oat", [D, T], BF16, kind=kind).ap()
    obt = dt("obt", [512, T], BF16, kind=kind).ap()
    qkv = dt("qkv", [T, 3, 1536], BF16, kind=kind).ap()
    with contextlib.ExitStack() as stack:
        cx, S = make_ctx(nc, stack)
        load_consts(cx, S, cst)
        cur = x
        for l in range(depth):
            last = (l == depth - 1)
            tg = "L%d" % l
            P = phases
            if P is None or "f1" in P:
                phase_ffn(cx, S, cur, xa, f1n[l:l + 1, :], f1i[l], f1o[l], T, tg + "f1")
            if P is None or "hg" in P:
                phase_hgrn(cx, S, xa, oat, mn[l:l + 1, :], win[l], lgT, gnT[l], T, l, tg + "hg")
            if P is None or "ap" in P:
                phase_attn_proj(cx, S, xa, qkv, mn[l:l + 1, :], win[l], aqn[l], akn[l], cos4, sin4, T, tg + "ap")
            if P is None or "ac" in P:
                phase_attn_core(cx, S, qkv, obt, T, tg + "ac")
            if P is None or "po" in P:
                phase_out(cx, S, xa, xb_, oat, obt, mn[l:l + 1, :], win[l], wa[l], wb[l], wo[l], T, tg + "po")
            if P is None or "f2" in P:
                phase_ffn(cx, S, xb_, y if last else xa, f2n[l:l + 1, :], f2i[l], f2o[l], T, tg + "f2")
            cur = xa
        barrier(cx, S)
        cx.ninst = S.ninst
        print("ninst", S.ninst, {k: v for k, v in S.cnt.items()})
    return nc


def rope_host(T):
    pos = np.arange(T, dtype=np.float32)
    inv = (np.float32(10000.0) ** (-np.arange(0, 128, 2, dtype=np.float32) / np.float32(128))).astype(np.float32)
    ang = pos[:, None] * inv[None, :]
    ang = np.concatenate([ang, ang], axis=-1).astype(np.float32)
    cos = np.cos(ang).astype(np.float32)
    sin = np.sin(ang).astype(np.float32)
    sin[:, :64] = -sin[:, :64]
    return np.tile(cos, (1, 4)), np.tile(sin, (1, 4))


def make_in_maps(inputs, T, depth=2):
    cos4, sin4 = rope_host(T)
    base = {k: np.ascontiguousarray(v, dtype=np.float32) for k, v in inputs.items() if k not in ("x", "hgrn_lb_logits", "hgrn_out_norm")}
    base["cst"] = host_consts()
    base["cos4"] = np.ascontiguousarray(cos4)
    base["sin4"] = np.ascontiguousarray(sin4)
    lg = np.asarray(inputs["hgrn_lb_logits"], np.float32)
    base["lgT"] = np.ascontiguousarray(lg.reshape(depth, 8, 128).transpose(2, 0, 1))
    gn = np.asarray(inputs["hgrn_out_norm"], np.float32)
    base["gnT"] = np.ascontiguousarray(gn.reshape(depth, 8, 128).transpose(0, 2, 1))
    x = np.asarray(inputs["x"], np.float32)
    maps = []
    for c in range(NCORES):
        m = dict(base)
        m["x"] = np.ascontiguousarray(x[c % x.shape[0], :T])
        maps.append(m)
    return maps


def kernel(**inputs):
    x = np.asarray(inputs["x"])
    B, T, _ = x.shape
    nc = build_nc(T, depth=2)
    maps = make_in_maps(inputs, T, depth=2)
    res = run_bass_kernel_spmd(nc, maps, core_ids=list(range(NCORES)))
    out = np.stack([np.asarray(res.results[b]["y"], dtype=np.float32) for b in range(B)], axis=0)
    return out


def build_ac_only(T):
    nc = bass.Bass("TRN2", target_bir_lowering=False)
    dt = nc.dram_tensor
    cst = dt("cst", [128, 1796], F32, kind="ExternalInput").ap()
    qkv = dt("qkv", [T, 3, 1536], BF16, kind="ExternalInput").ap()
    obt = dt("obt", [512, T], BF16, kind="ExternalOutput").ap()
    with contextlib.ExitStack() as stack:
        cx, S = make_ctx(nc, stack)
        load_consts(cx, S, cst)
        phase_attn_core(cx, S, qkv, obt, T, "ac")
        barrier(cx, S)
    return nc
```

```python
import contextlib
import numpy as np
import concourse.bass as bass
import concourse.mybir as mybir
from concourse.bass_utils import run_bass_kernel_spmd

F32 = mybir.dt.float32
BF16 = mybir.dt.bfloat16
AF = mybir.ActivationFunctionType
ALU = mybir.AluOpType
AX = mybir.AxisListType

D = 1024
DFF = 2816
P_IN = 10752
EPS = 1e-6
NCORES = 8


class Buf:
    __slots__ = ("w", "r", "name")

    def __init__(self, name=""):
        self.w = None
        self.r = {}
        self.name = name


class Sched:
    def __init__(self, nc, stack, ndma=6):
        self.nc = nc
        self.h = {"pe": nc.tensor, "act": nc.scalar, "dve": nc.vector,
                  "pool": nc.gpsimd, "sp": nc.sync}
        self.sems = {}
        self.cnt = {}
        for e in ("pe", "act", "dve", "pool"):
            self.sems[e] = stack.enter_context(nc.semaphore("s_" + e))
            self.cnt[e] = 0
        self.seen = {e: {} for e in self.h}
        self.dq = {}
        for q in ("sp", "pool", "act"):
            names = ["d_%s%d" % (q, i) for i in range(ndma)]
            for n in names:
                self.sems[n] = stack.enter_context(nc.semaphore(n))
                self.cnt[n] = 0
            self.dq[q] = [names, 0]
        self.ninst = 0

    def _wait(self, e, deps):
        need = {}
        for d in deps:
            if d is None:
                continue
            k, v, src = d
            if src == e and e == "pe":
                continue
            if self.seen[e].get(k, 0) >= v:
                continue
            if need.get(k, 0) < v:
                need[k] = v
        for k, v in need.items():
            self.h[e].wait_ge(self.sems[k], v)
            self.seen[e][k] = v
            self.ninst += 1

    def _deps(self, e, reads, writes):
        deps = []
        for b in reads:
            deps.append(b.w)
        for b in writes:
            deps.append(b.w)
            for r in b.r.values():
                if r[2] == e and e != "pool":
                    continue
                deps.append(r)
        return deps

    def _record(self, ev, reads, writes):
        for b in reads:
            old = b.r.get(ev[0])
            if old is None or old[1] < ev[1]:
                b.r[ev[0]] = ev
        for b in writes:
            b.w = ev
            b.r = {}

    def op(self, e, fn, reads=(), writes=(), inc=True):
        self._wait(e, self._deps(e, reads, writes))
        ins = fn(self.h[e])
        if inc:
            self.cnt[e] += 1
            ins.then_inc(self.sems[e], 1)
            ev = (e, self.cnt[e], e)
        else:
            ev = (e, self.cnt[e] + 1, e)
        self.ninst += 1
        self._record(ev, reads, writes)
        return ev

    def dma(self, q, out, in_, reads=(), writes=(), **kw):
        names, idx = self.dq[q]
        k = names[idx % len(names)]
        self.dq[q][1] = idx + 1
        deps = self._deps("dma_issue", reads, writes)
        if self.cnt[k] > 0:
            deps.append((k, self.cnt[k], "dma"))
        self._wait(q, deps)
        ins = self.h[q].dma_start(out=out, in_=in_, **kw)
        self.cnt[k] += 16
        ins.then_inc(self.sems[k], 16)
        ev = (k, self.cnt[k], "dma")
        self.ninst += 1
        self._record(ev, reads, writes)
        return ev

    def finish(self, e="sp"):
        for k, c in self.cnt.items():
            if c > 0 and self.seen[e].get(k, 0) < c:
                self.h[e].wait_ge(self.sems[k], c)
                self.seen[e][k] = c


class Ctx:
    pass


def sb(cx, name, shape, dt):
    t = cx.stack.enter_context(cx.nc.sbuf_tensor("sb_" + name, list(shape), dt))
    return t


def ps(cx, name, shape, dt):
    t = cx.stack.enter_context(cx.nc.psum_tensor("ps_" + name, list(shape), dt))
    return t


def emit_norm_T(cx, S, xt, xb, gain, hT, hTb, col0, tag, fixed=None):
    nc = cx.nc
    i = cx.rr % 2 if fixed is None else fixed
    cx.rr += 1
    junk, junkb = cx.junk[i], cx.junkb[i]
    ss, ssb = cx.ss[i], cx.ssb[i]
    hb, hbb = cx.hbf[i], cx.hbfb[i]
    pT, pTb = cx.psT[i], cx.psTb[i]
    S.op("act", lambda e: e.activation(out=junk[:], in_=xt[:], func=AF.Square,
                                       accum_out=ss[:, 0:1]),
         reads=[xb], writes=[junkb, ssb])
    S.op("act", lambda e: e.activation(out=ss[:, 1:2], in_=ss[:, 0:1], func=AF.Sqrt,
                                       scale=1.0 / D, bias=EPS),
         reads=[ssb], writes=[ssb])
    S.op("dve", lambda e: e.reciprocal(out=ss[:, 2:3], in_=ss[:, 1:2]),
         reads=[ssb], writes=[ssb])
    S.op("dve", lambda e: e.scalar_tensor_tensor(out=hb[:], in0=xt[:], scalar=ss[:, 2:3],
                                                 in1=gain[:], op0=ALU.mult, op1=ALU.mult),
         reads=[xb, ssb, cx.constb], writes=[hbb])
    for k in range(8):
        S.op("pe", lambda e, k=k: e.transpose(pT[:, k * 128:(k + 1) * 128],
                                              hb[:, k * 128:(k + 1) * 128], cx.ident[:]),
             reads=[hbb, cx.constb], writes=[pTb], inc=(k == 7))
    S.op("act", lambda e: e.copy(out=hT[:, :, col0:col0 + 128],
                                 in_=pT[:].rearrange("p (k t) -> p k t", k=8)),
         reads=[pTb], writes=[hTb])


def load_weight_bf16(cx, S, dst, dstb, src_ap, nk, ncols, c0=0, c1=None, cstep=1408):
    c1 = ncols if c1 is None else c1
    v = src_ap.rearrange("(k p) c -> p k c", p=128)
    c = c0
    while c < c1:
        ce = min(c1, c + cstep)
        for k in range(nk):
            S.dma("pool", dst[:, k, c - c0:ce - c0], v[:, k, c:ce], writes=[dstb])
        c = ce


def phase_ffn(cx, S, xin, xout, g_ap, w_in_ap, w_out_ap, T, tag):
    nc = cx.nc
    G = 256
    with contextlib.ExitStack() as st:
        old = cx.stack
        cx.stack = st
        w1 = sb(cx, tag + "w1", [128, 8, 2 * DFF], BF16)
        w2 = sb(cx, tag + "w2", [128, 22, D], BF16)
        gain = sb(cx, tag + "g", [128, D], F32)
        xts = [sb(cx, tag + "xt%d" % i, [128, D], F32) for i in range(4)]
        hT = [sb(cx, tag + "hT%d" % i, [128, 8, G], BF16) for i in range(2)]
        uT = sb(cx, tag + "uT", [128, 22, G], BF16)
        sa = [sb(cx, tag + "sa%d" % i, [128, G], F32) for i in range(2)]
        w1b, w2b, uTb = Buf(), Buf(), Buf()
        xtb = [Buf() for _ in range(4)]
        hTb = [Buf() for _ in range(2)]
        sab = [Buf() for _ in range(2)]
        S.dma("sp", gain[:], g_ap.partition_broadcast(128), writes=[cx.constb])
        load_weight_bf16(cx, S, w1, w1b, w_in_ap, 8, 2 * DFF)
        load_weight_bf16(cx, S, w2, w2b, w_out_ap, 22, D, cstep=1024)
        ng = T // G
        for gi in range(ng):
            hTg, hTgb = hT[gi % 2], hTb[gi % 2]
            for i in range(2):
                xt, xb = xts[(gi % 2) * 2 + i], xtb[(gi % 2) * 2 + i]
                r0 = gi * G + i * 128
                S.dma("sp", xt[:], xin[r0:r0 + 128, :], writes=[xb])
                emit_norm_T(cx, S, xt, xb, gain, hTg, hTgb, i * 128, tag)
            for j in range(22):
                pA, pAb = cx.psA[j % 2], cx.psAb[j % 2]
                pB, pBb = cx.psB[j % 2], cx.psBb[j % 2]
                for k in range(8):
                    S.op("pe", lambda e, k=k: e.matmul(pA[:, 0:G], w1[:, k, j * 128:(j + 1) * 128],
                                                       hTg[:, k, :], start=(k == 0), stop=(k == 7)),
                         reads=[w1b, hTgb], writes=[pAb], inc=(k == 7))
                for k in range(8):
                    S.op("pe", lambda e, k=k: e.matmul(pB[:, 0:G],
                                                       w1[:, k, DFF + j * 128:DFF + (j + 1) * 128],
                                                       hTg[:, k, :], start=(k == 0), stop=(k == 7)),
                         reads=[w1b, hTgb], writes=[pBb], inc=(k == 7))
                s_, s_b = sa[j % 2], sab[j % 2]
                S.op("act", lambda e: e.activation(out=s_[:], in_=pA[:, 0:G], func=AF.Silu),
                     reads=[pAb], writes=[s_b])
                S.op("dve", lambda e: e.tensor_tensor(out=uT[:, j, :], in0=s_[:], in1=pB[:, 0:G],
                                                      op=ALU.mult),
                     reads=[s_b, pBb], writes=[uTb])
            for i in range(2):
                xt, xb = xts[(gi % 2) * 2 + i], xtb[(gi % 2) * 2 + i]
                r0 = gi * G + i * 128
                for n in range(2):
                    pY, pYb = cx.psY[n], cx.psYb[n]
                    for j in range(22):
                        S.op("pe", lambda e, j=j: e.matmul(pY[:], uT[:, j, i * 128:(i + 1) * 128],
                                                           w2[:, j, n * 512:(n + 1) * 512],
                                                           start=(j == 0), stop=(j == 21)),
                             reads=[uTb, w2b], writes=[pYb], inc=(j == 21))
                    S.op("dve", lambda e: e.scalar_tensor_tensor(
                        out=xt[:, n * 512:(n + 1) * 512], in0=pY[:], scalar=0.5,
                        in1=xt[:, n * 512:(n + 1) * 512], op0=ALU.mult, op1=ALU.add),
                        reads=[pYb, xb], writes=[xb])
                S.dma("sp", xout[r0:r0 + 128, :], xt[:], reads=[xb])
        cx.stack = old
        S.finish("sp")
        barrier(cx, S)


def barrier(cx, S):
    for e in ("pe", "act", "dve", "pool", "sp"):
        S.finish(e)


def make_ctx(nc, stack):
    cx = Ctx()
    cx.nc = nc
    cx.stack = stack
    cx.rr = 0
    S = Sched(nc, stack)
    cx.ident = sb(cx, "ident", [128, 128], BF16)
    cx.constb = Buf()
    cx.junk = [sb(cx, "junk%d" % i, [128, D], BF16) for i in range(2)]
    cx.junkb = [Buf() for _ in range(2)]
    cx.ss = [sb(cx, "ss%d" % i, [128, 4], F32) for i in range(2)]
    cx.ssb = [Buf() for _ in range(2)]
    cx.hbf = [sb(cx, "hbf%d" % i, [128, D], BF16) for i in range(2)]
    cx.hbfb = [Buf() for _ in range(2)]
    cx.psT = [ps(cx, "psT%d" % i, [128, 1024], BF16) for i in range(2)]
    cx.psTb = [Buf() for _ in range(2)]
    cx.psA = [ps(cx, "psA%d" % i, [128, 512], F32) for i in range(2)]
    cx.psAb = [Buf() for _ in range(2)]
    cx.psB = [ps(cx, "psB%d" % i, [128, 512], F32) for i in range(2)]
    cx.psBb = [Buf() for _ in range(2)]
    cx.psY = [ps(cx, "psY%d" % i, [128, 512], F32) for i in range(2)]
    cx.psYb = [Buf() for _ in range(2)]
    return cx, S


def load_consts(cx, S, ident_ap):
    S.dma("pool", cx.ident[:], ident_ap, writes=[cx.constb])


def load_consts(cx, S, cst_ap):
    S.dma("pool", cx.cbf[:], cst_ap[:, 0:640], writes=[cx.constb])
    S.dma("sp", cx.rmask[:], cst_ap[:, 640:768], writes=[cx.constb])
    S.dma("sp", cx.cmask[:], cst_ap[:, 768:772], writes=[cx.constb])
    S.dma("pool", cx.bmask4[:], cst_ap[:, 772:1284], writes=[cx.constb])
    S.dma("sp", cx.rmask4[:], cst_ap[:, 1284:1796], writes=[cx.constb])


def host_consts():
    c = np.zeros((128, 1796), np.float32)
    i = np.arange(128)
    c[:, 0:128] = np.eye(128)
    c[:, 128:256] = 1.0
    s_, t_ = i[:, None], i[None, :]
    c[:, 256:384] = ((s_ // 32 == t_ // 32) & (s_ <= t_))
    c[:, 384:512] = (s_ >= t_)
    c[:, 512:640] = (s_ <= t_)
    c[:, 640:768] = (t_ % 32 != 0) * np.ones((128, 1))
    for j in range(4):
        c[:, 768 + j] = (i // 32 == j)
        c[:, 772 + j * 128:772 + (j + 1) * 128] = c[:, 256:384]
        c[:, 1284 + j * 128:1284 + (j + 1) * 128] = c[:, 640:768]
    return c


def phase_hgrn(cx, S, xin, oat, g_ap, w_in_ap, lgT_ap, gnT_ap, T, layer, tag):
    nc = cx.nc
    ident = cx.cbf[:, 0:128]
    ones = cx.cbf[:, 128:256]
    with contextlib.ExitStack() as st:
        old = cx.stack
        cx.stack = st
        W = sb(cx, tag + "W", [128, 8, 4096], BF16)
        Wb = Buf()
        gain = sb(cx, tag + "g", [128, D], F32)
        lg = sb(cx, tag + "lg", [128, lgT_ap.shape[1], 8], F32)
        lb = sb(cx, tag + "lb", [128, 8], F32)
        oml = sb(cx, tag + "oml", [128, 8], F32)
        gn = sb(cx, tag + "gn", [128, 8], F32)
        zer = sb(cx, tag + "zer", [128, 128], F32)
        lbf = sb(cx, tag + "lbf", [128, 8, 128], F32)
        omlf = sb(cx, tag + "omlf", [128, 8, 128], F32)
        gnf = sb(cx, tag + "gnf", [128, 8, 128], F32)
        parb = Buf()
        xts = [sb(cx, tag + "xt%d" % i, [128, D], F32) for i in range(2)]
        xtb = [Buf() for _ in range(2)]
        hT = [sb(cx, tag + "hT%d" % i, [128, 8, 128], BF16) for i in range(2)]
        hTb = [Buf() for _ in range(2)]
        v = [sb(cx, tag + "v%d" % i, [128, D], BF16) for i in range(2)]
        vb = [Buf() for _ in range(2)]
        S32 = [sb(cx, tag + "S32_%d" % i, [128, 512], F32) for i in range(2)]
        S32b = [Buf() for _ in range(2)]
        Sbf = [sb(cx, tag + "Sbf_%d" % i, [128, 8, 512], BF16) for i in range(2)]
        Sbfb = [[Buf() for _ in range(8)] for _ in range(2)]
        NB = 2

        def mk(name, dt, w=512, n=NB):
            return ([sb(cx, tag + name + str(i), [128, w], dt) for i in range(n)],
                    [Buf() for _ in range(n)])
        q32, q32b = mk("q", F32)
        f32t, f32b = mk("f", F32)
        gt, gtb = mk("gt", F32, n=4)
        lf, lfb = mk("lf", F32)
        kk, kkb = mk("kk", F32)
        gc, gcb = mk("gc", F32)
        eq, eqb = mk("eq", F32, n=4)
        ek, ekb = mk("ek", F32)
        qg, qgb = mk("qg", BF16, n=4)
        kg, kgb = mk("kg", BF16)
        kgT, kgTb = mk("kgT", BF16, w=2048)
        AT, ATb = mk("AT", BF16, n=4)
        sq, sqb = mk("sq", BF16)
        rs, rsb = mk("rs", F32)
        on, onb = mk("on", F32)
        tmp, tmpb = mk("tmp", F32)
        oa, oab = mk("oa", BF16, n=4)

        S.dma("sp", gain[:], g_ap.partition_broadcast(128), writes=[cx.constb])
        S.dma("sp", lg[:], lgT_ap, writes=[parb])
        S.dma("sp", gn[:], gnT_ap, writes=[parb])
        if layer == 0:
            S.op("dve", lambda e: e.memset(lb[:], 0.0), writes=[parb])
        else:
            S.op("dve", lambda e: e.tensor_tensor(out=lb[:], in0=lg[:, 1, :], in1=lg[:, 0, :],
                                                  op=ALU.subtract), reads=[parb], writes=[parb])
            S.op("act", lambda e: e.activation(out=lb[:], in_=lb[:], func=AF.Sigmoid),
                 reads=[parb], writes=[parb])
        S.op("dve", lambda e: e.tensor_scalar(out=oml[:], in0=lb[:], scalar1=-1.0, scalar2=1.0,
                                              op0=ALU.mult, op1=ALU.add), reads=[parb], writes=[parb])
        S.op("dve", lambda e: e.memset(zer[:], 0.0), writes=[parb])
        for hd in range(8):
            for (dst, src) in ((lbf, lb), (omlf, oml), (gnf, gn)):
                S.op("dve", lambda e: e.tensor_scalar(out=dst[:, hd, :], in0=zer[:], scalar1=src[:, hd:hd + 1],
                                                      scalar2=None, op0=ALU.add),
                     reads=[parb], writes=[parb])
        for gi in range(2):
            S.op("dve", lambda e: e.memset(S32[gi][:], 0.0), writes=[S32b[gi]])
            S.op("pool", lambda e: e.memset(Sbf[gi][:], 0.0), writes=Sbfb[gi])
        load_weight_bf16(cx, S, W, Wb, w_in_ap, 8, P_IN, c0=0, c1=4096, cstep=2048)

        pQ, pQb = cx.psA[0], cx.psAb[0]
        pF, pFb = cx.psA[1], cx.psAb[1]
        pG, pGb = cx.psB[0], cx.psBb[0]
        pAT, pATb = cx.psB[1], cx.psBb[1]
        pS, pSb = cx.psY[0], cx.psYb[0]
        pO, pOb = cx.psY[1], cx.psYb[1]
        pK, pKb = cx.psT[1], cx.psTb[1]
        oat_v = oat.rearrange("(h p) t -> p h t", p=128)
        nt = T // 128
        for ti in range(nt):
            t0 = ti * 128
            xt, xb = xts[ti % 2], xtb[ti % 2]
            hTt, hTtb = hT[ti % 2], hTb[ti % 2]
            vt, vtb = v[ti % 2], vb[ti % 2]
            S.dma("sp", xt[:], xin[t0:t0 + 128, :], writes=[xb])
            emit_norm_T(cx, S, xt, xb, gain, hTt, hTtb, 0, tag, fixed=0)
            for half in range(2):
                for k in range(8):
                    S.op("pe", lambda e, k=k: e.matmul(pS[:], hTt[:, k, :],
                                                       W[:, k, 2048 + half * 512:2048 + (half + 1) * 512],
                                                       start=(k == 0), stop=(k == 7)),
                         reads=[hTtb, Wb], writes=[pSb], inc=(k == 7))
                S.op("act", lambda e: e.copy(out=vt[:, half * 512:(half + 1) * 512], in_=pS[:]),
                     reads=[pSb], writes=[vtb])
            for gi in range(2):
                b = gi
                b4 = (2 * ti + gi) % 4
                c0 = gi * 512
                hsl = slice(gi * 4, gi * 4 + 4)
                for (pt, ptb, base) in ((pQ, pQb, 0), (pF, pFb, 1024), (pG, pGb, 3072)):
                    for hh in range(4):
                        for k in range(8):
                            S.op("pe", lambda e, k=k: e.matmul(
                                pt[:, hh * 128:(hh + 1) * 128],
                                W[:, k, base + c0 + hh * 128:base + c0 + (hh + 1) * 128],
                                hTt[:, k, :], start=(k == 0), stop=(k == 7)),
                                reads=[hTtb, Wb], writes=[ptb], inc=(k == 7 and hh == 3))
                S.op("act", lambda e: e.activation(out=q32[b][:], in_=pQ[:], func=AF.Silu),
                     reads=[pQb], writes=[q32b[b]])
                S.op("act", lambda e: e.activation(out=gt[b4][:], in_=pG[:], func=AF.Silu),
                     reads=[pGb], writes=[gtb[b4]])
                S.op("act", lambda e: e.activation(out=f32t[b][:], in_=pF[:], func=AF.Sigmoid),
                     reads=[pFb], writes=[f32b[b]])
                S.op("pool", lambda e: e.tensor_tensor(out=gt[b4][:], in0=gt[b4][:],
                                                       in1=gnf[:, hsl, :].rearrange("p h t -> p (h t)"), op=ALU.mult),
                     reads=[gtb[b4], parb], writes=[gtb[b4]])
                if layer != 0:
                    S.op("dve", lambda e: e.tensor_tensor(out=f32t[b][:], in0=f32t[b][:],
                                                          in1=omlf[:, hsl, :].rearrange("p h t -> p (h t)"),
                                                          op=ALU.mult),
                         reads=[f32b[b], parb], writes=[f32b[b]])
                    S.op("dve", lambda e: e.tensor_tensor(out=f32t[b][:], in0=f32t[b][:],
                                                          in1=lbf[:, hsl, :].rearrange("p h t -> p (h t)"),
                                                          op=ALU.add),
                         reads=[f32b[b], parb], writes=[f32b[b]])
                S.op("act", lambda e: e.activation(out=lf[b][:], in_=f32t[b][:], func=AF.Ln),
                     reads=[f32b[b]], writes=[lfb[b]])
                S.op("act", lambda e: e.activation(out=kk[b][:], in_=f32t[b][:], func=AF.Identity,
                                                   scale=-1.0, bias=1.0),
                     reads=[f32b[b]], writes=[kkb[b]])
                S.op("dve", lambda e: e.tensor_tensor_scan(out=gc[b][:], data0=cx.rmask4[:], data1=lf[b][:],
                                                           initial=0.0, op0=ALU.mult, op1=ALU.add),
                     reads=[lfb[b], cx.constb], writes=[gcb[b]])
                S.op("act", lambda e: e.activation(out=eq[b4][:], in_=gc[b][:], func=AF.Exp),
                     reads=[gcb[b]], writes=[eqb[b4]])
                S.op("act", lambda e: e.activation(out=ek[b][:], in_=gc[b][:], func=AF.Exp, scale=-1.0),
                     reads=[gcb[b]], writes=[ekb[b]])
                S.op("dve", lambda e: e.tensor_tensor(out=kg[b][:], in0=kk[b][:], in1=ek[b][:], op=ALU.mult),
                     reads=[kkb[b], ekb[b]], writes=[kgb[b]])
                S.op("dve", lambda e: e.tensor_tensor(out=qg[b4][:], in0=q32[b][:], in1=eq[b4][:], op=ALU.mult),
                     reads=[q32b[b], eqb[b4]], writes=[qgb[b4]])
            for gi in range(2):
                b = gi
                b4 = (2 * ti + gi) % 4
                c0 = gi * 512
                hsl = slice(gi * 4, gi * 4 + 4)
                for hh in range(4):
                    S.op("pe", lambda e: e.transpose(pK[:, hh * 128:(hh + 1) * 128],
                                                     kg[b][:, hh * 128:(hh + 1) * 128], ident),
                         reads=[kgb[b], cx.constb], writes=[pKb], inc=(hh == 3))
                for j in range(4):
                    S.op("act", lambda e: e.activation(out=kgT[b][:, j * 512:(j + 1) * 512], in_=pK[:, 0:512],
                                                       func=AF.Copy, scale=cx.cmask[:, j:j + 1]),
                         reads=[pKb, cx.constb], writes=[kgTb[b]])
                for hh in range(4):
                    hs = slice(hh * 128, (hh + 1) * 128)
                    S.op("pe", lambda e: e.matmul(pAT[:, hs], kg[b][:, hs], qg[b4][:, hs], start=True, stop=True),
                         reads=[kgb[b], qgb[b4]], writes=[pATb], inc=(hh == 3))
                S.op("dve", lambda e: e.tensor_tensor(out=AT[b4][:], in0=pAT[:], in1=cx.bmask4[:], op=ALU.mult),
                     reads=[pATb, cx.constb], writes=[ATb[b4]])
                eqv = eq[b4][:].rearrange("p (h t) -> p h t", h=4)
                for j in range(4):
                    for hh in range(4):
                        hs = slice(hh * 128, (hh + 1) * 128)
                        S.op("pe", lambda e: e.matmul(pS[:, hs], kgT[b][:, j * 512 + hh * 128:j * 512 + (hh + 1) * 128],
                                                      vt[:, c0 + hh * 128:c0 + (hh + 1) * 128],
                                                      start=True, stop=True),
                             reads=[kgTb[b], vtb], writes=[pSb], inc=(hh == 3))
                    eglb = eqv[:, :, 32 * j + 31:32 * j + 32].to_broadcast([128, 4, 128])
                    S.op("dve", lambda e: e.tensor_tensor(out=tmp[b][:], in0=S32[gi][:], in1=pS[:], op=ALU.add),
                         reads=[S32b[gi], pSb], writes=[tmpb[b]])
                    S.op("dve", lambda e: e.tensor_tensor(out=S32[gi][:].rearrange("p (h t) -> p h t", h=4),
                                                          in0=tmp[b][:].rearrange("p (h t) -> p h t", h=4),
                                                          in1=eglb, op=ALU.mult),
                         reads=[tmpb[b], eqb[b4]], writes=[S32b[gi]])
                    sl = ((ti % 2) * 4 + j + 1) % 8
                    S.op("act", lambda e: e.copy(out=Sbf[gi][:, sl, :], in_=S32[gi][:]),
                         reads=[S32b[gi]], writes=[Sbfb[gi][sl]])
            for gi in range(2):
                b = gi
                b4 = (2 * ti + gi) % 4
                c0 = gi * 512
                hsl = slice(gi * 4, gi * 4 + 4)
                for hh in range(4):
                    hs = slice(hh * 128, (hh + 1) * 128)
                    S.op("pe", lambda e: e.matmul(pO[:, hs], vt[:, c0 + hh * 128:c0 + (hh + 1) * 128], AT[b4][:, hs],
                                                  start=True, stop=False),
                         reads=[vtb, ATb[b4]], writes=[pOb], inc=False)
                    for j in range(4):
                        sl = ((ti % 2) * 4 + j) % 8
                        S.op("pe", lambda e: e.matmul(pO[:, hh * 128 + 32 * j:hh * 128 + 32 * j + 32],
                                                      Sbf[gi][:, sl, hs], qg[b4][:, hh * 128 + 32 * j:hh * 128 + 32 * j + 32],
                                                      start=False, stop=(j == 3)),
                             reads=[Sbfb[gi][sl], qgb[b4]], writes=[pOb], inc=(j == 3 and hh == 3))
                S.op("act", lambda e: e.activation(out=sq[b][:], in_=pO[:], func=AF.Square),
                     reads=[pOb], writes=[sqb[b]])
                S.op("pe", lambda e: e.matmul(pAT[:], ones, sq[b][:], start=True, stop=True),
                     reads=[sqb[b], cx.constb], writes=[pATb])
                S.op("act", lambda e: e.activation(out=rs[b][:], in_=pAT[:], func=AF.Sqrt,
                                                   scale=1.0 / 128, bias=EPS),
                     reads=[pATb], writes=[rsb[b]])
                S.op("dve", lambda e: e.reciprocal(out=rs[b][:], in_=rs[b][:]),
                     reads=[rsb[b]], writes=[rsb[b]])
                S.op("dve", lambda e: e.tensor_tensor(out=on[b][:], in0=pO[:], in1=rs[b][:], op=ALU.mult),
                     reads=[pOb, rsb[b]], writes=[onb[b]])
                ob = (2 * ti + gi) % 4
                S.op("dve", lambda e: e.tensor_tensor(out=oa[ob][:], in0=on[b][:], in1=gt[b4][:], op=ALU.mult),
                     reads=[onb[b], gtb[b4]], writes=[oab[ob]])
                S.dma("sp", oat_v[:, gi * 4:gi * 4 + 4, t0:t0 + 128],
                      oa[ob][:].rearrange("p (h t) -> p h t", h=4), reads=[oab[ob]])
        cx.stack = old
        barrier(cx, S)


def phase_attn_proj(cx, S, xin, qkv, g_ap, w_in_ap, qn_ap, kn_ap, cos_ap, sin_ap, T, tag):
    nc = cx.nc
    with contextlib.ExitStack() as st:
        old = cx.stack
        cx.stack = st
        W = sb(cx, tag + "W", [128, 8, 4608], BF16)
        Wb = Buf()
        gain = sb(cx, tag + "g", [128, D], F32)
        gq = sb(cx, tag + "gq", [128, 2, 3, 128], F32)
        parb = Buf()
        xts = [sb(cx, tag + "xt%d" % i, [128, D], F32) for i in range(2)]
        xtb = [Buf() for _ in range(2)]
        hT = [sb(cx, tag + "hT%d" % i, [128, 8, 128], BF16) for i in range(2)]
        hTb = [Buf() for _ in range(2)]
        cs = [sb(cx, tag + "cs%d" % i, [128, 2, 512], F32) for i in range(2)]
        csb = [Buf() for _ in range(2)]
        ob = [sb(cx, tag + "ob%d" % i, [128, 3, 1536], BF16) for i in range(2)]
        obb = [Buf() for _ in range(2)]
        NB = 2
        ssq = [sb(cx, tag + "ssq%d" % i, [128, 8], F32) for i in range(NB)]
        ssqb = [Buf() for _ in range(NB)]
        jk = [sb(cx, tag + "jk%d" % i, [128, 128], BF16) for i in range(NB)]
        jkb = [Buf() for _ in range(NB)]
        qn = [sb(cx, tag + "qn%d" % i, [128, 512], F32) for i in range(NB)]
        qnb = [Buf() for _ in range(NB)]
        t1 = [sb(cx, tag + "t1%d" % i, [128, 512], F32) for i in range(NB)]
        t1b = [Buf() for _ in range(NB)]
        t2 = [sb(cx, tag + "t2%d" % i, [128, 512], F32) for i in range(NB)]
        t2b = [Buf() for _ in range(NB)]
        S.dma("sp", gain[:], g_ap.partition_broadcast(128), writes=[cx.constb])
        for g in range(3):
            S.dma("sp", gq[:, 0, g, :], qn_ap[g:g + 1, :].partition_broadcast(128), writes=[parb])
            S.dma("sp", gq[:, 1, g, :], kn_ap[g:g + 1, :].partition_broadcast(128), writes=[parb])
        load_weight_bf16(cx, S, W, Wb, w_in_ap, 8, P_IN, c0=4096, c1=8704, cstep=1536)
        pss = [(cx.psA[0], cx.psAb[0]), (cx.psA[1], cx.psAb[1]), (cx.psB[0], cx.psBb[0]), (cx.psB[1], cx.psBb[1])]
        nt = T // 128
        cnt = 0
        for ti in range(nt):
            t0 = ti * 128
            xt, xb = xts[ti % 2], xtb[ti % 2]
            hTt, hTtb = hT[ti % 2], hTb[ti % 2]
            c_, c_b = cs[ti % 2], csb[ti % 2]
            o_, o_b = ob[ti % 2], obb[ti % 2]
            S.dma("sp", xt[:], xin[t0:t0 + 128, :], writes=[xb])
            S.dma("sp", c_[:, 0, :], cos_ap[t0:t0 + 128, :], writes=[c_b])
            S.dma("sp", c_[:, 1, :], sin_ap[t0:t0 + 128, :], writes=[c_b])
            emit_norm_T(cx, S, xt, xb, gain, hTt, hTtb, 0, tag)
            for which in range(3):
                for g in range(3):
                    pt, ptb = pss[cnt % 4]
                    b = cnt % NB
                    cnt += 1
                    cb = which * 1536 + g * 512
                    for k in range(8):
                        S.op("pe", lambda e, k=k: e.matmul(pt[:], hTt[:, k, :], W[:, k, cb:cb + 512],
                                                           start=(k == 0), stop=(k == 7)),
                             reads=[hTtb, Wb], writes=[ptb], inc=(k == 7))
                    if which == 2:
                        S.op("act", lambda e: e.copy(out=o_[:, 2, g * 512:(g + 1) * 512], in_=pt[:]),
                             reads=[ptb], writes=[o_b])
                        continue
                    for hh in range(4):
                        S.op("act", lambda e: e.activation(out=jk[b][:], in_=pt[:, hh * 128:(hh + 1) * 128],
                                                           func=AF.Square, accum_out=ssq[b][:, hh:hh + 1]),
                             reads=[ptb], writes=[jkb[b], ssqb[b]])
                    S.op("act", lambda e: e.activation(out=ssq[b][:, 4:8], in_=ssq[b][:, 0:4], func=AF.Sqrt,
                                                       scale=1.0 / 128, bias=EPS),
                         reads=[ssqb[b]], writes=[ssqb[b]])
                    S.op("dve", lambda e: e.reciprocal(out=ssq[b][:, 4:8], in_=ssq[b][:, 4:8]),
                         reads=[ssqb[b]], writes=[ssqb[b]])
                    for hh in range(4):
                        S.op("dve", lambda e: e.scalar_tensor_tensor(
                            out=qn[b][:, hh * 128:(hh + 1) * 128], in0=pt[:, hh * 128:(hh + 1) * 128],
                            scalar=ssq[b][:, 4 + hh:5 + hh], in1=gq[:, which, g, :], op0=ALU.mult, op1=ALU.mult),
                            reads=[ptb, ssqb[b], parb], writes=[qnb[b]])
                    S.op("pool", lambda e: e.tensor_tensor(out=t1[b][:], in0=qn[b][:], in1=c_[:, 0, :], op=ALU.mult),
                         reads=[qnb[b], c_b], writes=[t1b[b]])
                    qv = qn[b][:].rearrange("p (h two d) -> p h two d", h=4, two=2)
                    sv = c_[:, 1, :].rearrange("p (h two d) -> p h two d", h=4, two=2)
                    tv = t2[b][:].rearrange("p (h two d) -> p h two d", h=4, two=2)
                    S.op("pool", lambda e: e.tensor_tensor(out=tv[:, :, 0, :], in0=qv[:, :, 1, :], in1=sv[:, :, 0, :],
                                                           op=ALU.mult),
                         reads=[qnb[b], c_b], writes=[t2b[b]])
                    S.op("pool", lambda e: e.tensor_tensor(out=tv[:, :, 1, :], in0=qv[:, :, 0, :], in1=sv[:, :, 1, :],
                                                           op=ALU.mult),
                         reads=[qnb[b], c_b], writes=[t2b[b]])
                    S.op("dve", lambda e: e.tensor_tensor(out=o_[:, which, g * 512:(g + 1) * 512], in0=t1[b][:],
                                                          in1=t2[b][:], op=ALU.add),
                         reads=[t1b[b], t2b[b]], writes=[o_b])
            S.dma("sp", qkv[t0:t0 + 128, :, :], o_[:], reads=[o_b])
        cx.stack = old
        barrier(cx, S)


ATT_PAT = ((128, 1), (512, 4), (2048, 16))
AC_STAGE = 4


def phase_attn_core(cx, S, qkv, obt, T, tag):
    nc = cx.nc
    ident = cx.cbf[:, 0:128]
    ones = cx.cbf[:, 128:256]
    amask = cx.cbf[:, 384:640]
    SB = 2048
    scale = 128 ** -0.5
    with contextlib.ExitStack() as st:
        old = cx.stack
        cx.stack = st
        num = sb(cx, tag + "num", [128, 4, SB], F32)
        den = sb(cx, tag + "den", [128, 4, SB], F32)
        numb = [Buf() for _ in range(4)]
        denb = [Buf() for _ in range(4)]
        NB = 2
        qt = [sb(cx, tag + "q%d" % i, [128, 512], BF16) for i in range(NB)]
        kp = [sb(cx, tag + "kp%d" % i, [128, 512], BF16) for i in range(NB)]
        ko = [sb(cx, tag + "ko%d" % i, [128, 512], BF16) for i in range(NB)]
        vp = [sb(cx, tag + "vp%d" % i, [128, 512], BF16) for i in range(NB)]
        vo = [sb(cx, tag + "vo%d" % i, [128, 512], BF16) for i in range(NB)]
        qtb, kpb, kob, vpb, vob = ([Buf() for _ in range(NB)] for _ in range(5))
        qT = [sb(cx, tag + "qT%d" % i, [128, 512], BF16) for i in range(NB)]
        kpT = [sb(cx, tag + "kpT%d" % i, [128, 512], BF16) for i in range(NB)]
        koT = [sb(cx, tag + "koT%d" % i, [128, 512], BF16) for i in range(NB)]
        qTb, kpTb, koTb = ([Buf() for _ in range(NB)] for _ in range(3))
        PT = [sb(cx, tag + "PT%d" % i, [128, 256], BF16) for i in range(4)]
        PTb = [Buf() for _ in range(4)]
        obf = [sb(cx, tag + "obf%d" % i, [128, SB], BF16) for i in range(2)]
        obfb = [Buf() for _ in range(2)]
        blk = 0
        hc = 0
        tc = 0
        oc = 0
        for sbi in range(T // SB):
            for g, (_, d) in enumerate(ATT_PAT):
                nper = SB // (128 * d)
                for r in range(d):
                    for nn in range(nper):
                        n = sbi * nper + nn
                        b = blk % NB
                        blk += 1
                        ts = n * 128 * d + r
                        te = ts + 127 * d + 1
                        off = ts - sbi * SB
                        has_prev = n > 0
                        c0, c1 = g * 512, (g + 1) * 512
                        S.dma("sp", qt[b][:], qkv[ts:te:d, 0, c0:c1], writes=[qtb[b]])
                        S.dma("sp", ko[b][:], qkv[ts:te:d, 1, c0:c1], writes=[kob[b]])
                        S.dma("sp", vo[b][:], qkv[ts:te:d, 2, c0:c1], writes=[vob[b]])
                        if has_prev:
                            ps_ = ts - 128 * d
                            S.dma("sp", kp[b][:], qkv[ps_:ps_ + 127 * d + 1:d, 1, c0:c1], writes=[kpb[b]])
                            S.dma("sp", vp[b][:], qkv[ps_:ps_ + 127 * d + 1:d, 2, c0:c1], writes=[vpb[b]])
                        todo = [(qt[b], qtb[b], qT[b], qTb[b]), (ko[b], kob[b], koT[b], koTb[b])]
                        if has_prev:
                            todo.append((kp[b], kpb[b], kpT[b], kpTb[b]))
                        for (src, srcb, dst, dstb) in todo:
                            pT, pTb = cx.psT[tc % 2], cx.psTb[tc % 2]
                            tc += 1
                            for hh in range(4):
                                S.op("pe", lambda e: e.transpose(pT[:, hh * 128:(hh + 1) * 128],
                                                                 src[:, hh * 128:(hh + 1) * 128], ident),
                                     reads=[srcb, cx.constb], writes=[pTb], inc=(hh == 3))
                            S.op("act", lambda e: e.copy(out=dst[:], in_=pT[:, 0:512]),
                                 reads=[pTb], writes=[dstb])
                        for hh in range(4 if AC_STAGE >= 2 else 0):
                            hs = slice(hh * 128, (hh + 1) * 128)
                            psS, psSb = cx.psA[hc % 2], cx.psAb[hc % 2]
                            psN, psNb = cx.psB[hc % 2], cx.psBb[hc % 2]
                            psD, psDb = cx.psY[hc % 2], cx.psYb[hc % 2]
                            P, Pb = PT[hc % 4], PTb[hc % 4]
                            hc += 1
                            lo = 0 if has_prev else 128
                            if has_prev:
                                S.op("pe", lambda e: e.matmul(psS[:, 0:128], kpT[b][:, hs], qT[b][:, hs],
                                                              start=True, stop=True),
                                     reads=[kpTb[b], qTb[b]], writes=[psSb], inc=False)
                            S.op("pe", lambda e: e.matmul(psS[:, 128:256], koT[b][:, hs], qT[b][:, hs],
                                                          start=True, stop=True),
                                 reads=[koTb[b], qTb[b]], writes=[psSb])
                            S.op("act", lambda e: e.activation(out=P[:, lo:256], in_=psS[:, lo:256], func=AF.Exp,
                                                               scale=scale),
                                 reads=[psSb], writes=[Pb])
                            S.op("pool", lambda e: e.tensor_tensor(out=P[:, lo:256], in0=P[:, lo:256],
                                                                   in1=amask[:, lo:256], op=ALU.mult),
                                 reads=[Pb, cx.constb], writes=[Pb])
                            if AC_STAGE < 3:
                                continue
                            if has_prev:
                                S.op("pe", lambda e: e.matmul(psN[:, 0:128], vp[b][:, hs], P[:, 0:128],
                                                              start=True, stop=False),
                                     reads=[vpb[b], Pb], writes=[psNb], inc=False)
                            S.op("pe", lambda e: e.matmul(psN[:, 0:128], vo[b][:, hs], P[:, 128:256],
                                                          start=(not has_prev), stop=True),
                                 reads=[vob[b], Pb], writes=[psNb], inc=False)
                            if has_prev:
                                S.op("pe", lambda e: e.matmul(psD[:, 0:128], ones, P[:, 0:128],
                                                              start=True, stop=False),
                                     reads=[cx.constb, Pb], writes=[psDb], inc=False)
                            S.op("pe", lambda e: e.matmul(psD[:, 0:128], ones, P[:, 128:256],
                                                          start=(not has_prev), stop=True),
                                 reads=[cx.constb, Pb], writes=[psDb])
                            if AC_STAGE < 3.2 or (AC_STAGE in (3.25, 3.27) and g > 0):
                                S.op("pe", lambda e: e.matmul(psN[:, 256:384], ones, P[:, 128:256], start=True, stop=True),
                                     reads=[cx.constb, Pb], writes=[psNb])
                                continue
                            if AC_STAGE < 3.4 and g > 0:
                                S.op("pe", lambda e: e.matmul(psN[:, 256:384], ones, P[:, 128:256], start=True, stop=True),
                                     reads=[cx.constb, Pb], writes=[psNb])
                                continue
                            nv = num[:, hh, off:off + 127 * d + 1:d]
                            dv = den[:, hh, off:off + 127 * d + 1:d]
                            if g == 0:
                                if AC_STAGE != 3.27:
                                    S.op("act", lambda e: e.copy(out=nv, in_=psN[:, 0:128]),
                                         reads=[psNb], writes=[numb[hh]])
                                if AC_STAGE != 3.25:
                                    S.op("act", lambda e: e.copy(out=dv, in_=psD[:, 0:128]),
                                         reads=[psDb], writes=[denb[hh]])
                            else:
                                S.op("dve", lambda e: e.tensor_tensor(out=nv, in0=nv, in1=psN[:, 0:128], op=ALU.add),
                                     reads=[psNb, numb[hh]], writes=[numb[hh]])
                                S.op("dve", lambda e: e.tensor_tensor(out=dv, in0=dv, in1=psD[:, 0:128], op=ALU.add),
                                     reads=[psDb, denb[hh]], writes=[denb[hh]])
            for hh in range(4 if AC_STAGE >= 4 else 0):
                o_, o_b = obf[oc % 2], obfb[oc % 2]
                oc += 1
                S.op("dve", lambda e: e.reciprocal(out=den[:, hh, :], in_=den[:, hh, :]),
                     reads=[denb[hh]], writes=[denb[hh]])
                S.op("pool", lambda e: e.tensor_tensor(out=o_[:], in0=num[:, hh, :], in1=den[:, hh, :], op=ALU.mult),
                     reads=[numb[hh], denb[hh]], writes=[o_b])
                S.dma("sp", obt[hh * 128:(hh + 1) * 128, sbi * SB:(sbi + 1) * SB], o_[:], reads=[o_b])
        cx.stack = old
        barrier(cx, S)


def phase_out(cx, S, xin, xout, oat, obt, g_ap, w_in_ap, wa_ap, wb_ap, wo_ap, T, tag):
    nc = cx.nc
    with contextlib.ExitStack() as st:
        old = cx.stack
        cx.stack = st
        WG = sb(cx, tag + "WG", [128, 8, 2048], BF16)
        WA = sb(cx, tag + "WA", [128, 8, D], BF16)
        WB = sb(cx, tag + "WB", [128, 4, D], BF16)
        WO = sb(cx, tag + "WO", [128, 8, D], BF16)
        Wb = Buf()
        gain = sb(cx, tag + "g", [128, D], F32)
        parb = Buf()
        xts = [sb(cx, tag + "xt%d" % i, [128, D], F32) for i in range(2)]
        xtb = [Buf() for _ in range(2)]
        hT = [sb(cx, tag + "hT%d" % i, [128, 8, 128], BF16) for i in range(2)]
        hTb = [Buf() for _ in range(2)]
        oaT = [sb(cx, tag + "oaT%d" % i, [128, 8, 128], BF16) for i in range(2)]
        oaTb = [Buf() for _ in range(2)]
        obT = [sb(cx, tag + "obT%d" % i, [128, 4, 128], BF16) for i in range(2)]
        obTb = [Buf() for _ in range(2)]
        sga = [sb(cx, tag + "sga%d" % i, [128, D], F32) for i in range(2)]
        sgb = [sb(cx, tag + "sgb%d" % i, [128, D], F32) for i in range(2)]
        sgab = [Buf() for _ in range(2)]
        sgbb = [Buf() for _ in range(2)]
        m1 = [sb(cx, tag + "m1%d" % i, [128, 512], F32) for i in range(2)]
        m2 = [sb(cx, tag + "m2%d" % i, [128, 512], F32) for i in range(2)]
        m1b = [Buf() for _ in range(2)]
        m2b = [Buf() for _ in range(2)]
        mg = [sb(cx, tag + "mg%d" % i, [128, D], BF16) for i in range(2)]
        mgb = [Buf() for _ in range(2)]
        mT = [sb(cx, tag + "mT%d" % i, [128, 8, 128], BF16) for i in range(2)]
        mTb = [Buf() for _ in range(2)]
        S.dma("sp", gain[:], g_ap.partition_broadcast(128), writes=[cx.constb])
        load_weight_bf16(cx, S, WG, Wb, w_in_ap, 8, P_IN, c0=8704, c1=10752, cstep=2048)
        load_weight_bf16(cx, S, WA, Wb, wa_ap, 8, D, cstep=1024)
        load_weight_bf16(cx, S, WB, Wb, wb_ap, 4, D, cstep=1024)
        load_weight_bf16(cx, S, WO, Wb, wo_ap, 8, D, cstep=1024)
        oat_v = oat.rearrange("(c p) t -> p c t", p=128)
        obt_v = obt.rearrange("(c p) t -> p c t", p=128)
        nt = T // 128
        for ti in range(nt):
            t0 = ti * 128
            i2 = ti % 2
            xt, xb = xts[i2], xtb[i2]
            S.dma("sp", xt[:], xin[t0:t0 + 128, :], writes=[xb])
            S.dma("sp", oaT[i2][:], oat_v[:, :, t0:t0 + 128], writes=[oaTb[i2]])
            S.dma("sp", obT[i2][:], obt_v[:, :, t0:t0 + 128], writes=[obTb[i2]])
            emit_norm_T(cx, S, xt, xb, gain, hT[i2], hTb[i2], 0, tag)
            for n in range(2):
                for (pt, ptb, base, dst, dstb) in ((cx.psA[n], cx.psAb[n], 0, sga[i2], sgab[i2]),
                                                   (cx.psB[n], cx.psBb[n], 1024, sgb[i2], sgbb[i2])):
                    for k in range(8):
                        S.op("pe", lambda e, k=k: e.matmul(pt[:], hT[i2][:, k, :],
                                                           WG[:, k, base + n * 512:base + (n + 1) * 512],
                                                           start=(k == 0), stop=(k == 7)),
                             reads=[hTb[i2], Wb], writes=[ptb], inc=(k == 7))
                    S.op("act", lambda e: e.activation(out=dst[:, n * 512:(n + 1) * 512], in_=pt[:],
                                                       func=AF.Sigmoid),
                         reads=[ptb], writes=[dstb])
            for n in range(2):
                ns = slice(n * 512, (n + 1) * 512)
                pa, pab = cx.psY[0], cx.psYb[0]
                pb_, pbb = cx.psY[1], cx.psYb[1]
                for c in range(8):
                    S.op("pe", lambda e, c=c: e.matmul(pa[:], oaT[i2][:, c, :], WA[:, c, ns],
                                                       start=(c == 0), stop=(c == 7)),
                         reads=[oaTb[i2], Wb], writes=[pab], inc=(c == 7))
                for c in range(4):
                    S.op("pe", lambda e, c=c: e.matmul(pb_[:], obT[i2][:, c, :], WB[:, c, ns],
                                                       start=(c == 0), stop=(c == 3)),
                         reads=[obTb[i2], Wb], writes=[pbb], inc=(c == 3))
                S.op("dve", lambda e: e.tensor_tensor(out=m1[n][:], in0=sga[i2][:, ns], in1=pa[:], op=ALU.mult),
                     reads=[sgab[i2], pab], writes=[m1b[n]])
                S.op("dve", lambda e: e.tensor_tensor(out=m2[n][:], in0=sgb[i2][:, ns], in1=pb_[:], op=ALU.mult),
                     reads=[sgbb[i2], pbb], writes=[m2b[n]])
                S.op("pool", lambda e: e.tensor_tensor(out=mg[i2][:, ns], in0=m1[n][:], in1=m2[n][:], op=ALU.add),
                     reads=[m1b[n], m2b[n]], writes=[mgb[i2]])
            pT, pTb = cx.psT[ti % 2], cx.psTb[ti % 2]
            for c in range(8):
                S.op("pe", lambda e, c=c: e.transpose(pT[:, c * 128:(c + 1) * 128],
                                                      mg[i2][:, c * 128:(c + 1) * 128], cx.cbf[:, 0:128]),
                     reads=[mgb[i2], cx.constb], writes=[pTb], inc=(c == 7))
            S.op("act", lambda e: e.copy(out=mT[i2][:], in_=pT[:].rearrange("p (k t) -> p k t", k=8)),
                 reads=[pTb], writes=[mTb[i2]])
            for n in range(2):
                ns = slice(n * 512, (n + 1) * 512)
                pt, ptb = cx.psA[n], cx.psAb[n]
                for c in range(8):
                    S.op("pe", lambda e, c=c: e.matmul(pt[:], mT[i2][:, c, :], WO[:, c, ns],
                                                       start=(c == 0), stop=(c == 7)),
                         reads=[mTb[i2], Wb], writes=[ptb], inc=(c == 7))
                S.op("dve", lambda e: e.tensor_tensor(out=xt[:, ns], in0=xt[:, ns], in1=pt[:], op=ALU.add),
                     reads=[ptb, xb], writes=[xb])
            S.dma("sp", xout[t0:t0 + 128, :], xt[:], reads=[xb])
        cx.stack = old
        barrier(cx, S)


def make_ctx(nc, stack):
    cx = Ctx()
    cx.nc = nc
    cx.stack = stack
    cx.rr = 0
    S = Sched(nc, stack)
    cx.cbf = sb(cx, "cbf", [128, 640], BF16)
    cx.rmask = sb(cx, "rmask", [128, 128], F32)
    cx.cmask = sb(cx, "cmask", [128, 4], F32)
    cx.bmask4 = sb(cx, "bmask4", [128, 512], BF16)
    cx.rmask4 = sb(cx, "rmask4", [128, 512], F32)
    cx.ident = cx.cbf[:, 0:128]
    cx.constb = Buf()
    cx.junk = [sb(cx, "junk%d" % i, [128, D], BF16) for i in range(2)]
    cx.junkb = [Buf() for _ in range(2)]
    cx.ss = [sb(cx, "ss%d" % i, [128, 4], F32) for i in range(2)]
    cx.ssb = [Buf() for _ in range(2)]
    cx.hbf = [sb(cx, "hbf%d" % i, [128, D], BF16) for i in range(2)]
    cx.hbfb = [Buf() for _ in range(2)]
    cx.psT = [ps(cx, "psT%d" % i, [128, 1024], BF16) for i in range(2)]
    cx.psTb = [Buf() for _ in range(2)]
    cx.psA = [ps(cx, "psA%d" % i, [128, 512], F32) for i in range(2)]
    cx.psAb = [Buf() for _ in range(2)]
    cx.psB = [ps(cx, "psB%d" % i, [128, 512], F32) for i in range(2)]
    cx.psBb = [Buf() for _ in range(2)]
    cx.psY = [ps(cx, "psY%d" % i, [128, 512], F32) for i in range(2)]
    cx.psYb = [Buf() for _ in range(2)]
    return cx, S


def build_nc(T, depth=2, phases=None, debug=False):
    nc = bass.Bass("TRN2", target_bir_lowering=False)
    dt = nc.dram_tensor
    x = dt("x", [T, D], F32, kind="ExternalInput").ap()
    cst = dt("cst", [128, 1796], F32, kind="ExternalInput").ap()
    cos4 = dt("cos4", [T, 512], F32, kind="ExternalInput").ap()
    sin4 = dt("sin4", [T, 512], F32, kind="ExternalInput").ap()
    f1n = dt("ffn1_norm", [depth, D], F32, kind="ExternalInput").ap()
    f1i = dt("ffn1_w_in", [depth, D, 2 * DFF], F32, kind="ExternalInput").ap()
    f1o = dt("ffn1_w_out", [depth, DFF, D], F32, kind="ExternalInput").ap()
    mn = dt("mix_norm", [depth, D], F32, kind="ExternalInput").ap()
    win = dt("w_in", [depth, D, P_IN], F32, kind="ExternalInput").ap()
    lgT = dt("lgT", [128, depth, 8], F32, kind="ExternalInput").ap()
    gnT = dt("gnT", [depth, 128, 8], F32, kind="ExternalInput").ap()
    aqn = dt("attn_q_norm", [depth, 3, 128], F32, kind="ExternalInput").ap()
    akn = dt("attn_k_norm", [depth, 3, 128], F32, kind="ExternalInput").ap()
    wa = dt("w_branch_a", [depth, D, D], F32, kind="ExternalInput").ap()
    wb = dt("w_branch_b", [depth, 512, D], F32, kind="ExternalInput").ap()
    wo = dt("w_out", [depth, D, D], F32, kind="ExternalInput").ap()
    f2n = dt("ffn2_norm", [depth, D], F32, kind="ExternalInput").ap()
    f2i = dt("ffn2_w_in", [depth, D, 2 * DFF], F32, kind="ExternalInput").ap()
    f2o = dt("ffn2_w_out", [depth, DFF, D], F32, kind="ExternalInput").ap()
    y = dt("y", [T, D], F32, kind="ExternalOutput").ap()
    kind = "ExternalOutput" if debug else "Internal"
    xa = dt("xa", [T, D], F32, kind=kind).ap()
    xb_ = dt("xb", [T, D], F32, kind=kind).ap()
    oat = dt("oat", [D, T], BF16, kind=kind).ap()
    obt = dt("obt", [512, T], BF16, kind=kind).ap()
    qkv = dt("qkv", [T, 3, 1536], BF16, kind=kind).ap()
    with contextlib.ExitStack() as stack:
        cx, S = make_ctx(nc, stack)
        load_consts(cx, S, cst)
        cur = x
        for l in range(depth):
            last = (l == depth - 1)
            tg = "L%d" % l
            P = phases
            if P is None or "f1" in P:
                phase_ffn(cx, S, cur, xa, f1n[l:l + 1, :], f1i[l], f1o[l], T, tg + "f1")
            if P is None or "hg" in P:
                phase_hgrn(cx, S, xa, oat, mn[l:l + 1, :], win[l], lgT, gnT[l], T, l, tg + "hg")
            if P is None or "ap" in P:
                phase_attn_proj(cx, S, xa, qkv, mn[l:l + 1, :], win[l], aqn[l], akn[l], cos4, sin4, T, tg + "ap")
            if P is None or "ac" in P:
                phase_attn_core(cx, S, qkv, obt, T, tg + "ac")
            if P is None or "po" in P:
                phase_out(cx, S, xa, xb_, oat, obt, mn[l:l + 1, :], win[l], wa[l], wb[l], wo[l], T, tg + "po")
            if P is None or "f2" in P:
                phase_ffn(cx, S, xb_, y if last else xa, f2n[l:l + 1, :], f2i[l], f2o[l], T, tg + "f2")
            cur = xa
        barrier(cx, S)
        cx.ninst = S.ninst
        print("ninst", S.ninst, {k: v for k, v in S.cnt.items()})
    return nc


def rope_host(T):
    pos = np.arange(T, dtype=np.float32)
    inv = (np.float32(10000.0) ** (-np.arange(0, 128, 2, dtype=np.float32) / np.float32(128))).astype(np.float32)
    ang = pos[:, None] * inv[None, :]
    ang = np.concatenate([ang, ang], axis=-1).astype(np.float32)
    cos = np.cos(ang).astype(np.float32)
    sin = np.sin(ang).astype(np.float32)
    sin[:, :64] = -sin[:, :64]
    return np.tile(cos, (1, 4)), np.tile(sin, (1, 4))


def make_in_maps(inputs, T, depth=2):
    cos4, sin4 = rope_host(T)
    base = {k: np.ascontiguousarray(v, dtype=np.float32) for k, v in inputs.items() if k not in ("x", "hgrn_lb_logits", "hgrn_out_norm")}
    base["cst"] = host_consts()
    base["cos4"] = np.ascontiguousarray(cos4)
    base["sin4"] = np.ascontiguousarray(sin4)
    lg = np.asarray(inputs["hgrn_lb_logits"], np.float32)
    base["lgT"] = np.ascontiguousarray(lg.reshape(depth, 8, 128).transpose(2, 0, 1))
    gn = np.asarray(inputs["hgrn_out_norm"], np.float32)
    base["gnT"] = np.ascontiguousarray(gn.reshape(depth, 8, 128).transpose(0, 2, 1))
    x = np.asarray(inputs["x"], np.float32)
    maps = []
    for c in range(NCORES):
        m = dict(base)
        m["x"] = np.ascontiguousarray(x[c % x.shape[0], :T])
        maps.append(m)
    return maps


def kernel(**inputs):
    x = np.asarray(inputs["x"])
    B, T, _ = x.shape
    nc = build_nc(T, depth=2)
    maps = make_in_maps(inputs, T, depth=2)
    res = run_bass_kernel_spmd(nc, maps, core_ids=list(range(NCORES)))
    out = np.stack([np.asarray(res.results[b]["y"], dtype=np.float32) for b in range(B)], axis=0)
    return out


def build_ac_only(T):
    nc = bass.Bass("TRN2", target_bir_lowering=False)
    dt = nc.dram_tensor
    cst = dt("cst", [128, 1796], F32, kind="ExternalInput").ap()
    qkv = dt("qkv", [T, 3, 1536], BF16, kind="ExternalInput").ap()
    obt = dt("obt", [512, T], BF16, kind="ExternalOutput").ap()
    with contextlib.ExitStack() as stack:
        cx, S = make_ctx(nc, stack)
        load_consts(cx, S, cst)
        phase_attn_core(cx, S, qkv, obt, T, "ac")
        barrier(cx, S)
    return nc
```

```python
import contextlib
import numpy as np
import concourse.bass as bass
import concourse.mybir as mybir
from concourse.bass_utils import run_bass_kernel_spmd

F32 = mybir.dt.float32
BF16 = mybir.dt.bfloat16
AF = mybir.ActivationFunctionType
ALU = mybir.AluOpType
AX = mybir.AxisListType

D = 1024
DFF = 2816
P_IN = 10752
EPS = 1e-6
NCORES = 8


class Buf:
    __slots__ = ("w", "r", "name")

    def __init__(self, name=""):
        self.w = None
        self.r = {}
        self.name = name


class Sched:
    def __init__(self, nc, stack, ndma=6):
        self.nc = nc
        self.h = {"pe": nc.tensor, "act": nc.scalar, "dve": nc.vector,
                  "pool": nc.gpsimd, "sp": nc.sync}
        self.sems = {}
        self.cnt = {}
        for e in ("pe", "act", "dve", "pool"):
            self.sems[e] = stack.enter_context(nc.semaphore("s_" + e))
            self.cnt[e] = 0
        self.seen = {e: {} for e in self.h}
        self.dq = {}
        for q in ("sp", "pool", "act"):
            names = ["d_%s%d" % (q, i) for i in range(ndma)]
            for n in names:
                self.sems[n] = stack.enter_context(nc.semaphore(n))
                self.cnt[n] = 0
            self.dq[q] = [names, 0]
        self.ninst = 0

    def _wait(self, e, deps):
        need = {}
        for d in deps:
            if d is None:
                continue
            k, v, src = d
            if src == e and e == "pe":
                continue
            if self.seen[e].get(k, 0) >= v:
                continue
            if need.get(k, 0) < v:
                need[k] = v
        for k, v in need.items():
            self.h[e].wait_ge(self.sems[k], v)
            self.seen[e][k] = v
            self.ninst += 1

    def _deps(self, e, reads, writes):
        deps = []
        for b in reads:
            deps.append(b.w)
        for b in writes:
            deps.append(b.w)
            for r in b.r.values():
                if r[2] == e and e != "pool":
                    continue
                deps.append(r)
        return deps

    def _record(self, ev, reads, writes):
        for b in reads:
            old = b.r.get(ev[0])
            if old is None or old[1] < ev[1]:
                b.r[ev[0]] = ev
        for b in writes:
            b.w = ev
            b.r = {}

    def op(self, e, fn, reads=(), writes=(), inc=True):
        self._wait(e, self._deps(e, reads, writes))
        ins = fn(self.h[e])
        if inc:
            self.cnt[e] += 1
            ins.then_inc(self.sems[e], 1)
            ev = (e, self.cnt[e], e)
        else:
            ev = (e, self.cnt[e] + 1, e)
        self.ninst += 1
        self._record(ev, reads, writes)
        return ev

    def dma(self, q, out, in_, reads=(), writes=(), **kw):
        names, idx = self.dq[q]
        k = names[idx % len(names)]
        self.dq[q][1] = idx + 1
        deps = self._deps("dma_issue", reads, writes)
        if self.cnt[k] > 0:
            deps.append((k, self.cnt[k], "dma"))
        self._wait(q, deps)
        ins = self.h[q].dma_start(out=out, in_=in_, **kw)
        self.cnt[k] += 16
        ins.then_inc(self.sems[k], 16)
        ev = (k, self.cnt[k], "dma")
        self.ninst += 1
        self._record(ev, reads, writes)
        return ev

    def finish(self, e="sp"):
        for k, c in self.cnt.items():
            if c > 0 and self.seen[e].get(k, 0) < c:
                self.h[e].wait_ge(self.sems[k], c)
                self.seen[e][k] = c


class Ctx:
    pass


def sb(cx, name, shape, dt):
    t = cx.stack.enter_context(cx.nc.sbuf_tensor("sb_" + name, list(shape), dt))
    return t


def ps(cx, name, shape, dt):
    t = cx.stack.enter_context(cx.nc.psum_tensor("ps_" + name, list(shape), dt))
    return t


def emit_norm_T(cx, S, xt, xb, gain, hT, hTb, col0, tag, fixed=None):
    nc = cx.nc
    i = cx.rr % 2 if fixed is None else fixed
    cx.rr += 1
    junk, junkb = cx.junk[i], cx.junkb[i]
    ss, ssb = cx.ss[i], cx.ssb[i]
    hb, hbb = cx.hbf[i], cx.hbfb[i]
    pT, pTb = cx.psT[i], cx.psTb[i]
    S.op("act", lambda e: e.activation(out=junk[:], in_=xt[:], func=AF.Square,
                                       accum_out=ss[:, 0:1]),
         reads=[xb], writes=[junkb, ssb])
    S.op("act", lambda e: e.activation(out=ss[:, 1:2], in_=ss[:, 0:1], func=AF.Sqrt,
                                       scale=1.0 / D, bias=EPS),
         reads=[ssb], writes=[ssb])
    S.op("dve", lambda e: e.reciprocal(out=ss[:, 2:3], in_=ss[:, 1:2]),
         reads=[ssb], writes=[ssb])
    S.op("dve", lambda e: e.scalar_tensor_tensor(out=hb[:], in0=xt[:], scalar=ss[:, 2:3],
                                                 in1=gain[:], op0=ALU.mult, op1=ALU.mult),
         reads=[xb, ssb, cx.constb], writes=[hbb])
    for k in range(8):
        S.op("pe", lambda e, k=k: e.transpose(pT[:, k * 128:(k + 1) * 128],
                                              hb[:, k * 128:(k + 1) * 128], cx.ident[:]),
             reads=[hbb, cx.constb], writes=[pTb], inc=(k == 7))
    S.op("act", lambda e: e.copy(out=hT[:, :, col0:col0 + 128],
                                 in_=pT[:].rearrange("p (k t) -> p k t", k=8)),
         reads=[pTb], writes=[hTb])


def load_weight_bf16(cx, S, dst, dstb, src_ap, nk, ncols, c0=0, c1=None, cstep=1408):
    c1 = ncols if c1 is None else c1
    v = src_ap.rearrange("(k p) c -> p k c", p=128)
    c = c0
    while c < c1:
        ce = min(c1, c + cstep)
        for k in range(nk):
            S.dma("pool", dst[:, k, c - c0:ce - c0], v[:, k, c:ce], writes=[dstb])
        c = ce


def phase_ffn(cx, S, xin, xout, g_ap, w_in_ap, w_out_ap, T, tag):
    nc = cx.nc
    G = 256
    with contextlib.ExitStack() as st:
        old = cx.stack
        cx.stack = st
        w1 = sb(cx, tag + "w1", [128, 8, 2 * DFF], BF16)
        w2 = sb(cx, tag + "w2", [128, 22, D], BF16)
        gain = sb(cx, tag + "g", [128, D], F32)
        xts = [sb(cx, tag + "xt%d" % i, [128, D], F32) for i in range(4)]
        hT = [sb(cx, tag + "hT%d" % i, [128, 8, G], BF16) for i in range(2)]
        uT = sb(cx, tag + "uT", [128, 22, G], BF16)
        sa = [sb(cx, tag + "sa%d" % i, [128, G], F32) for i in range(2)]
        w1b, w2b, uTb = Buf(), Buf(), Buf()
        xtb = [Buf() for _ in range(4)]
        hTb = [Buf() for _ in range(2)]
        sab = [Buf() for _ in range(2)]
        S.dma("sp", gain[:], g_ap.partition_broadcast(128), writes=[cx.constb])
        load_weight_bf16(cx, S, w1, w1b, w_in_ap, 8, 2 * DFF)
        load_weight_bf16(cx, S, w2, w2b, w_out_ap, 22, D, cstep=1024)
        ng = T // G
        for gi in range(ng):
            hTg, hTgb = hT[gi % 2], hTb[gi % 2]
            for i in range(2):
                xt, xb = xts[(gi % 2) * 2 + i], xtb[(gi % 2) * 2 + i]
                r0 = gi * G + i * 128
                S.dma("sp", xt[:], xin[r0:r0 + 128, :], writes=[xb])
                emit_norm_T(cx, S, xt, xb, gain, hTg, hTgb, i * 128, tag)
            for j in range(22):
                pA, pAb = cx.psA[j % 2], cx.psAb[j % 2]
                pB, pBb = cx.psB[j % 2], cx.psBb[j % 2]
                for k in range(8):
                    S.op("pe", lambda e, k=k: e.matmul(pA[:, 0:G], w1[:, k, j * 128:(j + 1) * 128],
                                                       hTg[:, k, :], start=(k == 0), stop=(k == 7)),
                         reads=[w1b, hTgb], writes=[pAb], inc=(k == 7))
                for k in range(8):
                    S.op("pe", lambda e, k=k: e.matmul(pB[:, 0:G],
                                                       w1[:, k, DFF + j * 128:DFF + (j + 1) * 128],
                                                       hTg[:, k, :], start=(k == 0), stop=(k == 7)),
                         reads=[w1b, hTgb], writes=[pBb], inc=(k == 7))
                s_, s_b = sa[j % 2], sab[j % 2]
                S.op("act", lambda e: e.activation(out=s_[:], in_=pA[:, 0:G], func=AF.Silu),
                     reads=[pAb], writes=[s_b])
                S.op("dve", lambda e: e.tensor_tensor(out=uT[:, j, :], in0=s_[:], in1=pB[:, 0:G],
                                                      op=ALU.mult),
                     reads=[s_b, pBb], writes=[uTb])
            for i in range(2):
                xt, xb = xts[(gi % 2) * 2 + i], xtb[(gi % 2) * 2 + i]
                r0 = gi * G + i * 128
                for n in range(2):
                    pY, pYb = cx.psY[n], cx.psYb[n]
                    for j in range(22):
                        S.op("pe", lambda e, j=j: e.matmul(pY[:], uT[:, j, i * 128:(i + 1) * 128],
                                                           w2[:, j, n * 512:(n + 1) * 512],
                                                           start=(j == 0), stop=(j == 21)),
                             reads=[uTb, w2b], writes=[pYb], inc=(j == 21))
                    S.op("dve", lambda e: e.scalar_tensor_tensor(
                        out=xt[:, n * 512:(n + 1) * 512], in0=pY[:], scalar=0.5,
                        in1=xt[:, n * 512:(n + 1) * 512], op0=ALU.mult, op1=ALU.add),
                        reads=[pYb, xb], writes=[xb])
                S.dma("sp", xout[r0:r0 + 128, :], xt[:], reads=[xb])
        cx.stack = old
        S.finish("sp")
        barrier(cx, S)


def barrier(cx, S):
    for e in ("pe", "act", "dve", "pool", "sp"):
        S.finish(e)


def make_ctx(nc, stack):
    cx = Ctx()
    cx.nc = nc
    cx.stack = stack
    cx.rr = 0
    S = Sched(nc, stack)
    cx.ident = sb(cx, "ident", [128, 128], BF16)
    cx.constb = Buf()
    cx.junk = [sb(cx, "junk%d" % i, [128, D], BF16) for i in range(2)]
    cx.junkb = [Buf() for _ in range(2)]
    cx.ss = [sb(cx, "ss%d" % i, [128, 4], F32) for i in range(2)]
    cx.ssb = [Buf() for _ in range(2)]
    cx.hbf = [sb(cx, "hbf%d" % i, [128, D], BF16) for i in range(2)]
    cx.hbfb = [Buf() for _ in range(2)]
    cx.psT = [ps(cx, "psT%d" % i, [128, 1024], BF16) for i in range(2)]
    cx.psTb = [Buf() for _ in range(2)]
    cx.psA = [ps(cx, "psA%d" % i, [128, 512], F32) for i in range(2)]
    cx.psAb = [Buf() for _ in range(2)]
    cx.psB = [ps(cx, "psB%d" % i, [128, 512], F32) for i in range(2)]
    cx.psBb = [Buf() for _ in range(2)]
    cx.psY = [ps(cx, "psY%d" % i, [128, 512], F32) for i in range(2)]
    cx.psYb = [Buf() for _ in range(2)]
    return cx, S


def load_consts(cx, S, ident_ap):
    S.dma("pool", cx.ident[:], ident_ap, writes=[cx.constb])


def load_consts(cx, S, cst_ap):
    S.dma("pool", cx.cbf[:], cst_ap[:, 0:640], writes=[cx.constb])
    S.dma("sp", cx.rmask[:], cst_ap[:, 640:768], writes=[cx.constb])
    S.dma("sp", cx.cmask[:], cst_ap[:, 768:772], writes=[cx.constb])
    S.dma("pool", cx.bmask4[:], cst_ap[:, 772:1284], writes=[cx.constb])
    S.dma("sp", cx.rmask4[:], cst_ap[:, 1284:1796], writes=[cx.constb])


def host_consts():
    c = np.zeros((128, 1796), np.float32)
    i = np.arange(128)
    c[:, 0:128] = np.eye(128)
    c[:, 128:256] = 1.0
    s_, t_ = i[:, None], i[None, :]
    c[:, 256:384] = ((s_ // 32 == t_ // 32) & (s_ <= t_))
    c[:, 384:512] = (s_ >= t_)
    c[:, 512:640] = (s_ <= t_)
    c[:, 640:768] = (t_ % 32 != 0) * np.ones((128, 1))
    for j in range(4):
        c[:, 768 + j] = (i // 32 == j)
        c[:, 772 + j * 128:772 + (j + 1) * 128] = c[:, 256:384]
        c[:, 1284 + j * 128:1284 + (j + 1) * 128] = c[:, 640:768]
    return c


def phase_hgrn(cx, S, xin, oat, g_ap, w_in_ap, lgT_ap, gnT_ap, T, layer, tag):
    nc = cx.nc
    ident = cx.cbf[:, 0:128]
    ones = cx.cbf[:, 128:256]
    with contextlib.ExitStack() as st:
        old = cx.stack
        cx.stack = st
        W = sb(cx, tag + "W", [128, 8, 4096], BF16)
        Wb = Buf()
        gain = sb(cx, tag + "g", [128, D], F32)
        lg = sb(cx, tag + "lg", [128, lgT_ap.shape[1], 8], F32)
        lb = sb(cx, tag + "lb", [128, 8], F32)
        oml = sb(cx, tag + "oml", [128, 8], F32)
        gn = sb(cx, tag + "gn", [128, 8], F32)
        zer = sb(cx, tag + "zer", [128, 128], F32)
        lbf = sb(cx, tag + "lbf", [128, 8, 128], F32)
        omlf = sb(cx, tag + "omlf", [128, 8, 128], F32)
        gnf = sb(cx, tag + "gnf", [128, 8, 128], F32)
        parb = Buf()
        xts = [sb(cx, tag + "xt%d" % i, [128, D], F32) for i in range(2)]
        xtb = [Buf() for _ in range(2)]
        hT = [sb(cx, tag + "hT%d" % i, [128, 8, 128], BF16) for i in range(2)]
        hTb = [Buf() for _ in range(2)]
        v = [sb(cx, tag + "v%d" % i, [128, D], BF16) for i in range(2)]
        vb = [Buf() for _ in range(2)]
        S32 = [sb(cx, tag + "S32_%d" % i, [128, 512], F32) for i in range(2)]
        S32b = [Buf() for _ in range(2)]
        Sbf = [sb(cx, tag + "Sbf_%d" % i, [128, 8, 512], BF16) for i in range(2)]
        Sbfb = [[Buf() for _ in range(8)] for _ in range(2)]
        NB = 2

        def mk(name, dt, w=512, n=NB):
            return ([sb(cx, tag + name + str(i), [128, w], dt) for i in range(n)],
                    [Buf() for _ in range(n)])
        q32, q32b = mk("q", F32)
        f32t, f32b = mk("f", F32)
        gt, gtb = mk("gt", F32, n=4)
        lf, lfb = mk("lf", F32)
        kk, kkb = mk("kk", F32)
        gc, gcb = mk("gc", F32)
        eq, eqb = mk("eq", F32, n=4)
        ek, ekb = mk("ek", F32)
        qg, qgb = mk("qg", BF16, n=4)
        kg, kgb = mk("kg", BF16)
        kgT, kgTb = mk("kgT", BF16, w=2048)
        AT, ATb = mk("AT", BF16, n=4)
        sq, sqb = mk("sq", BF16)
        rs, rsb = mk("rs", F32)
        on, onb = mk("on", F32)
        tmp, tmpb = mk("tmp", F32)
        oa, oab = mk("oa", BF16, n=4)

        S.dma("sp", gain[:], g_ap.partition_broadcast(128), writes=[cx.constb])
        S.dma("sp", lg[:], lgT_ap, writes=[parb])
        S.dma("sp", gn[:], gnT_ap, writes=[parb])
        if layer == 0:
            S.op("dve", lambda e: e.memset(lb[:], 0.0), writes=[parb])
        else:
            S.op("dve", lambda e: e.tensor_tensor(out=lb[:], in0=lg[:, 1, :], in1=lg[:, 0, :],
                                                  op=ALU.subtract), reads=[parb], writes=[parb])
            S.op("act", lambda e: e.activation(out=lb[:], in_=lb[:], func=AF.Sigmoid),
                 reads=[parb], writes=[parb])
        S.op("dve", lambda e: e.tensor_scalar(out=oml[:], in0=lb[:], scalar1=-1.0, scalar2=1.0,
                                              op0=ALU.mult, op1=ALU.add), reads=[parb], writes=[parb])
        S.op("dve", lambda e: e.memset(zer[:], 0.0), writes=[parb])
        for hd in range(8):
            for (dst, src) in ((lbf, lb), (omlf, oml), (gnf, gn)):
                S.op("dve", lambda e: e.tensor_scalar(out=dst[:, hd, :], in0=zer[:], scalar1=src[:, hd:hd + 1],
                                                      scalar2=None, op0=ALU.add),
                     reads=[parb], writes=[parb])
        for gi in range(2):
            S.op("dve", lambda e: e.memset(S32[gi][:], 0.0), writes=[S32b[gi]])
            S.op("pool", lambda e: e.memset(Sbf[gi][:], 0.0), writes=Sbfb[gi])
        load_weight_bf16(cx, S, W, Wb, w_in_ap, 8, P_IN, c0=0, c1=4096, cstep=2048)

        pQ, pQb = cx.psA[0], cx.psAb[0]
        pF, pFb = cx.psA[1], cx.psAb[1]
        pG, pGb = cx.psB[0], cx.psBb[0]
        pAT, pATb = cx.psB[1], cx.psBb[1]
        pS, pSb = cx.psY[0], cx.psYb[0]
        pO, pOb = cx.psY[1], cx.psYb[1]
        pK, pKb = cx.psT[1], cx.psTb[1]
        oat_v = oat.rearrange("(h p) t -> p h t", p=128)
        nt = T // 128
        for ti in range(nt):
            t0 = ti * 128
            xt, xb = xts[ti % 2], xtb[ti % 2]
            hTt, hTtb = hT[ti % 2], hTb[ti % 2]
            vt, vtb = v[ti % 2], vb[ti % 2]
            S.dma("sp", xt[:], xin[t0:t0 + 128, :], writes=[xb])
            emit_norm_T(cx, S, xt, xb, gain, hTt, hTtb, 0, tag, fixed=0)
            for half in range(2):
                for k in range(8):
                    S.op("pe", lambda e, k=k: e.matmul(pS[:], hTt[:, k, :],
                                                       W[:, k, 2048 + half * 512:2048 + (half + 1) * 512],
                                                       start=(k == 0), stop=(k == 7)),
                         reads=[hTtb, Wb], writes=[pSb], inc=(k == 7))
                S.op("act", lambda e: e.copy(out=vt[:, half * 512:(half + 1) * 512], in_=pS[:]),
                     reads=[pSb], writes=[vtb])
            for gi in range(2):
                b = gi
                b4 = (2 * ti + gi) % 4
                c0 = gi * 512
                hsl = slice(gi * 4, gi * 4 + 4)
                for (pt, ptb, base) in ((pQ, pQb, 0), (pF, pFb, 1024), (pG, pGb, 3072)):
                    for hh in range(4):
                        for k in range(8):
                            S.op("pe", lambda e, k=k: e.matmul(
                                pt[:, hh * 128:(hh + 1) * 128],
                                W[:, k, base + c0 + hh * 128:base + c0 + (hh + 1) * 128],
                                hTt[:, k, :], start=(k == 0), stop=(k == 7)),
                                reads=[hTtb, Wb], writes=[ptb], inc=(k == 7 and hh == 3))
                S.op("act", lambda e: e.activation(out=q32[b][:], in_=pQ[:], func=AF.Silu),
                     reads=[pQb], writes=[q32b[b]])
                S.op("act", lambda e: e.activation(out=gt[b4][:], in_=pG[:], func=AF.Silu),
                     reads=[pGb], writes=[gtb[b4]])
                S.op("act", lambda e: e.activation(out=f32t[b][:], in_=pF[:], func=AF.Sigmoid),
                     reads=[pFb], writes=[f32b[b]])
                S.op("pool", lambda e: e.tensor_tensor(out=gt[b4][:], in0=gt[b4][:],
                                                       in1=gnf[:, hsl, :].rearrange("p h t -> p (h t)"), op=ALU.mult),
                     reads=[gtb[b4], parb], writes=[gtb[b4]])
                if layer != 0:
                    S.op("dve", lambda e: e.tensor_tensor(out=f32t[b][:], in0=f32t[b][:],
                                                          in1=omlf[:, hsl, :].rearrange("p h t -> p (h t)"),
                                                          op=ALU.mult),
                         reads=[f32b[b], parb], writes=[f32b[b]])
                    S.op("dve", lambda e: e.tensor_tensor(out=f32t[b][:], in0=f32t[b][:],
                                                          in1=lbf[:, hsl, :].rearrange("p h t -> p (h t)"),
                                                          op=ALU.add),
                         reads=[f32b[b], parb], writes=[f32b[b]])
                S.op("act", lambda e: e.activation(out=lf[b][:], in_=f32t[b][:], func=AF.Ln),
                     reads=[f32b[b]], writes=[lfb[b]])
                S.op("act", lambda e: e.activation(out=kk[b][:], in_=f32t[b][:], func=AF.Identity,
                                                   scale=-1.0, bias=1.0),
                     reads=[f32b[b]], writes=[kkb[b]])
                S.op("dve", lambda e: e.tensor_tensor_scan(out=gc[b][:], data0=cx.rmask4[:], data1=lf[b][:],
                                                           initial=0.0, op0=ALU.mult, op1=ALU.add),
                     reads=[lfb[b], cx.constb], writes=[gcb[b]])
                S.op("act", lambda e: e.activation(out=eq[b4][:], in_=gc[b][:], func=AF.Exp),
                     reads=[gcb[b]], writes=[eqb[b4]])
                S.op("act", lambda e: e.activation(out=ek[b][:], in_=gc[b][:], func=AF.Exp, scale=-1.0),
                     reads=[gcb[b]], writes=[ekb[b]])
                S.op("dve", lambda e: e.tensor_tensor(out=kg[b][:], in0=kk[b][:], in1=ek[b][:], op=ALU.mult),
                     reads=[kkb[b], ekb[b]], writes=[kgb[b]])
                S.op("dve", lambda e: e.tensor_tensor(out=qg[b4][:], in0=q32[b][:], in1=eq[b4][:], op=ALU.mult),
                     reads=[q32b[b], eqb[b4]], writes=[qgb[b4]])
            for gi in range(2):
                b = gi
                b4 = (2 * ti + gi) % 4
                c0 = gi * 512
                hsl = slice(gi * 4, gi * 4 + 4)
                for hh in range(4):
                    S.op("pe", lambda e: e.transpose(pK[:, hh * 128:(hh + 1) * 128],
                                                     kg[b][:, hh * 128:(hh + 1) * 128], ident),
                         reads=[kgb[b], cx.constb], writes=[pKb], inc=(hh == 3))
                for j in range(4):
                    S.op("act", lambda e: e.activation(out=kgT[b][:, j * 512:(j + 1) * 512], in_=pK[:, 0:512],
                                                       func=AF.Copy, scale=cx.cmask[:, j:j + 1]),
                         reads=[pKb, cx.constb], writes=[kgTb[b]])
                for hh in range(4):
                    hs = slice(hh * 128, (hh + 1) * 128)
                    S.op("pe", lambda e: e.matmul(pAT[:, hs], kg[b][:, hs], qg[b4][:, hs], start=True, stop=True),
                         reads=[kgb[b], qgb[b4]], writes=[pATb], inc=(hh == 3))
                S.op("dve", lambda e: e.tensor_tensor(out=AT[b4][:], in0=pAT[:], in1=cx.bmask4[:], op=ALU.mult),
                     reads=[pATb, cx.constb], writes=[ATb[b4]])
                eqv = eq[b4][:].rearrange("p (h t) -> p h t", h=4)
                for j in range(4):
                    for hh in range(4):
                        hs = slice(hh * 128, (hh + 1) * 128)
                        S.op("pe", lambda e: e.matmul(pS[:, hs], kgT[b][:, j * 512 + hh * 128:j * 512 + (hh + 1) * 128],
                                                      vt[:, c0 + hh * 128:c0 + (hh + 1) * 128],
                                                      start=True, stop=True),
                             reads=[kgTb[b], vtb], writes=[pSb], inc=(hh == 3))
                    eglb = eqv[:, :, 32 * j + 31:32 * j + 32].to_broadcast([128, 4, 128])
                    S.op("dve", lambda e: e.tensor_tensor(out=tmp[b][:], in0=S32[gi][:], in1=pS[:], op=ALU.add),
                         reads=[S32b[gi], pSb], writes=[tmpb[b]])
                    S.op("dve", lambda e: e.tensor_tensor(out=S32[gi][:].rearrange("p (h t) -> p h t", h=4),
                                                          in0=tmp[b][:].rearrange("p (h t) -> p h t", h=4),
                                                          in1=eglb, op=ALU.mult),
                         reads=[tmpb[b], eqb[b4]], writes=[S32b[gi]])
                    sl = ((ti % 2) * 4 + j + 1) % 8
                    S.op("act", lambda e: e.copy(out=Sbf[gi][:, sl, :], in_=S32[gi][:]),
                         reads=[S32b[gi]], writes=[Sbfb[gi][sl]])
            for gi in range(2):
                b = gi
                b4 = (2 * ti + gi) % 4
                c0 = gi * 512
                hsl = slice(gi * 4, gi * 4 + 4)
                for hh in range(4):
                    hs = slice(hh * 128, (hh + 1) * 128)
                    S.op("pe", lambda e: e.matmul(pO[:, hs], vt[:, c0 + hh * 128:c0 + (hh + 1) * 128], AT[b4][:, hs],
                                                  start=True, stop=False),
                         reads=[vtb, ATb[b4]], writes=[pOb], inc=False)
                    for j in range(4):
                        sl = ((ti % 2) * 4 + j) % 8
                        S.op("pe", lambda e: e.matmul(pO[:, hh * 128 + 32 * j:hh * 128 + 32 * j + 32],
                                                      Sbf[gi][:, sl, hs], qg[b4][:, hh * 128 + 32 * j:hh * 128 + 32 * j + 32],
                                                      start=False, stop=(j == 3)),
                             reads=[Sbfb[gi][sl], qgb[b4]], writes=[pOb], inc=(j == 3 and hh == 3))
                S.op("act", lambda e: e.activation(out=sq[b][:], in_=pO[:], func=AF.Square),
                     reads=[pOb], writes=[sqb[b]])
                S.op("pe", lambda e: e.matmul(pAT[:], ones, sq[b][:], start=True, stop=True),
                     reads=[sqb[b], cx.constb], writes=[pATb])
                S.op("act", lambda e: e.activation(out=rs[b][:], in_=pAT[:], func=AF.Sqrt,
                                                   scale=1.0 / 128, bias=EPS),
                     reads=[pATb], writes=[rsb[b]])
                S.op("dve", lambda e: e.reciprocal(out=rs[b][:], in_=rs[b][:]),
                     reads=[rsb[b]], writes=[rsb[b]])
                S.op("dve", lambda e: e.tensor_tensor(out=on[b][:], in0=pO[:], in1=rs[b][:], op=ALU.mult),
                     reads=[pOb, rsb[b]], writes=[onb[b]])
                ob = (2 * ti + gi) % 4
                S.op("dve", lambda e: e.tensor_tensor(out=oa[ob][:], in0=on[b][:], in1=gt[b4][:], op=ALU.mult),
                     reads=[onb[b], gtb[b4]], writes=[oab[ob]])
                S.dma("sp", oat_v[:, gi * 4:gi * 4 + 4, t0:t0 + 128],
                      oa[ob][:].rearrange("p (h t) -> p h t", h=4), reads=[oab[ob]])
        cx.stack = old
        barrier(cx, S)


def phase_attn_proj(cx, S, xin, qkv, g_ap, w_in_ap, qn_ap, kn_ap, cos_ap, sin_ap, T, tag):
    nc = cx.nc
    with contextlib.ExitStack() as st:
        old = cx.stack
        cx.stack = st
        W = sb(cx, tag + "W", [128, 8, 4608], BF16)
        Wb = Buf()
        gain = sb(cx, tag + "g", [128, D], F32)
        gq = sb(cx, tag + "gq", [128, 2, 3, 128], F32)
        parb = Buf()
        xts = [sb(cx, tag + "xt%d" % i, [128, D], F32) for i in range(2)]
        xtb = [Buf() for _ in range(2)]
        hT = [sb(cx, tag + "hT%d" % i, [128, 8, 128], BF16) for i in range(2)]
        hTb = [Buf() for _ in range(2)]
        cs = [sb(cx, tag + "cs%d" % i, [128, 2, 512], F32) for i in range(2)]
        csb = [Buf() for _ in range(2)]
        ob = [sb(cx, tag + "ob%d" % i, [128, 3, 1536], BF16) for i in range(2)]
        obb = [Buf() for _ in range(2)]
        NB = 2
        ssq = [sb(cx, tag + "ssq%d" % i, [128, 8], F32) for i in range(NB)]
        ssqb = [Buf() for _ in range(NB)]
        jk = [sb(cx, tag + "jk%d" % i, [128, 128], BF16) for i in range(NB)]
        jkb = [Buf() for _ in range(NB)]
        qn = [sb(cx, tag + "qn%d" % i, [128, 512], F32) for i in range(NB)]
        qnb = [Buf() for _ in range(NB)]
        t1 = [sb(cx, tag + "t1%d" % i, [128, 512], F32) for i in range(NB)]
        t1b = [Buf() for _ in range(NB)]
        t2 = [sb(cx, tag + "t2%d" % i, [128, 512], F32) for i in range(NB)]
        t2b = [Buf() for _ in range(NB)]
        S.dma("sp", gain[:], g_ap.partition_broadcast(128), writes=[cx.constb])
        for g in range(3):
            S.dma("sp", gq[:, 0, g, :], qn_ap[g:g + 1, :].partition_broadcast(128), writes=[parb])
            S.dma("sp", gq[:, 1, g, :], kn_ap[g:g + 1, :].partition_broadcast(128), writes=[parb])
        load_weight_bf16(cx, S, W, Wb, w_in_ap, 8, P_IN, c0=4096, c1=8704, cstep=1536)
        pss = [(cx.psA[0], cx.psAb[0]), (cx.psA[1], cx.psAb[1]), (cx.psB[0], cx.psBb[0]), (cx.psB[1], cx.psBb[1])]
        nt = T // 128
        cnt = 0
        for ti in range(nt):
            t0 = ti * 128
            xt, xb = xts[ti % 2], xtb[ti % 2]
            hTt, hTtb = hT[ti % 2], hTb[ti % 2]
            c_, c_b = cs[ti % 2], csb[ti % 2]
            o_, o_b = ob[ti % 2], obb[ti % 2]
            S.dma("sp", xt[:], xin[t0:t0 + 128, :], writes=[xb])
            S.dma("sp", c_[:, 0, :], cos_ap[t0:t0 + 128, :], writes=[c_b])
            S.dma("sp", c_[:, 1, :], sin_ap[t0:t0 + 128, :], writes=[c_b])
            emit_norm_T(cx, S, xt, xb, gain, hTt, hTtb, 0, tag)
            def unit_A(which, g, pt, ptb, b):
                cb = which * 1536 + g * 512
                for k in range(8):
                    S.op("pe", lambda e, k=k: e.matmul(pt[:], hTt[:, k, :], W[:, k, cb:cb + 512],
                                                       start=(k == 0), stop=(k == 7)),
                         reads=[hTtb, Wb], writes=[ptb], inc=(k == 7))
                if which == 2:
                    S.op("act", lambda e: e.copy(out=o_[:, 2, g * 512:(g + 1) * 512], in_=pt[:]),
                         reads=[ptb], writes=[o_b])
                    return
                for hh in range(4):
                    S.op("act", lambda e: e.activation(out=jk[b][:], in_=pt[:, hh * 128:(hh + 1) * 128],
                                                       func=AF.Square, accum_out=ssq[b][:, hh:hh + 1]),
                         reads=[ptb], writes=[jkb[b], ssqb[b]])
                S.op("act", lambda e: e.activation(out=ssq[b][:, 4:8], in_=ssq[b][:, 0:4], func=AF.Sqrt,
                                                   scale=1.0 / 128, bias=EPS),
                     reads=[ssqb[b]], writes=[ssqb[b]])
                S.op("dve", lambda e: e.reciprocal(out=ssq[b][:, 4:8], in_=ssq[b][:, 4:8]),
                     reads=[ssqb[b]], writes=[ssqb[b]])
                for hh in range(4):
                    S.op("dve", lambda e: e.scalar_tensor_tensor(
                        out=qn[b][:, hh * 128:(hh + 1) * 128], in0=pt[:, hh * 128:(hh + 1) * 128],
                        scalar=ssq[b][:, 4 + hh:5 + hh], in1=gq[:, which, g, :], op0=ALU.mult, op1=ALU.mult),
                        reads=[ptb, ssqb[b], parb], writes=[qnb[b]])
                S.op("dve", lambda e: e.tensor_tensor(out=t1[b][:], in0=qn[b][:], in1=c_[:, 0, :], op=ALU.mult),
                     reads=[qnb[b], c_b], writes=[t1b[b]])

            def unit_B(which, g, b):
                qv = qn[b][:].rearrange("p (h two d) -> p h two d", h=4, two=2)
                sv = c_[:, 1, :].rearrange("p (h two d) -> p h two d", h=4, two=2)
                tv = t2[b][:].rearrange("p (h two d) -> p h two d", h=4, two=2)
                S.op("pool", lambda e: e.tensor_tensor(out=tv[:, :, 0, :], in0=qv[:, :, 1, :], in1=sv[:, :, 0, :],
                                                       op=ALU.mult),
                     reads=[qnb[b], c_b], writes=[t2b[b]])
                S.op("pool", lambda e: e.tensor_tensor(out=tv[:, :, 1, :], in0=qv[:, :, 0, :], in1=sv[:, :, 1, :],
                                                       op=ALU.mult),
                     reads=[qnb[b], c_b], writes=[t2b[b]])

            def unit_C(which, g, b):
                S.op("dve", lambda e: e.tensor_tensor(out=o_[:, which, g * 512:(g + 1) * 512], in0=t1[b][:],
                                                      in1=t2[b][:], op=ALU.add),
                     reads=[t1b[b], t2b[b]], writes=[o_b])

            units = [(w_, g_) for w_ in range(2) for g_ in range(3)]
            slots = []
            for (w_, g_) in units:
                slots.append((pss[cnt % 4], cnt % NB))
                cnt += 1
            unit_A(units[0][0], units[0][1], slots[0][0][0], slots[0][0][1], slots[0][1])
            for u, (w_, g_) in enumerate(units):
                unit_B(w_, g_, slots[u][1])
                if u + 1 < len(units):
                    nw, ng = units[u + 1]
                    unit_A(nw, ng, slots[u + 1][0][0], slots[u + 1][0][1], slots[u + 1][1])
                else:
                    pt, ptb = pss[cnt % 4]
                    cnt += 1
                    unit_A(2, 0, pt, ptb, 0)
                unit_C(w_, g_, slots[u][1])
            for g_ in (1, 2):
                pt, ptb = pss[cnt % 4]
                cnt += 1
                unit_A(2, g_, pt, ptb, 0)
            S.dma("sp", qkv[t0:t0 + 128, :, :], o_[:], reads=[o_b])
        cx.stack = old
        barrier(cx, S)


ATT_PAT = ((128, 1), (512, 4), (2048, 16))
AC_STAGE = 4


def phase_attn_core(cx, S, qkv, obt, T, tag):
    nc = cx.nc
    ident = cx.cbf[:, 0:128]
    ones = cx.cbf[:, 128:256]
    amask = cx.cbf[:, 384:640]
    SB = 2048
    scale = 128 ** -0.5
    with contextlib.ExitStack() as st:
        old = cx.stack
        cx.stack = st
        num = sb(cx, tag + "num", [128, 4, SB], F32)
        den = sb(cx, tag + "den", [128, 4, SB], F32)
        numb = [Buf() for _ in range(4)]
        denb = [Buf() for _ in range(4)]
        NB = 2
        qt = [sb(cx, tag + "q%d" % i, [128, 512], BF16) for i in range(NB)]
        kp = [sb(cx, tag + "kp%d" % i, [128, 512], BF16) for i in range(NB)]
        ko = [sb(cx, tag + "ko%d" % i, [128, 512], BF16) for i in range(NB)]
        vp = [sb(cx, tag + "vp%d" % i, [128, 512], BF16) for i in range(NB)]
        vo = [sb(cx, tag + "vo%d" % i, [128, 512], BF16) for i in range(NB)]
        qtb, kpb, kob, vpb, vob = ([Buf() for _ in range(NB)] for _ in range(5))
        qT = [sb(cx, tag + "qT%d" % i, [128, 512], BF16) for i in range(NB)]
        kpT = [sb(cx, tag + "kpT%d" % i, [128, 512], BF16) for i in range(NB)]
        koT = [sb(cx, tag + "koT%d" % i, [128, 512], BF16) for i in range(NB)]
        qTb, kpTb, koTb = ([Buf() for _ in range(NB)] for _ in range(3))
        PT = [sb(cx, tag + "PT%d" % i, [128, 256], BF16) for i in range(4)]
        PTb = [Buf() for _ in range(4)]
        obf = [sb(cx, tag + "obf%d" % i, [128, SB], BF16) for i in range(2)]
        obfb = [Buf() for _ in range(2)]
        blk = 0
        hc = 0
        tc = 0
        oc = 0
        for sbi in range(T // SB):
            for g, (_, d) in enumerate(ATT_PAT):
                nper = SB // (128 * d)
                for r in range(d):
                    for nn in range(nper):
                        n = sbi * nper + nn
                        b = blk % NB
                        blk += 1
                        ts = n * 128 * d + r
                        te = ts + 127 * d + 1
                        off = ts - sbi * SB
                        has_prev = n > 0
                        c0, c1 = g * 512, (g + 1) * 512
                        S.dma("sp", qt[b][:], qkv[ts:te:d, 0, c0:c1], writes=[qtb[b]])
                        S.dma("sp", ko[b][:], qkv[ts:te:d, 1, c0:c1], writes=[kob[b]])
                        S.dma("sp", vo[b][:], qkv[ts:te:d, 2, c0:c1], writes=[vob[b]])
                        if has_prev:
                            ps_ = ts - 128 * d
                            S.dma("sp", kp[b][:], qkv[ps_:ps_ + 127 * d + 1:d, 1, c0:c1], writes=[kpb[b]])
                            S.dma("sp", vp[b][:], qkv[ps_:ps_ + 127 * d + 1:d, 2, c0:c1], writes=[vpb[b]])
                        todo = [(qt[b], qtb[b], qT[b], qTb[b]), (ko[b], kob[b], koT[b], koTb[b])]
                        if has_prev:
                            todo.append((kp[b], kpb[b], kpT[b], kpTb[b]))
                        for (src, srcb, dst, dstb) in todo:
                            pT, pTb = cx.psT[tc % 2], cx.psTb[tc % 2]
                            tc += 1
                            for hh in range(4):
                                S.op("pe", lambda e: e.transpose(pT[:, hh * 128:(hh + 1) * 128],
                                                                 src[:, hh * 128:(hh + 1) * 128], ident),
                                     reads=[srcb, cx.constb], writes=[pTb], inc=(hh == 3))
                            S.op("act", lambda e: e.copy(out=dst[:], in_=pT[:, 0:512]),
                                 reads=[pTb], writes=[dstb])
                        for hh in range(4 if AC_STAGE >= 2 else 0):
                            hs = slice(hh * 128, (hh + 1) * 128)
                            psS, psSb = cx.psA[hc % 2], cx.psAb[hc % 2]
                            psN, psNb = cx.psB[hc % 2], cx.psBb[hc % 2]
                            psD, psDb = cx.psY[hc % 2], cx.psYb[hc % 2]
                            P, Pb = PT[hc % 4], PTb[hc % 4]
                            hc += 1
                            lo = 0 if has_prev else 128
                            if has_prev:
                                S.op("pe", lambda e: e.matmul(psS[:, 0:128], kpT[b][:, hs], qT[b][:, hs],
                                                              start=True, stop=True),
                                     reads=[kpTb[b], qTb[b]], writes=[psSb], inc=False)
                            S.op("pe", lambda e: e.matmul(psS[:, 128:256], koT[b][:, hs], qT[b][:, hs],
                                                          start=True, stop=True),
                                 reads=[koTb[b], qTb[b]], writes=[psSb])
                            S.op("act", lambda e: e.activation(out=P[:, lo:256], in_=psS[:, lo:256], func=AF.Exp,
                                                               scale=scale),
                                 reads=[psSb], writes=[Pb])
                            S.op("pool", lambda e: e.tensor_tensor(out=P[:, lo:256], in0=P[:, lo:256],
                                                                   in1=amask[:, lo:256], op=ALU.mult),
                                 reads=[Pb, cx.constb], writes=[Pb])
                            if AC_STAGE < 3:
                                continue
                            if has_prev:
                                S.op("pe", lambda e: e.matmul(psN[:, 0:128], vp[b][:, hs], P[:, 0:128],
                                                              start=True, stop=False),
                                     reads=[vpb[b], Pb], writes=[psNb], inc=False)
                            S.op("pe", lambda e: e.matmul(psN[:, 0:128], vo[b][:, hs], P[:, 128:256],
                                                          start=(not has_prev), stop=True),
                                 reads=[vob[b], Pb], writes=[psNb], inc=False)
                            if has_prev:
                                S.op("pe", lambda e: e.matmul(psD[:, 0:128], ones, P[:, 0:128],
                                                              start=True, stop=False),
                                     reads=[cx.constb, Pb], writes=[psDb], inc=False)
                            S.op("pe", lambda e: e.matmul(psD[:, 0:128], ones, P[:, 128:256],
                                                          start=(not has_prev), stop=True),
                                 reads=[cx.constb, Pb], writes=[psDb])
                            if AC_STAGE < 3.2 or (AC_STAGE in (3.25, 3.27) and g > 0):
                                S.op("pe", lambda e: e.matmul(psN[:, 256:384], ones, P[:, 128:256], start=True, stop=True),
                                     reads=[cx.constb, Pb], writes=[psNb])
                                continue
                            if AC_STAGE < 3.4 and g > 0:
                                S.op("pe", lambda e: e.matmul(psN[:, 256:384], ones, P[:, 128:256], start=True, stop=True),
                                     reads=[cx.constb, Pb], writes=[psNb])
                                continue
                            nv = num[:, hh, off:off + 127 * d + 1:d]
                            dv = den[:, hh, off:off + 127 * d + 1:d]
                            if g == 0:
                                if AC_STAGE != 3.27:
                                    S.op("act", lambda e: e.copy(out=nv, in_=psN[:, 0:128]),
                                         reads=[psNb], writes=[numb[hh]])
                                if AC_STAGE != 3.25:
                                    S.op("act", lambda e: e.copy(out=dv, in_=psD[:, 0:128]),
                                         reads=[psDb], writes=[denb[hh]])
                            else:
                                S.op("dve", lambda e: e.tensor_tensor(out=nv, in0=nv, in1=psN[:, 0:128], op=ALU.add),
                                     reads=[psNb, numb[hh]], writes=[numb[hh]])
                                S.op("dve", lambda e: e.tensor_tensor(out=dv, in0=dv, in1=psD[:, 0:128], op=ALU.add),
                                     reads=[psDb, denb[hh]], writes=[denb[hh]])
            for hh in range(4 if AC_STAGE >= 4 else 0):
                o_, o_b = obf[oc % 2], obfb[oc % 2]
                oc += 1
                S.op("dve", lambda e: e.reciprocal(out=den[:, hh, :], in_=den[:, hh, :]),
                     reads=[denb[hh]], writes=[denb[hh]])
                S.op("pool", lambda e: e.tensor_tensor(out=o_[:], in0=num[:, hh, :], in1=den[:, hh, :], op=ALU.mult),
                     reads=[numb[hh], denb[hh]], writes=[o_b])
                S.dma("sp", obt[hh * 128:(hh + 1) * 128, sbi * SB:(sbi + 1) * SB], o_[:], reads=[o_b])
        cx.stack = old
        barrier(cx, S)


def phase_out(cx, S, xin, xout, oat, obt, g_ap, w_in_ap, wa_ap, wb_ap, wo_ap, T, tag):
    nc = cx.nc
    with contextlib.ExitStack() as st:
        old = cx.stack
        cx.stack = st
        WG = sb(cx, tag + "WG", [128, 8, 2048], BF16)
        WA = sb(cx, tag + "WA", [128, 8, D], BF16)
        WB = sb(cx, tag + "WB", [128, 4, D], BF16)
        WO = sb(cx, tag + "WO", [128, 8, D], BF16)
        Wb = Buf()
        gain = sb(cx, tag + "g", [128, D], F32)
        parb = Buf()
        xts = [sb(cx, tag + "xt%d" % i, [128, D], F32) for i in range(2)]
        xtb = [Buf() for _ in range(2)]
        hT = [sb(cx, tag + "hT%d" % i, [128, 8, 128], BF16) for i in range(2)]
        hTb = [Buf() for _ in range(2)]
        oaT = [sb(cx, tag + "oaT%d" % i, [128, 8, 128], BF16) for i in range(2)]
        oaTb = [Buf() for _ in range(2)]
        obT = [sb(cx, tag + "obT%d" % i, [128, 4, 128], BF16) for i in range(2)]
        obTb = [Buf() for _ in range(2)]
        sga = [sb(cx, tag + "sga%d" % i, [128, D], F32) for i in range(2)]
        sgb = [sb(cx, tag + "sgb%d" % i, [128, D], F32) for i in range(2)]
        sgab = [Buf() for _ in range(2)]
        sgbb = [Buf() for _ in range(2)]
        m1 = [sb(cx, tag + "m1%d" % i, [128, 512], F32) for i in range(2)]
        m2 = [sb(cx, tag + "m2%d" % i, [128, 512], F32) for i in range(2)]
        m1b = [Buf() for _ in range(2)]
        m2b = [Buf() for _ in range(2)]
        mg = [sb(cx, tag + "mg%d" % i, [128, D], BF16) for i in range(2)]
        mgb = [Buf() for _ in range(2)]
        mT = [sb(cx, tag + "mT%d" % i, [128, 8, 128], BF16) for i in range(2)]
        mTb = [Buf() for _ in range(2)]
        S.dma("sp", gain[:], g_ap.partition_broadcast(128), writes=[cx.constb])
        load_weight_bf16(cx, S, WG, Wb, w_in_ap, 8, P_IN, c0=8704, c1=10752, cstep=2048)
        load_weight_bf16(cx, S, WA, Wb, wa_ap, 8, D, cstep=1024)
        load_weight_bf16(cx, S, WB, Wb, wb_ap, 4, D, cstep=1024)
        load_weight_bf16(cx, S, WO, Wb, wo_ap, 8, D, cstep=1024)
        oat_v = oat.rearrange("(c p) t -> p c t", p=128)
        obt_v = obt.rearrange("(c p) t -> p c t", p=128)
        nt = T // 128
        def stage_X(ti):
            t0 = ti * 128
            i2 = ti % 2
            xt, xb = xts[i2], xtb[i2]
            S.dma("sp", xt[:], xin[t0:t0 + 128, :], writes=[xb])
            S.dma("sp", oaT[i2][:], oat_v[:, :, t0:t0 + 128], writes=[oaTb[i2]])
            S.dma("sp", obT[i2][:], obt_v[:, :, t0:t0 + 128], writes=[obTb[i2]])
            emit_norm_T(cx, S, xt, xb, gain, hT[i2], hTb[i2], 0, tag)
            for n in range(2):
                for (pt, ptb, base, dst, dstb) in ((cx.psA[n], cx.psAb[n], 0, sga[i2], sgab[i2]),
                                                   (cx.psB[n], cx.psBb[n], 1024, sgb[i2], sgbb[i2])):
                    for k in range(8):
                        S.op("pe", lambda e, k=k: e.matmul(pt[:], hT[i2][:, k, :],
                                                           WG[:, k, base + n * 512:base + (n + 1) * 512],
                                                           start=(k == 0), stop=(k == 7)),
                             reads=[hTb[i2], Wb], writes=[ptb], inc=(k == 7))
                    S.op("act", lambda e: e.activation(out=dst[:, n * 512:(n + 1) * 512], in_=pt[:],
                                                       func=AF.Sigmoid),
                         reads=[ptb], writes=[dstb])
            for n in range(2):
                ns = slice(n * 512, (n + 1) * 512)
                pa, pab = cx.psY[0], cx.psYb[0]
                pb_, pbb = cx.psY[1], cx.psYb[1]
                for c in range(8):
                    S.op("pe", lambda e, c=c: e.matmul(pa[:], oaT[i2][:, c, :], WA[:, c, ns],
                                                       start=(c == 0), stop=(c == 7)),
                         reads=[oaTb[i2], Wb], writes=[pab], inc=(c == 7))
                for c in range(4):
                    S.op("pe", lambda e, c=c: e.matmul(pb_[:], obT[i2][:, c, :], WB[:, c, ns],
                                                       start=(c == 0), stop=(c == 3)),
                         reads=[obTb[i2], Wb], writes=[pbb], inc=(c == 3))
                S.op("dve", lambda e: e.tensor_tensor(out=m1[n][:], in0=sga[i2][:, ns], in1=pa[:], op=ALU.mult),
                     reads=[sgab[i2], pab], writes=[m1b[n]])
                S.op("dve", lambda e: e.tensor_tensor(out=m2[n][:], in0=sgb[i2][:, ns], in1=pb_[:], op=ALU.mult),
                     reads=[sgbb[i2], pbb], writes=[m2b[n]])
                S.op("dve", lambda e: e.tensor_tensor(out=mg[i2][:, ns], in0=m1[n][:], in1=m2[n][:], op=ALU.add),
                     reads=[m1b[n], m2b[n]], writes=[mgb[i2]])

        def stage_Y(ti):
            t0 = ti * 128
            i2 = ti % 2
            xt, xb = xts[i2], xtb[i2]
            pT, pTb = cx.psT[ti % 2], cx.psTb[ti % 2]
            for c in range(8):
                S.op("pe", lambda e, c=c: e.transpose(pT[:, c * 128:(c + 1) * 128],
                                                      mg[i2][:, c * 128:(c + 1) * 128], cx.cbf[:, 0:128]),
                     reads=[mgb[i2], cx.constb], writes=[pTb], inc=(c == 7))
            S.op("act", lambda e: e.copy(out=mT[i2][:], in_=pT[:].rearrange("p (k t) -> p k t", k=8)),
                 reads=[pTb], writes=[mTb[i2]])
            for n in range(2):
                ns = slice(n * 512, (n + 1) * 512)
                pt, ptb = cx.psA[n], cx.psAb[n]
                for c in range(8):
                    S.op("pe", lambda e, c=c: e.matmul(pt[:], mT[i2][:, c, :], WO[:, c, ns],
                                                       start=(c == 0), stop=(c == 7)),
                         reads=[mTb[i2], Wb], writes=[ptb], inc=(c == 7))
                S.op("dve", lambda e: e.tensor_tensor(out=xt[:, ns], in0=xt[:, ns], in1=pt[:], op=ALU.add),
                     reads=[ptb, xb], writes=[xb])
            S.dma("sp", xout[t0:t0 + 128, :], xt[:], reads=[xb])
        stage_X(0)
        for ti in range(nt):
            if ti + 1 < nt:
                stage_X(ti + 1)
            stage_Y(ti)
        cx.stack = old
        barrier(cx, S)


def make_ctx(nc, stack):
    cx = Ctx()
    cx.nc = nc
    cx.stack = stack
    cx.rr = 0
    S = Sched(nc, stack)
    cx.cbf = sb(cx, "cbf", [128, 640], BF16)
    cx.rmask = sb(cx, "rmask", [128, 128], F32)
    cx.cmask = sb(cx, "cmask", [128, 4], F32)
    cx.bmask4 = sb(cx, "bmask4", [128, 512], BF16)
    cx.rmask4 = sb(cx, "rmask4", [128, 512], F32)
    cx.ident = cx.cbf[:, 0:128]
    cx.constb = Buf()
    cx.junk = [sb(cx, "junk%d" % i, [128, D], BF16) for i in range(2)]
    cx.junkb = [Buf() for _ in range(2)]
    cx.ss = [sb(cx, "ss%d" % i, [128, 4], F32) for i in range(2)]
    cx.ssb = [Buf() for _ in range(2)]
    cx.hbf = [sb(cx, "hbf%d" % i, [128, D], BF16) for i in range(2)]
    cx.hbfb = [Buf() for _ in range(2)]
    cx.psT = [ps(cx, "psT%d" % i, [128, 1024], BF16) for i in range(2)]
    cx.psTb = [Buf() for _ in range(2)]
    cx.psA = [ps(cx, "psA%d" % i, [128, 512], F32) for i in range(2)]
    cx.psAb = [Buf() for _ in range(2)]
    cx.psB = [ps(cx, "psB%d" % i, [128, 512], F32) for i in range(2)]
    cx.psBb = [Buf() for _ in range(2)]
    cx.psY = [ps(cx, "psY%d" % i, [128, 512], F32) for i in range(2)]
    cx.psYb = [Buf() for _ in range(2)]
    return cx, S


def build_nc(T, depth=2, phases=None, debug=False):
    nc = bass.Bass("TRN2", target_bir_lowering=False)
    dt = nc.dram_tensor
    x = dt("x", [T, D], F32, kind="ExternalInput").ap()
    cst = dt("cst", [128, 1796], F32, kind="ExternalInput").ap()
    cos4 = dt("cos4", [T, 512], F32, kind="ExternalInput").ap()
    sin4 = dt("sin4", [T, 512], F32, kind="ExternalInput").ap()
    f1n = dt("ffn1_norm", [depth, D], F32, kind="ExternalInput").ap()
    f1i = dt("ffn1_w_in", [depth, D, 2 * DFF], F32, kind="ExternalInput").ap()
    f1o = dt("ffn1_w_out", [depth, DFF, D], F32, kind="ExternalInput").ap()
    mn = dt("mix_norm", [depth, D], F32, kind="ExternalInput").ap()
    win = dt("w_in", [depth, D, P_IN], F32, kind="ExternalInput").ap()
    lgT = dt("lgT", [128, depth, 8], F32, kind="ExternalInput").ap()
    gnT = dt("gnT", [depth, 128, 8], F32, kind="ExternalInput").ap()
    aqn = dt("attn_q_norm", [depth, 3, 128], F32, kind="ExternalInput").ap()
    akn = dt("attn_k_norm", [depth, 3, 128], F32, kind="ExternalInput").ap()
    wa = dt("w_branch_a", [depth, D, D], F32, kind="ExternalInput").ap()
    wb = dt("w_branch_b", [depth, 512, D], F32, kind="ExternalInput").ap()
    wo = dt("w_out", [depth, D, D], F32, kind="ExternalInput").ap()
    f2n = dt("ffn2_norm", [depth, D], F32, kind="ExternalInput").ap()
    f2i = dt("ffn2_w_in", [depth, D, 2 * DFF], F32, kind="ExternalInput").ap()
    f2o = dt("ffn2_w_out", [depth, DFF, D], F32, kind="ExternalInput").ap()
    y = dt("y", [T, D], F32, kind="ExternalOutput").ap()
    kind = "ExternalOutput" if debug else "Internal"
    xa = dt("xa", [T, D], F32, kind=kind).ap()
    xb_ = dt("xb", [T, D], F32, kind=kind).ap()
    oat = dt("oat", [D, T], BF16, kind=kind).ap()
    obt = dt("obt", [512, T], BF16, kind=kind).ap()
    qkv = dt("qkv", [T, 3, 1536], BF16, kind=kind).ap()
    with contextlib.ExitStack() as stack:
        cx, S = make_ctx(nc, stack)
        load_consts(cx, S, cst)
        cur = x
        for l in range(depth):
            last = (l == depth - 1)
            tg = "L%d" % l
            P = phases
            if P is None or "f1" in P:
                phase_ffn(cx, S, cur, xa, f1n[l:l + 1, :], f1i[l], f1o[l], T, tg + "f1")
            if P is None or "hg" in P:
                phase_hgrn(cx, S, xa, oat, mn[l:l + 1, :], win[l], lgT, gnT[l], T, l, tg + "hg")
            if P is None or "ap" in P:
                phase_attn_proj(cx, S, xa, qkv, mn[l:l + 1, :], win[l], aqn[l], akn[l], cos4, sin4, T, tg + "ap")
            if P is None or "ac" in P:
                phase_attn_core(cx, S, qkv, obt, T, tg + "ac")
            if P is None or "po" in P:
                phase_out(cx, S, xa, xb_, oat, obt, mn[l:l + 1, :], win[l], wa[l], wb[l], wo[l], T, tg + "po")
            if P is None or "f2" in P:
                phase_ffn(cx, S, xb_, y if last else xa, f2n[l:l + 1, :], f2i[l], f2o[l], T, tg + "f2")
            cur = xa
        barrier(cx, S)
        cx.ninst = S.ninst
        print("ninst", S.ninst, {k: v for k, v in S.cnt.items()})
    return nc


def rope_host(T):
    pos = np.arange(T, dtype=np.float32)
    inv = (np.float32(10000.0) ** (-np.arange(0, 128, 2, dtype=np.float32) / np.float32(128))).astype(np.float32)
    ang = pos[:, None] * inv[None, :]
    ang = np.concatenate([ang, ang], axis=-1).astype(np.float32)
    cos = np.cos(ang).astype(np.float32)
    sin = np.sin(ang).astype(np.float32)
    sin[:, :64] = -sin[:, :64]
    return np.tile(cos, (1, 4)), np.tile(sin, (1, 4))


def make_in_maps(inputs, T, depth=2):
    cos4, sin4 = rope_host(T)
    base = {k: np.ascontiguousarray(v, dtype=np.float32) for k, v in inputs.items() if k not in ("x", "hgrn_lb_logits", "hgrn_out_norm")}
    base["cst"] = host_consts()
    base["cos4"] = np.ascontiguousarray(cos4)
    base["sin4"] = np.ascontiguousarray(sin4)
    lg = np.asarray(inputs["hgrn_lb_logits"], np.float32)
    base["lgT"] = np.ascontiguousarray(lg.reshape(depth, 8, 128).transpose(2, 0, 1))
    gn = np.asarray(inputs["hgrn_out_norm"], np.float32)
    base["gnT"] = np.ascontiguousarray(gn.reshape(depth, 8, 128).transpose(0, 2, 1))
    x = np.asarray(inputs["x"], np.float32)
    maps = []
    for c in range(NCORES):
        m = dict(base)
        m["x"] = np.ascontiguousarray(x[c % x.shape[0], :T])
        maps.append(m)
    return maps


def kernel(**inputs):
    x = np.asarray(inputs["x"])
    B, T, _ = x.shape
    nc = build_nc(T, depth=2)
    maps = make_in_maps(inputs, T, depth=2)
    res = run_bass_kernel_spmd(nc, maps, core_ids=list(range(NCORES)))
    out = np.stack([np.asarray(res.results[b]["y"], dtype=np.float32) for b in range(B)], axis=0)
    return out


def build_ac_only(T):
    nc = bass.Bass("TRN2", target_bir_lowering=False)
    dt = nc.dram_tensor
    cst = dt("cst", [128, 1796], F32, kind="ExternalInput").ap()
    qkv = dt("qkv", [T, 3, 1536], BF16, kind="ExternalInput").ap()
    obt = dt("obt", [512, T], BF16, kind="ExternalOutput").ap()
    with contextlib.ExitStack() as stack:
        cx, S = make_ctx(nc, stack)
        load_consts(cx, S, cst)
        phase_attn_core(cx, S, qkv, obt, T, "ac")
        barrier(cx, S)
    return nc
```

```python
import contextlib
import numpy as np
import concourse.bass as bass
import concourse.mybir as mybir
from concourse.bass_utils import run_bass_kernel_spmd

F32 = mybir.dt.float32
BF16 = mybir.dt.bfloat16
AF = mybir.ActivationFunctionType
ALU = mybir.AluOpType
AX = mybir.AxisListType

D = 1024
DFF = 2816
P_IN = 10752
EPS = 1e-6
NCORES = 8


class Buf:
    __slots__ = ("w", "r", "name", "extra")

    def __init__(self, name=""):
        self.w = None
        self.r = {}
        self.name = name
        self.extra = []


class Sched:
    def __init__(self, nc, stack, ndma=6):
        self.nc = nc
        self.h = {"pe": nc.tensor, "act": nc.scalar, "dve": nc.vector,
                  "pool": nc.gpsimd, "sp": nc.sync}
        self.sems = {}
        self.cnt = {}
        for e in ("pe", "act", "dve", "pool"):
            self.sems[e] = stack.enter_context(nc.semaphore("s_" + e))
            self.cnt[e] = 0
        self.seen = {e: {} for e in self.h}
        self.dq = {}
        for q in ("sp", "pool", "act"):
            names = ["d_%s%d" % (q, i) for i in range(ndma)]
            for n in names:
                self.sems[n] = stack.enter_context(nc.semaphore(n))
                self.cnt[n] = 0
            self.dq[q] = [names, 0]
        self.ninst = 0

    def _wait(self, e, deps):
        need = {}
        for d in deps:
            if d is None:
                continue
            k, v, src = d
            if src == e and e == "pe":
                continue
            if self.seen[e].get(k, 0) >= v:
                continue
            if need.get(k, 0) < v:
                need[k] = v
        for k, v in need.items():
            self.h[e].wait_ge(self.sems[k], v)
            self.seen[e][k] = v
            self.ninst += 1

    def _deps(self, e, reads, writes):
        deps = []
        for b in reads:
            deps.append(b.w)
            if b.extra:
                deps.extend(b.extra)
        for b in writes:
            deps.append(b.w)
            for r in b.r.values():
                if r[2] == e and e != "pool":
                    continue
                deps.append(r)
        return deps

    def _record(self, ev, reads, writes):
        for b in reads:
            old = b.r.get(ev[0])
            if old is None or old[1] < ev[1]:
                b.r[ev[0]] = ev
        for b in writes:
            b.w = ev
            b.r = {}

    def op(self, e, fn, reads=(), writes=(), inc=True):
        self._wait(e, self._deps(e, reads, writes))
        ins = fn(self.h[e])
        if inc:
            self.cnt[e] += 1
            ins.then_inc(self.sems[e], 1)
            ev = (e, self.cnt[e], e)
        else:
            ev = (e, self.cnt[e] + 1, e)
        self.ninst += 1
        self._record(ev, reads, writes)
        return ev

    def dma(self, q, out, in_, reads=(), writes=(), **kw):
        names, idx = self.dq[q]
        k = names[idx % len(names)]
        self.dq[q][1] = idx + 1
        deps = self._deps("dma_issue", reads, writes)
        if self.cnt[k] > 0:
            deps.append((k, self.cnt[k], "dma"))
        self._wait(q, deps)
        ins = self.h[q].dma_start(out=out, in_=in_, **kw)
        self.cnt[k] += 16
        ins.then_inc(self.sems[k], 16)
        ev = (k, self.cnt[k], "dma")
        self.ninst += 1
        self._record(ev, reads, writes)
        return ev

    def finish(self, e="sp"):
        for k, c in self.cnt.items():
            if c > 0 and self.seen[e].get(k, 0) < c:
                self.h[e].wait_ge(self.sems[k], c)
                self.seen[e][k] = c


class Ctx:
    pass


def sb(cx, name, shape, dt):
    t = cx.stack.enter_context(cx.nc.sbuf_tensor("sb_" + name, list(shape), dt))
    return t


def ps(cx, name, shape, dt):
    t = cx.stack.enter_context(cx.nc.psum_tensor("ps_" + name, list(shape), dt))
    return t


def emit_norm_T(cx, S, xt, xb, gain, hT, hTb, col0, tag, fixed=None):
    nc = cx.nc
    i = cx.rr % 2 if fixed is None else fixed
    cx.rr += 1
    junk, junkb = cx.junk[i], cx.junkb[i]
    ss, ssb = cx.ss[i], cx.ssb[i]
    hb, hbb = cx.hbf[i], cx.hbfb[i]
    pT, pTb = cx.psT[i], cx.psTb[i]
    S.op("act", lambda e: e.activation(out=junk[:], in_=xt[:], func=AF.Square,
                                       accum_out=ss[:, 0:1]),
         reads=[xb], writes=[junkb, ssb])
    S.op("act", lambda e: e.activation(out=ss[:, 1:2], in_=ss[:, 0:1], func=AF.Sqrt,
                                       scale=1.0 / D, bias=EPS),
         reads=[ssb], writes=[ssb])
    S.op("dve", lambda e: e.reciprocal(out=ss[:, 2:3], in_=ss[:, 1:2]),
         reads=[ssb], writes=[ssb])
    S.op("dve", lambda e: e.scalar_tensor_tensor(out=hb[:], in0=xt[:], scalar=ss[:, 2:3],
                                                 in1=gain[:], op0=ALU.mult, op1=ALU.mult),
         reads=[xb, ssb, cx.constb], writes=[hbb])
    for k in range(8):
        S.op("pe", lambda e, k=k: e.transpose(pT[:, k * 128:(k + 1) * 128],
                                              hb[:, k * 128:(k + 1) * 128], cx.ident[:]),
             reads=[hbb, cx.constb], writes=[pTb], inc=(k == 7))
    S.op("act", lambda e: e.copy(out=hT[:, :, col0:col0 + 128],
                                 in_=pT[:].rearrange("p (k t) -> p k t", k=8)),
         reads=[pTb], writes=[hTb])


def load_weight_bf16(cx, S, dst, dstb, src_ap, nk, ncols, c0=0, c1=None, cstep=1408):
    c1 = ncols if c1 is None else c1
    v = src_ap.rearrange("(k p) c -> p k c", p=128)
    c = c0
    while c < c1:
        ce = min(c1, c + cstep)
        for k in range(nk):
            ev = S.dma("pool", dst[:, k, c - c0:ce - c0], v[:, k, c:ce])
            dstb.extra.append(ev)
        c = ce


def phase_ffn(cx, S, xin, xout, g_ap, w_in_ap, w_out_ap, T, tag):
    nc = cx.nc
    G = 256
    with contextlib.ExitStack() as st:
        old = cx.stack
        cx.stack = st
        w1 = sb(cx, tag + "w1", [128, 8, 2 * DFF], BF16)
        w2 = sb(cx, tag + "w2", [128, 22, D], BF16)
        gain = sb(cx, tag + "g", [128, D], F32)
        xts = [sb(cx, tag + "xt%d" % i, [128, D], F32) for i in range(4)]
        hT = [sb(cx, tag + "hT%d" % i, [128, 8, G], BF16) for i in range(2)]
        uT = sb(cx, tag + "uT", [128, 22, G], BF16)
        sa = [sb(cx, tag + "sa%d" % i, [128, G], F32) for i in range(2)]
        w1b, w2b, uTb = Buf(), Buf(), Buf()
        xtb = [Buf() for _ in range(4)]
        hTb = [Buf() for _ in range(2)]
        sab = [Buf() for _ in range(2)]
        S.dma("sp", gain[:], g_ap.partition_broadcast(128), writes=[cx.constb])
        load_weight_bf16(cx, S, w1, w1b, w_in_ap, 8, 2 * DFF)
        load_weight_bf16(cx, S, w2, w2b, w_out_ap, 22, D, cstep=1024)
        ng = T // G
        for gi in range(ng):
            hTg, hTgb = hT[gi % 2], hTb[gi % 2]
            for i in range(2):
                xt, xb = xts[(gi % 2) * 2 + i], xtb[(gi % 2) * 2 + i]
                r0 = gi * G + i * 128
                S.dma("sp", xt[:], xin[r0:r0 + 128, :], writes=[xb])
                emit_norm_T(cx, S, xt, xb, gain, hTg, hTgb, i * 128, tag)
            for j in range(22):
                pA, pAb = cx.psA[j % 2], cx.psAb[j % 2]
                pB, pBb = cx.psB[j % 2], cx.psBb[j % 2]
                for k in range(8):
                    S.op("pe", lambda e, k=k: e.matmul(pA[:, 0:G], w1[:, k, j * 128:(j + 1) * 128],
                                                       hTg[:, k, :], start=(k == 0), stop=(k == 7)),
                         reads=[w1b, hTgb], writes=[pAb], inc=(k == 7))
                for k in range(8):
                    S.op("pe", lambda e, k=k: e.matmul(pB[:, 0:G],
                                                       w1[:, k, DFF + j * 128:DFF + (j + 1) * 128],
                                                       hTg[:, k, :], start=(k == 0), stop=(k == 7)),
                         reads=[w1b, hTgb], writes=[pBb], inc=(k == 7))
                s_, s_b = sa[j % 2], sab[j % 2]
                S.op("act", lambda e: e.activation(out=s_[:], in_=pA[:, 0:G], func=AF.Silu),
                     reads=[pAb], writes=[s_b])
                S.op("dve", lambda e: e.tensor_tensor(out=uT[:, j, :], in0=s_[:], in1=pB[:, 0:G],
                                                      op=ALU.mult),
                     reads=[s_b, pBb], writes=[uTb])
            for i in range(2):
                xt, xb = xts[(gi % 2) * 2 + i], xtb[(gi % 2) * 2 + i]
                r0 = gi * G + i * 128
                for n in range(2):
                    pY, pYb = cx.psY[n], cx.psYb[n]
                    for j in range(22):
                        S.op("pe", lambda e, j=j: e.matmul(pY[:], uT[:, j, i * 128:(i + 1) * 128],
                                                           w2[:, j, n * 512:(n + 1) * 512],
                                                           start=(j == 0), stop=(j == 21)),
                             reads=[uTb, w2b], writes=[pYb], inc=(j == 21))
                    S.op("dve", lambda e: e.scalar_tensor_tensor(
                        out=xt[:, n * 512:(n + 1) * 512], in0=pY[:], scalar=0.5,
                        in1=xt[:, n * 512:(n + 1) * 512], op0=ALU.mult, op1=ALU.add),
                        reads=[pYb, xb], writes=[xb])
                S.dma("sp", xout[r0:r0 + 128, :], xt[:], reads=[xb])
        cx.stack = old
        S.finish("sp")
        barrier(cx, S)


def barrier(cx, S):
    for e in ("pe", "act", "dve", "pool", "sp"):
        S.finish(e)


def make_ctx(nc, stack):
    cx = Ctx()
    cx.nc = nc
    cx.stack = stack
    cx.rr = 0
    S = Sched(nc, stack)
    cx.ident = sb(cx, "ident", [128, 128], BF16)
    cx.constb = Buf()
    cx.junk = [sb(cx, "junk%d" % i, [128, D], BF16) for i in range(2)]
    cx.junkb = [Buf() for _ in range(2)]
    cx.ss = [sb(cx, "ss%d" % i, [128, 4], F32) for i in range(2)]
    cx.ssb = [Buf() for _ in range(2)]
    cx.hbf = [sb(cx, "hbf%d" % i, [128, D], BF16) for i in range(2)]
    cx.hbfb = [Buf() for _ in range(2)]
    cx.psT = [ps(cx, "psT%d" % i, [128, 1024], BF16) for i in range(2)]
    cx.psTb = [Buf() for _ in range(2)]
    cx.psA = [ps(cx, "psA%d" % i, [128, 512], F32) for i in range(2)]
    cx.psAb = [Buf() for _ in range(2)]
    cx.psB = [ps(cx, "psB%d" % i, [128, 512], F32) for i in range(2)]
    cx.psBb = [Buf() for _ in range(2)]
    cx.psY = [ps(cx, "psY%d" % i, [128, 512], F32) for i in range(2)]
    cx.psYb = [Buf() for _ in range(2)]
    return cx, S


def load_consts(cx, S, ident_ap):
    S.dma("pool", cx.ident[:], ident_ap, writes=[cx.constb])


def load_consts(cx, S, cst_ap):
    S.dma("pool", cx.cbf[:], cst_ap[:, 0:640], writes=[cx.constb])
    S.dma("sp", cx.rmask[:], cst_ap[:, 640:768], writes=[cx.constb])
    S.dma("sp", cx.cmask[:], cst_ap[:, 768:772], writes=[cx.constb])
    S.dma("pool", cx.bmask4[:], cst_ap[:, 772:1284], writes=[cx.constb])
    S.dma("sp", cx.rmask4[:], cst_ap[:, 1284:1796], writes=[cx.constb])


def host_consts():
    c = np.zeros((128, 1796), np.float32)
    i = np.arange(128)
    c[:, 0:128] = np.eye(128)
    c[:, 128:256] = 1.0
    s_, t_ = i[:, None], i[None, :]
    c[:, 256:384] = ((s_ // 32 == t_ // 32) & (s_ <= t_))
    c[:, 384:512] = (s_ >= t_)
    c[:, 512:640] = (s_ <= t_)
    c[:, 640:768] = (t_ % 32 != 0) * np.ones((128, 1))
    for j in range(4):
        c[:, 768 + j] = (i // 32 == j)
        c[:, 772 + j * 128:772 + (j + 1) * 128] = c[:, 256:384]
        c[:, 1284 + j * 128:1284 + (j + 1) * 128] = c[:, 640:768]
    return c


def phase_hgrn(cx, S, xin, oat, g_ap, w_in_ap, lgT_ap, gnT_ap, T, layer, tag):
    nc = cx.nc
    ident = cx.cbf[:, 0:128]
    ones = cx.cbf[:, 128:256]
    with contextlib.ExitStack() as st:
        old = cx.stack
        cx.stack = st
        W = sb(cx, tag + "W", [128, 8, 4096], BF16)
        Wb = Buf()
        gain = sb(cx, tag + "g", [128, D], F32)
        lg = sb(cx, tag + "lg", [128, lgT_ap.shape[1], 8], F32)
        lb = sb(cx, tag + "lb", [128, 8], F32)
        oml = sb(cx, tag + "oml", [128, 8], F32)
        gn = sb(cx, tag + "gn", [128, 8], F32)
        zer = sb(cx, tag + "zer", [128, 128], F32)
        lbf = sb(cx, tag + "lbf", [128, 8, 128], F32)
        omlf = sb(cx, tag + "omlf", [128, 8, 128], F32)
        gnf = sb(cx, tag + "gnf", [128, 8, 128], F32)
        parb = Buf()
        xts = [sb(cx, tag + "xt%d" % i, [128, D], F32) for i in range(2)]
        xtb = [Buf() for _ in range(2)]
        hT = [sb(cx, tag + "hT%d" % i, [128, 8, 128], BF16) for i in range(2)]
        hTb = [Buf() for _ in range(2)]
        v = [sb(cx, tag + "v%d" % i, [128, D], BF16) for i in range(2)]
        vb = [Buf() for _ in range(2)]
        S32 = [sb(cx, tag + "S32_%d" % i, [128, 512], F32) for i in range(2)]
        S32b = [Buf() for _ in range(2)]
        Sbf = [sb(cx, tag + "Sbf_%d" % i, [128, 8, 512], BF16) for i in range(2)]
        Sbfb = [[Buf() for _ in range(8)] for _ in range(2)]
        NB = 2

        def mk(name, dt, w=512, n=NB):
            return ([sb(cx, tag + name + str(i), [128, w], dt) for i in range(n)],
                    [Buf() for _ in range(n)])
        q32, q32b = mk("q", F32)
        f32t, f32b = mk("f", F32)
        gt, gtb = mk("gt", F32, n=4)
        lf, lfb = mk("lf", F32)
        kk, kkb = mk("kk", F32)
        gc, gcb = mk("gc", F32)
        eq, eqb = mk("eq", F32, n=4)
        ek, ekb = mk("ek", F32)
        qg, qgb = mk("qg", BF16, n=4)
        kg, kgb = mk("kg", BF16)
        kgT, kgTb = mk("kgT", BF16, w=2048)
        AT, ATb = mk("AT", BF16, n=4)
        sq, sqb = mk("sq", BF16)
        rs, rsb = mk("rs", F32)
        on, onb = mk("on", F32)
        tmp, tmpb = mk("tmp", F32)
        oa, oab = mk("oa", BF16, n=4)

        S.dma("sp", gain[:], g_ap.partition_broadcast(128), writes=[cx.constb])
        S.dma("sp", lg[:], lgT_ap, writes=[parb])
        S.dma("sp", gn[:], gnT_ap, writes=[parb])
        if layer == 0:
            S.op("dve", lambda e: e.memset(lb[:], 0.0), writes=[parb])
        else:
            S.op("dve", lambda e: e.tensor_tensor(out=lb[:], in0=lg[:, 1, :], in1=lg[:, 0, :],
                                                  op=ALU.subtract), reads=[parb], writes=[parb])
            S.op("act", lambda e: e.activation(out=lb[:], in_=lb[:], func=AF.Sigmoid),
                 reads=[parb], writes=[parb])
        S.op("dve", lambda e: e.tensor_scalar(out=oml[:], in0=lb[:], scalar1=-1.0, scalar2=1.0,
                                              op0=ALU.mult, op1=ALU.add), reads=[parb], writes=[parb])
        S.op("dve", lambda e: e.memset(zer[:], 0.0), writes=[parb])
        for hd in range(8):
            for (dst, src) in ((lbf, lb), (omlf, oml), (gnf, gn)):
                S.op("dve", lambda e: e.tensor_scalar(out=dst[:, hd, :], in0=zer[:], scalar1=src[:, hd:hd + 1],
                                                      scalar2=None, op0=ALU.add),
                     reads=[parb], writes=[parb])
        for gi in range(2):
            S.op("dve", lambda e: e.memset(S32[gi][:], 0.0), writes=[S32b[gi]])
            S.op("pool", lambda e: e.memset(Sbf[gi][:], 0.0), writes=Sbfb[gi])
        load_weight_bf16(cx, S, W, Wb, w_in_ap, 8, P_IN, c0=0, c1=4096, cstep=2048)

        pQ, pQb = cx.psA[0], cx.psAb[0]
        pF, pFb = cx.psA[1], cx.psAb[1]
        pG, pGb = cx.psB[0], cx.psBb[0]
        pAT, pATb = cx.psB[1], cx.psBb[1]
        pS, pSb = cx.psY[0], cx.psYb[0]
        pO, pOb = cx.psY[1], cx.psYb[1]
        pK, pKb = cx.psT[1], cx.psTb[1]
        oat_v = oat.rearrange("(h p) t -> p h t", p=128)
        nt = T // 128
        for ti in range(nt):
            t0 = ti * 128
            xt, xb = xts[ti % 2], xtb[ti % 2]
            hTt, hTtb = hT[ti % 2], hTb[ti % 2]
            vt, vtb = v[ti % 2], vb[ti % 2]
            S.dma("sp", xt[:], xin[t0:t0 + 128, :], writes=[xb])
            emit_norm_T(cx, S, xt, xb, gain, hTt, hTtb, 0, tag, fixed=0)
            for half in range(2):
                for k in range(8):
                    S.op("pe", lambda e, k=k: e.matmul(pS[:], hTt[:, k, :],
                                                       W[:, k, 2048 + half * 512:2048 + (half + 1) * 512],
                                                       start=(k == 0), stop=(k == 7)),
                         reads=[hTtb, Wb], writes=[pSb], inc=(k == 7))
                S.op("act", lambda e: e.copy(out=vt[:, half * 512:(half + 1) * 512], in_=pS[:]),
                     reads=[pSb], writes=[vtb])
            for gi in range(2):
                b = gi
                b4 = (2 * ti + gi) % 4
                c0 = gi * 512
                hsl = slice(gi * 4, gi * 4 + 4)
                for (pt, ptb, base) in ((pQ, pQb, 0), (pF, pFb, 1024), (pG, pGb, 3072)):
                    for hh in range(4):
                        for k in range(8):
                            S.op("pe", lambda e, k=k: e.matmul(
                                pt[:, hh * 128:(hh + 1) * 128],
                                W[:, k, base + c0 + hh * 128:base + c0 + (hh + 1) * 128],
                                hTt[:, k, :], start=(k == 0), stop=(k == 7)),
                                reads=[hTtb, Wb], writes=[ptb], inc=(k == 7 and hh == 3))
                S.op("act", lambda e: e.activation(out=q32[b][:], in_=pQ[:], func=AF.Silu),
                     reads=[pQb], writes=[q32b[b]])
                S.op("act", lambda e: e.activation(out=gt[b4][:], in_=pG[:], func=AF.Silu),
                     reads=[pGb], writes=[gtb[b4]])
                S.op("act", lambda e: e.activation(out=f32t[b][:], in_=pF[:], func=AF.Sigmoid),
                     reads=[pFb], writes=[f32b[b]])
                S.op("pool", lambda e: e.tensor_tensor(out=gt[b4][:], in0=gt[b4][:],
                                                       in1=gnf[:, hsl, :].rearrange("p h t -> p (h t)"), op=ALU.mult),
                     reads=[gtb[b4], parb], writes=[gtb[b4]])
                if layer != 0:
                    S.op("dve", lambda e: e.tensor_tensor(out=f32t[b][:], in0=f32t[b][:],
                                                          in1=omlf[:, hsl, :].rearrange("p h t -> p (h t)"),
                                                          op=ALU.mult),
                         reads=[f32b[b], parb], writes=[f32b[b]])
                    S.op("dve", lambda e: e.tensor_tensor(out=f32t[b][:], in0=f32t[b][:],
                                                          in1=lbf[:, hsl, :].rearrange("p h t -> p (h t)"),
                                                          op=ALU.add),
                         reads=[f32b[b], parb], writes=[f32b[b]])
                S.op("act", lambda e: e.activation(out=lf[b][:], in_=f32t[b][:], func=AF.Ln),
                     reads=[f32b[b]], writes=[lfb[b]])
                S.op("act", lambda e: e.activation(out=kk[b][:], in_=f32t[b][:], func=AF.Identity,
                                                   scale=-1.0, bias=1.0),
                     reads=[f32b[b]], writes=[kkb[b]])
                S.op("dve", lambda e: e.tensor_tensor_scan(out=gc[b][:], data0=cx.rmask4[:], data1=lf[b][:],
                                                           initial=0.0, op0=ALU.mult, op1=ALU.add),
                     reads=[lfb[b], cx.constb], writes=[gcb[b]])
                S.op("act", lambda e: e.activation(out=eq[b4][:], in_=gc[b][:], func=AF.Exp),
                     reads=[gcb[b]], writes=[eqb[b4]])
                S.op("act", lambda e: e.activation(out=ek[b][:], in_=gc[b][:], func=AF.Exp, scale=-1.0),
                     reads=[gcb[b]], writes=[ekb[b]])
                S.op("dve", lambda e: e.tensor_tensor(out=kg[b][:], in0=kk[b][:], in1=ek[b][:], op=ALU.mult),
                     reads=[kkb[b], ekb[b]], writes=[kgb[b]])
                S.op("dve", lambda e: e.tensor_tensor(out=qg[b4][:], in0=q32[b][:], in1=eq[b4][:], op=ALU.mult),
                     reads=[q32b[b], eqb[b4]], writes=[qgb[b4]])
            for gi in range(2):
                b = gi
                b4 = (2 * ti + gi) % 4
                c0 = gi * 512
                hsl = slice(gi * 4, gi * 4 + 4)
                for hh in range(4):
                    S.op("pe", lambda e: e.transpose(pK[:, hh * 128:(hh + 1) * 128],
                                                     kg[b][:, hh * 128:(hh + 1) * 128], ident),
                         reads=[kgb[b], cx.constb], writes=[pKb], inc=(hh == 3))
                for j in range(4):
                    S.op("act", lambda e: e.activation(out=kgT[b][:, j * 512:(j + 1) * 512], in_=pK[:, 0:512],
                                                       func=AF.Copy, scale=cx.cmask[:, j:j + 1]),
                         reads=[pKb, cx.constb], writes=[kgTb[b]])
                for hh in range(4):
                    hs = slice(hh * 128, (hh + 1) * 128)
                    S.op("pe", lambda e: e.matmul(pAT[:, hs], kg[b][:, hs], qg[b4][:, hs], start=True, stop=True),
                         reads=[kgb[b], qgb[b4]], writes=[pATb], inc=(hh == 3))
                S.op("dve", lambda e: e.tensor_tensor(out=AT[b4][:], in0=pAT[:], in1=cx.bmask4[:], op=ALU.mult),
                     reads=[pATb, cx.constb], writes=[ATb[b4]])
                eqv = eq[b4][:].rearrange("p (h t) -> p h t", h=4)
                for j in range(4):
                    for hh in range(4):
                        hs = slice(hh * 128, (hh + 1) * 128)
                        S.op("pe", lambda e: e.matmul(pS[:, hs], kgT[b][:, j * 512 + hh * 128:j * 512 + (hh + 1) * 128],
                                                      vt[:, c0 + hh * 128:c0 + (hh + 1) * 128],
                                                      start=True, stop=True),
                             reads=[kgTb[b], vtb], writes=[pSb], inc=(hh == 3))
                    eglb = eqv[:, :, 32 * j + 31:32 * j + 32].to_broadcast([128, 4, 128])
                    S.op("dve", lambda e: e.tensor_tensor(out=tmp[b][:], in0=S32[gi][:], in1=pS[:], op=ALU.add),
                         reads=[S32b[gi], pSb], writes=[tmpb[b]])
                    S.op("dve", lambda e: e.tensor_tensor(out=S32[gi][:].rearrange("p (h t) -> p h t", h=4),
                                                          in0=tmp[b][:].rearrange("p (h t) -> p h t", h=4),
                                                          in1=eglb, op=ALU.mult),
                         reads=[tmpb[b], eqb[b4]], writes=[S32b[gi]])
                    sl = ((ti % 2) * 4 + j + 1) % 8
                    S.op("act", lambda e: e.copy(out=Sbf[gi][:, sl, :], in_=S32[gi][:]),
                         reads=[S32b[gi]], writes=[Sbfb[gi][sl]])
            for gi in range(2):
                b = gi
                b4 = (2 * ti + gi) % 4
                c0 = gi * 512
                hsl = slice(gi * 4, gi * 4 + 4)
                for hh in range(4):
                    hs = slice(hh * 128, (hh + 1) * 128)
                    S.op("pe", lambda e: e.matmul(pO[:, hs], vt[:, c0 + hh * 128:c0 + (hh + 1) * 128], AT[b4][:, hs],
                                                  start=True, stop=False),
                         reads=[vtb, ATb[b4]], writes=[pOb], inc=False)
                    for j in range(4):
                        sl = ((ti % 2) * 4 + j) % 8
                        S.op("pe", lambda e: e.matmul(pO[:, hh * 128 + 32 * j:hh * 128 + 32 * j + 32],
                                                      Sbf[gi][:, sl, hs], qg[b4][:, hh * 128 + 32 * j:hh * 128 + 32 * j + 32],
                                                      start=False, stop=(j == 3)),
                             reads=[Sbfb[gi][sl], qgb[b4]], writes=[pOb], inc=(j == 3 and hh == 3))
                S.op("act", lambda e: e.activation(out=sq[b][:], in_=pO[:], func=AF.Square),
                     reads=[pOb], writes=[sqb[b]])
                S.op("pe", lambda e: e.matmul(pAT[:], ones, sq[b][:], start=True, stop=True),
                     reads=[sqb[b], cx.constb], writes=[pATb])
                S.op("act", lambda e: e.activation(out=rs[b][:], in_=pAT[:], func=AF.Sqrt,
                                                   scale=1.0 / 128, bias=EPS),
                     reads=[pATb], writes=[rsb[b]])
                S.op("dve", lambda e: e.reciprocal(out=rs[b][:], in_=rs[b][:]),
                     reads=[rsb[b]], writes=[rsb[b]])
                S.op("dve", lambda e: e.tensor_tensor(out=on[b][:], in0=pO[:], in1=rs[b][:], op=ALU.mult),
                     reads=[pOb, rsb[b]], writes=[onb[b]])
                ob = (2 * ti + gi) % 4
                S.op("dve", lambda e: e.tensor_tensor(out=oa[ob][:], in0=on[b][:], in1=gt[b4][:], op=ALU.mult),
                     reads=[onb[b], gtb[b4]], writes=[oab[ob]])
                S.dma("sp", oat_v[:, gi * 4:gi * 4 + 4, t0:t0 + 128],
                      oa[ob][:].rearrange("p (h t) -> p h t", h=4), reads=[oab[ob]])
        cx.stack = old
        barrier(cx, S)


def phase_attn_proj(cx, S, xin, qkv, g_ap, w_in_ap, qn_ap, kn_ap, cos_ap, sin_ap, T, tag):
    nc = cx.nc
    with contextlib.ExitStack() as st:
        old = cx.stack
        cx.stack = st
        W = sb(cx, tag + "W", [128, 8, 4608], BF16)
        Wb = Buf()
        gain = sb(cx, tag + "g", [128, D], F32)
        gq = sb(cx, tag + "gq", [128, 2, 3, 128], F32)
        parb = Buf()
        xts = [sb(cx, tag + "xt%d" % i, [128, D], F32) for i in range(2)]
        xtb = [Buf() for _ in range(2)]
        hT = [sb(cx, tag + "hT%d" % i, [128, 8, 128], BF16) for i in range(2)]
        hTb = [Buf() for _ in range(2)]
        cs = [sb(cx, tag + "cs%d" % i, [128, 2, 512], F32) for i in range(2)]
        csb = [Buf() for _ in range(2)]
        ob = [sb(cx, tag + "ob%d" % i, [128, 3, 1536], BF16) for i in range(2)]
        obb = [Buf() for _ in range(2)]
        NB = 2
        ssq = [sb(cx, tag + "ssq%d" % i, [128, 8], F32) for i in range(NB)]
        ssqb = [Buf() for _ in range(NB)]
        jk = [sb(cx, tag + "jk%d" % i, [128, 128], BF16) for i in range(NB)]
        jkb = [Buf() for _ in range(NB)]
        qn = [sb(cx, tag + "qn%d" % i, [128, 512], F32) for i in range(NB)]
        qnb = [Buf() for _ in range(NB)]
        t1 = [sb(cx, tag + "t1%d" % i, [128, 512], F32) for i in range(NB)]
        t1b = [Buf() for _ in range(NB)]
        t2 = [sb(cx, tag + "t2%d" % i, [128, 512], F32) for i in range(NB)]
        t2b = [Buf() for _ in range(NB)]
        S.dma("sp", gain[:], g_ap.partition_broadcast(128), writes=[cx.constb])
        for g in range(3):
            S.dma("sp", gq[:, 0, g, :], qn_ap[g:g + 1, :].partition_broadcast(128), writes=[parb])
            S.dma("sp", gq[:, 1, g, :], kn_ap[g:g + 1, :].partition_broadcast(128), writes=[parb])
        load_weight_bf16(cx, S, W, Wb, w_in_ap, 8, P_IN, c0=4096, c1=8704, cstep=1536)
        pss = [(cx.psA[0], cx.psAb[0]), (cx.psA[1], cx.psAb[1]), (cx.psB[0], cx.psBb[0]), (cx.psB[1], cx.psBb[1])]
        nt = T // 128
        cnt = 0
        for ti in range(nt):
            t0 = ti * 128
            xt, xb = xts[ti % 2], xtb[ti % 2]
            hTt, hTtb = hT[ti % 2], hTb[ti % 2]
            c_, c_b = cs[ti % 2], csb[ti % 2]
            o_, o_b = ob[ti % 2], obb[ti % 2]
            S.dma("sp", xt[:], xin[t0:t0 + 128, :], writes=[xb])
            S.dma("sp", c_[:, 0, :], cos_ap[t0:t0 + 128, :], writes=[c_b])
            S.dma("sp", c_[:, 1, :], sin_ap[t0:t0 + 128, :], writes=[c_b])
            emit_norm_T(cx, S, xt, xb, gain, hTt, hTtb, 0, tag)
            def unit_A(which, g, pt, ptb, b):
                cb = which * 1536 + g * 512
                for k in range(8):
                    S.op("pe", lambda e, k=k: e.matmul(pt[:], hTt[:, k, :], W[:, k, cb:cb + 512],
                                                       start=(k == 0), stop=(k == 7)),
                         reads=[hTtb, Wb], writes=[ptb], inc=(k == 7))
                if which == 2:
                    S.op("act", lambda e: e.copy(out=o_[:, 2, g * 512:(g + 1) * 512], in_=pt[:]),
                         reads=[ptb], writes=[o_b])
                    return
                for hh in range(4):
                    S.op("act", lambda e: e.activation(out=jk[b][:], in_=pt[:, hh * 128:(hh + 1) * 128],
                                                       func=AF.Square, accum_out=ssq[b][:, hh:hh + 1]),
                         reads=[ptb], writes=[jkb[b], ssqb[b]])
                S.op("act", lambda e: e.activation(out=ssq[b][:, 4:8], in_=ssq[b][:, 0:4], func=AF.Sqrt,
                                                   scale=1.0 / 128, bias=EPS),
                     reads=[ssqb[b]], writes=[ssqb[b]])
                S.op("dve", lambda e: e.reciprocal(out=ssq[b][:, 4:8], in_=ssq[b][:, 4:8]),
                     reads=[ssqb[b]], writes=[ssqb[b]])
                for hh in range(4):
                    S.op("dve", lambda e: e.scalar_tensor_tensor(
                        out=qn[b][:, hh * 128:(hh + 1) * 128], in0=pt[:, hh * 128:(hh + 1) * 128],
                        scalar=ssq[b][:, 4 + hh:5 + hh], in1=gq[:, which, g, :], op0=ALU.mult, op1=ALU.mult),
                        reads=[ptb, ssqb[b], parb], writes=[qnb[b]])
                S.op("dve", lambda e: e.tensor_tensor(out=t1[b][:], in0=qn[b][:], in1=c_[:, 0, :], op=ALU.mult),
                     reads=[qnb[b], c_b], writes=[t1b[b]])

            def unit_B(which, g, b):
                qv = qn[b][:].rearrange("p (h two d) -> p h two d", h=4, two=2)
                sv = c_[:, 1, :].rearrange("p (h two d) -> p h two d", h=4, two=2)
                tv = t2[b][:].rearrange("p (h two d) -> p h two d", h=4, two=2)
                S.op("pool", lambda e: e.tensor_tensor(out=tv[:, :, 0, :], in0=qv[:, :, 1, :], in1=sv[:, :, 0, :],
                                                       op=ALU.mult),
                     reads=[qnb[b], c_b], writes=[t2b[b]])
                S.op("pool", lambda e: e.tensor_tensor(out=tv[:, :, 1, :], in0=qv[:, :, 0, :], in1=sv[:, :, 1, :],
                                                       op=ALU.mult),
                     reads=[qnb[b], c_b], writes=[t2b[b]])

            def unit_C(which, g, b):
                S.op("dve", lambda e: e.tensor_tensor(out=o_[:, which, g * 512:(g + 1) * 512], in0=t1[b][:],
                                                      in1=t2[b][:], op=ALU.add),
                     reads=[t1b[b], t2b[b]], writes=[o_b])

            units = [(w_, g_) for w_ in range(2) for g_ in range(3)]
            slots = []
            for (w_, g_) in units:
                slots.append((pss[cnt % 4], cnt % NB))
                cnt += 1
            unit_A(units[0][0], units[0][1], slots[0][0][0], slots[0][0][1], slots[0][1])
            for u, (w_, g_) in enumerate(units):
                unit_B(w_, g_, slots[u][1])
                if u + 1 < len(units):
                    nw, ng = units[u + 1]
                    unit_A(nw, ng, slots[u + 1][0][0], slots[u + 1][0][1], slots[u + 1][1])
                else:
                    pt, ptb = pss[cnt % 4]
                    cnt += 1
                    unit_A(2, 0, pt, ptb, 0)
                unit_C(w_, g_, slots[u][1])
            for g_ in (1, 2):
                pt, ptb = pss[cnt % 4]
                cnt += 1
                unit_A(2, g_, pt, ptb, 0)
            S.dma("sp", qkv[t0:t0 + 128, :, :], o_[:], reads=[o_b])
        cx.stack = old
        barrier(cx, S)


ATT_PAT = ((128, 1), (512, 4), (2048, 16))
AC_STAGE = 4


def phase_attn_core(cx, S, qkv, obt, T, tag):
    nc = cx.nc
    ident = cx.cbf[:, 0:128]
    ones = cx.cbf[:, 128:256]
    amask = cx.cbf[:, 384:640]
    SB = 2048
    scale = 128 ** -0.5
    with contextlib.ExitStack() as st:
        old = cx.stack
        cx.stack = st
        num = sb(cx, tag + "num", [128, 4, SB], F32)
        den = sb(cx, tag + "den", [128, 4, SB], F32)
        numb = [Buf() for _ in range(4)]
        denb = [Buf() for _ in range(4)]
        NB = 2
        qt = [sb(cx, tag + "q%d" % i, [128, 512], BF16) for i in range(NB)]
        kp = [sb(cx, tag + "kp%d" % i, [128, 512], BF16) for i in range(NB)]
        ko = [sb(cx, tag + "ko%d" % i, [128, 512], BF16) for i in range(NB)]
        vp = [sb(cx, tag + "vp%d" % i, [128, 512], BF16) for i in range(NB)]
        vo = [sb(cx, tag + "vo%d" % i, [128, 512], BF16) for i in range(NB)]
        qtb, kpb, kob, vpb, vob = ([Buf() for _ in range(NB)] for _ in range(5))
        qT = [sb(cx, tag + "qT%d" % i, [128, 512], BF16) for i in range(NB)]
        kpT = [sb(cx, tag + "kpT%d" % i, [128, 512], BF16) for i in range(NB)]
        koT = [sb(cx, tag + "koT%d" % i, [128, 512], BF16) for i in range(NB)]
        qTb, kpTb, koTb = ([Buf() for _ in range(NB)] for _ in range(3))
        PT = [sb(cx, tag + "PT%d" % i, [128, 256], BF16) for i in range(4)]
        PTb = [Buf() for _ in range(4)]
        obf = [sb(cx, tag + "obf%d" % i, [128, SB], BF16) for i in range(2)]
        obfb = [Buf() for _ in range(2)]
        blk = 0
        hc = 0
        tc = 0
        oc = 0
        for sbi in range(T // SB):
            for g, (_, d) in enumerate(ATT_PAT):
                nper = SB // (128 * d)
                for r in range(d):
                    for nn in range(nper):
                        n = sbi * nper + nn
                        b = blk % NB
                        blk += 1
                        ts = n * 128 * d + r
                        te = ts + 127 * d + 1
                        off = ts - sbi * SB
                        has_prev = n > 0
                        c0, c1 = g * 512, (g + 1) * 512
                        S.dma("sp", qt[b][:], qkv[ts:te:d, 0, c0:c1], writes=[qtb[b]])
                        S.dma("sp", ko[b][:], qkv[ts:te:d, 1, c0:c1], writes=[kob[b]])
                        S.dma("sp", vo[b][:], qkv[ts:te:d, 2, c0:c1], writes=[vob[b]])
                        if has_prev:
                            ps_ = ts - 128 * d
                            S.dma("sp", kp[b][:], qkv[ps_:ps_ + 127 * d + 1:d, 1, c0:c1], writes=[kpb[b]])
                            S.dma("sp", vp[b][:], qkv[ps_:ps_ + 127 * d + 1:d, 2, c0:c1], writes=[vpb[b]])
                        todo = [(qt[b], qtb[b], qT[b], qTb[b]), (ko[b], kob[b], koT[b], koTb[b])]
                        if has_prev:
                            todo.append((kp[b], kpb[b], kpT[b], kpTb[b]))
                        for (src, srcb, dst, dstb) in todo:
                            pT, pTb = cx.psT[tc % 2], cx.psTb[tc % 2]
                            tc += 1
                            for hh in range(4):
                                S.op("pe", lambda e: e.transpose(pT[:, hh * 128:(hh + 1) * 128],
                                                                 src[:, hh * 128:(hh + 1) * 128], ident),
                                     reads=[srcb, cx.constb], writes=[pTb], inc=(hh == 3))
                            S.op("act", lambda e: e.copy(out=dst[:], in_=pT[:, 0:512]),
                                 reads=[pTb], writes=[dstb])
                        for hh in range(4 if AC_STAGE >= 2 else 0):
                            hs = slice(hh * 128, (hh + 1) * 128)
                            psS, psSb = cx.psA[hc % 2], cx.psAb[hc % 2]
                            psN, psNb = cx.psB[hc % 2], cx.psBb[hc % 2]
                            psD, psDb = cx.psY[hc % 2], cx.psYb[hc % 2]
                            P, Pb = PT[hc % 4], PTb[hc % 4]
                            hc += 1
                            lo = 0 if has_prev else 128
                            if has_prev:
                                S.op("pe", lambda e: e.matmul(psS[:, 0:128], kpT[b][:, hs], qT[b][:, hs],
                                                              start=True, stop=True),
                                     reads=[kpTb[b], qTb[b]], writes=[psSb], inc=False)
                            S.op("pe", lambda e: e.matmul(psS[:, 128:256], koT[b][:, hs], qT[b][:, hs],
                                                          start=True, stop=True),
                                 reads=[koTb[b], qTb[b]], writes=[psSb])
                            S.op("act", lambda e: e.activation(out=P[:, lo:256], in_=psS[:, lo:256], func=AF.Exp,
                                                               scale=scale),
                                 reads=[psSb], writes=[Pb])
                            S.op("pool", lambda e: e.tensor_tensor(out=P[:, lo:256], in0=P[:, lo:256],
                                                                   in1=amask[:, lo:256], op=ALU.mult),
                                 reads=[Pb, cx.constb], writes=[Pb])
                            if AC_STAGE < 3:
                                continue
                            if has_prev:
                                S.op("pe", lambda e: e.matmul(psN[:, 0:128], vp[b][:, hs], P[:, 0:128],
                                                              start=True, stop=False),
                                     reads=[vpb[b], Pb], writes=[psNb], inc=False)
                            S.op("pe", lambda e: e.matmul(psN[:, 0:128], vo[b][:, hs], P[:, 128:256],
                                                          start=(not has_prev), stop=True),
                                 reads=[vob[b], Pb], writes=[psNb], inc=False)
                            if has_prev:
                                S.op("pe", lambda e: e.matmul(psD[:, 0:128], ones, P[:, 0:128],
                                                              start=True, stop=False),
                                     reads=[cx.constb, Pb], writes=[psDb], inc=False)
                            S.op("pe", lambda e: e.matmul(psD[:, 0:128], ones, P[:, 128:256],
                                                          start=(not has_prev), stop=True),
                                 reads=[cx.constb, Pb], writes=[psDb])
                            if AC_STAGE < 3.2 or (AC_STAGE in (3.25, 3.27) and g > 0):
                                S.op("pe", lambda e: e.matmul(psN[:, 256:384], ones, P[:, 128:256], start=True, stop=True),
                                     reads=[cx.constb, Pb], writes=[psNb])
                                continue
                            if AC_STAGE < 3.4 and g > 0:
                                S.op("pe", lambda e: e.matmul(psN[:, 256:384], ones, P[:, 128:256], start=True, stop=True),
                                     reads=[cx.constb, Pb], writes=[psNb])
                                continue
                            nv = num[:, hh, off:off + 127 * d + 1:d]
                            dv = den[:, hh, off:off + 127 * d + 1:d]
                            if g == 0:
                                if AC_STAGE != 3.27:
                                    S.op("act", lambda e: e.copy(out=nv, in_=psN[:, 0:128]),
                                         reads=[psNb], writes=[numb[hh]])
                                if AC_STAGE != 3.25:
                                    S.op("act", lambda e: e.copy(out=dv, in_=psD[:, 0:128]),
                                         reads=[psDb], writes=[denb[hh]])
                            else:
                                S.op("dve", lambda e: e.tensor_tensor(out=nv, in0=nv, in1=psN[:, 0:128], op=ALU.add),
                                     reads=[psNb, numb[hh]], writes=[numb[hh]])
                                S.op("dve", lambda e: e.tensor_tensor(out=dv, in0=dv, in1=psD[:, 0:128], op=ALU.add),
                                     reads=[psDb, denb[hh]], writes=[denb[hh]])
            for hh in range(4 if AC_STAGE >= 4 else 0):
                o_, o_b = obf[oc % 2], obfb[oc % 2]
                oc += 1
                S.op("dve", lambda e: e.reciprocal(out=den[:, hh, :], in_=den[:, hh, :]),
                     reads=[denb[hh]], writes=[denb[hh]])
                S.op("pool", lambda e: e.tensor_tensor(out=o_[:], in0=num[:, hh, :], in1=den[:, hh, :], op=ALU.mult),
                     reads=[numb[hh], denb[hh]], writes=[o_b])
                S.dma("sp", obt[hh * 128:(hh + 1) * 128, sbi * SB:(sbi + 1) * SB], o_[:], reads=[o_b])
        cx.stack = old
        barrier(cx, S)


def phase_out(cx, S, xin, xout, oat, obt, g_ap, w_in_ap, wa_ap, wb_ap, wo_ap, T, tag):
    nc = cx.nc
    with contextlib.ExitStack() as st:
        old = cx.stack
        cx.stack = st
        WG = sb(cx, tag + "WG", [128, 8, 2048], BF16)
        WA = sb(cx, tag + "WA", [128, 8, D], BF16)
        WB = sb(cx, tag + "WB", [128, 4, D], BF16)
        WO = sb(cx, tag + "WO", [128, 8, D], BF16)
        Wb = Buf()
        gain = sb(cx, tag + "g", [128, D], F32)
        parb = Buf()
        xts = [sb(cx, tag + "xt%d" % i, [128, D], F32) for i in range(2)]
        xtb = [Buf() for _ in range(2)]
        hT = [sb(cx, tag + "hT%d" % i, [128, 8, 128], BF16) for i in range(2)]
        hTb = [Buf() for _ in range(2)]
        oaT = [sb(cx, tag + "oaT%d" % i, [128, 8, 128], BF16) for i in range(2)]
        oaTb = [Buf() for _ in range(2)]
        obT = [sb(cx, tag + "obT%d" % i, [128, 4, 128], BF16) for i in range(2)]
        obTb = [Buf() for _ in range(2)]
        sga = [sb(cx, tag + "sga%d" % i, [128, D], F32) for i in range(2)]
        sgb = [sb(cx, tag + "sgb%d" % i, [128, D], F32) for i in range(2)]
        sgab = [Buf() for _ in range(2)]
        sgbb = [Buf() for _ in range(2)]
        m1 = [sb(cx, tag + "m1%d" % i, [128, 512], F32) for i in range(2)]
        m2 = [sb(cx, tag + "m2%d" % i, [128, 512], F32) for i in range(2)]
        m1b = [Buf() for _ in range(2)]
        m2b = [Buf() for _ in range(2)]
        mg = [sb(cx, tag + "mg%d" % i, [128, D], BF16) for i in range(2)]
        mgb = [Buf() for _ in range(2)]
        mT = [sb(cx, tag + "mT%d" % i, [128, 8, 128], BF16) for i in range(2)]
        mTb = [Buf() for _ in range(2)]
        S.dma("sp", gain[:], g_ap.partition_broadcast(128), writes=[cx.constb])
        load_weight_bf16(cx, S, WG, Wb, w_in_ap, 8, P_IN, c0=8704, c1=10752, cstep=2048)
        load_weight_bf16(cx, S, WA, Wb, wa_ap, 8, D, cstep=1024)
        load_weight_bf16(cx, S, WB, Wb, wb_ap, 4, D, cstep=1024)
        load_weight_bf16(cx, S, WO, Wb, wo_ap, 8, D, cstep=1024)
        oat_v = oat.rearrange("(c p) t -> p c t", p=128)
        obt_v = obt.rearrange("(c p) t -> p c t", p=128)
        nt = T // 128
        def stage_X(ti):
            t0 = ti * 128
            i2 = ti % 2
            xt, xb = xts[i2], xtb[i2]
            S.dma("sp", xt[:], xin[t0:t0 + 128, :], writes=[xb])
            S.dma("sp", oaT[i2][:], oat_v[:, :, t0:t0 + 128], writes=[oaTb[i2]])
            S.dma("sp", obT[i2][:], obt_v[:, :, t0:t0 + 128], writes=[obTb[i2]])
            emit_norm_T(cx, S, xt, xb, gain, hT[i2], hTb[i2], 0, tag)
            for n in range(2):
                for (pt, ptb, base, dst, dstb) in ((cx.psA[n], cx.psAb[n], 0, sga[i2], sgab[i2]),
                                                   (cx.psB[n], cx.psBb[n], 1024, sgb[i2], sgbb[i2])):
                    for k in range(8):
                        S.op("pe", lambda e, k=k: e.matmul(pt[:], hT[i2][:, k, :],
                                                           WG[:, k, base + n * 512:base + (n + 1) * 512],
                                                           start=(k == 0), stop=(k == 7)),
                             reads=[hTb[i2], Wb], writes=[ptb], inc=(k == 7))
                    S.op("act", lambda e: e.activation(out=dst[:, n * 512:(n + 1) * 512], in_=pt[:],
                                                       func=AF.Sigmoid),
                         reads=[ptb], writes=[dstb])
            for n in range(2):
                ns = slice(n * 512, (n + 1) * 512)
                pa, pab = cx.psY[0], cx.psYb[0]
                pb_, pbb = cx.psY[1], cx.psYb[1]
                for c in range(8):
                    S.op("pe", lambda e, c=c: e.matmul(pa[:], oaT[i2][:, c, :], WA[:, c, ns],
                                                       start=(c == 0), stop=(c == 7)),
                         reads=[oaTb[i2], Wb], writes=[pab], inc=(c == 7))
                for c in range(4):
                    S.op("pe", lambda e, c=c: e.matmul(pb_[:], obT[i2][:, c, :], WB[:, c, ns],
                                                       start=(c == 0), stop=(c == 3)),
                         reads=[obTb[i2], Wb], writes=[pbb], inc=(c == 3))
                S.op("dve", lambda e: e.tensor_tensor(out=m1[n][:], in0=sga[i2][:, ns], in1=pa[:], op=ALU.mult),
                     reads=[sgab[i2], pab], writes=[m1b[n]])
                S.op("dve", lambda e: e.tensor_tensor(out=m2[n][:], in0=sgb[i2][:, ns], in1=pb_[:], op=ALU.mult),
                     reads=[sgbb[i2], pbb], writes=[m2b[n]])
                S.op("dve", lambda e: e.tensor_tensor(out=mg[i2][:, ns], in0=m1[n][:], in1=m2[n][:], op=ALU.add),
                     reads=[m1b[n], m2b[n]], writes=[mgb[i2]])

        def stage_Y(ti):
            t0 = ti * 128
            i2 = ti % 2
            xt, xb = xts[i2], xtb[i2]
            pT, pTb = cx.psT[ti % 2], cx.psTb[ti % 2]
            for c in range(8):
                S.op("pe", lambda e, c=c: e.transpose(pT[:, c * 128:(c + 1) * 128],
                                                      mg[i2][:, c * 128:(c + 1) * 128], cx.cbf[:, 0:128]),
                     reads=[mgb[i2], cx.constb], writes=[pTb], inc=(c == 7))
            S.op("act", lambda e: e.copy(out=mT[i2][:], in_=pT[:].rearrange("p (k t) -> p k t", k=8)),
                 reads=[pTb], writes=[mTb[i2]])
            for n in range(2):
                ns = slice(n * 512, (n + 1) * 512)
                pt, ptb = cx.psA[n], cx.psAb[n]
                for c in range(8):
                    S.op("pe", lambda e, c=c: e.matmul(pt[:], mT[i2][:, c, :], WO[:, c, ns],
                                                       start=(c == 0), stop=(c == 7)),
                         reads=[mTb[i2], Wb], writes=[ptb], inc=(c == 7))
                S.op("dve", lambda e: e.tensor_tensor(out=xt[:, ns], in0=xt[:, ns], in1=pt[:], op=ALU.add),
                     reads=[ptb, xb], writes=[xb])
            S.dma("sp", xout[t0:t0 + 128, :], xt[:], reads=[xb])
        stage_X(0)
        for ti in range(nt):
            if ti + 1 < nt:
                stage_X(ti + 1)
            stage_Y(ti)
        cx.stack = old
        barrier(cx, S)


def make_ctx(nc, stack):
    cx = Ctx()
    cx.nc = nc
    cx.stack = stack
    cx.rr = 0
    S = Sched(nc, stack)
    cx.cbf = sb(cx, "cbf", [128, 640], BF16)
    cx.rmask = sb(cx, "rmask", [128, 128], F32)
    cx.cmask = sb(cx, "cmask", [128, 4], F32)
    cx.bmask4 = sb(cx, "bmask4", [128, 512], BF16)
    cx.rmask4 = sb(cx, "rmask4", [128, 512], F32)
    cx.ident = cx.cbf[:, 0:128]
    cx.constb = Buf()
    cx.junk = [sb(cx, "junk%d" % i, [128, D], BF16) for i in range(2)]
    cx.junkb = [Buf() for _ in range(2)]
    cx.ss = [sb(cx, "ss%d" % i, [128, 4], F32) for i in range(2)]
    cx.ssb = [Buf() for _ in range(2)]
    cx.hbf = [sb(cx, "hbf%d" % i, [128, D], BF16) for i in range(2)]
    cx.hbfb = [Buf() for _ in range(2)]
    cx.psT = [ps(cx, "psT%d" % i, [128, 1024], BF16) for i in range(2)]
    cx.psTb = [Buf() for _ in range(2)]
    cx.psA = [ps(cx, "psA%d" % i, [128, 512], F32) for i in range(2)]
    cx.psAb = [Buf() for _ in range(2)]
    cx.psB = [ps(cx, "psB%d" % i, [128, 512], F32) for i in range(2)]
    cx.psBb = [Buf() for _ in range(2)]
    cx.psY = [ps(cx, "psY%d" % i, [128, 512], F32) for i in range(2)]
    cx.psYb = [Buf() for _ in range(2)]
    return cx, S


def build_nc(T, depth=2, phases=None, debug=False):
    nc = bass.Bass("TRN2", target_bir_lowering=False)
    dt = nc.dram_tensor
    x = dt("x", [T, D], F32, kind="ExternalInput").ap()
    cst = dt("cst", [128, 1796], F32, kind="ExternalInput").ap()
    cos4 = dt("cos4", [T, 512], F32, kind="ExternalInput").ap()
    sin4 = dt("sin4", [T, 512], F32, kind="ExternalInput").ap()
    f1n = dt("ffn1_norm", [depth, D], F32, kind="ExternalInput").ap()
    f1i = dt("ffn1_w_in", [depth, D, 2 * DFF], F32, kind="ExternalInput").ap()
    f1o = dt("ffn1_w_out", [depth, DFF, D], F32, kind="ExternalInput").ap()
    mn = dt("mix_norm", [depth, D], F32, kind="ExternalInput").ap()
    win = dt("w_in", [depth, D, P_IN], F32, kind="ExternalInput").ap()
    lgT = dt("lgT", [128, depth, 8], F32, kind="ExternalInput").ap()
    gnT = dt("gnT", [depth, 128, 8], F32, kind="ExternalInput").ap()
    aqn = dt("attn_q_norm", [depth, 3, 128], F32, kind="ExternalInput").ap()
    akn = dt("attn_k_norm", [depth, 3, 128], F32, kind="ExternalInput").ap()
    wa = dt("w_branch_a", [depth, D, D], F32, kind="ExternalInput").ap()
    wb = dt("w_branch_b", [depth, 512, D], F32, kind="ExternalInput").ap()
    wo = dt("w_out", [depth, D, D], F32, kind="ExternalInput").ap()
    f2n = dt("ffn2_norm", [depth, D], F32, kind="ExternalInput").ap()
    f2i = dt("ffn2_w_in", [depth, D, 2 * DFF], F32, kind="ExternalInput").ap()
    f2o = dt("ffn2_w_out", [depth, DFF, D], F32, kind="ExternalInput").ap()
    y = dt("y", [T, D], F32, kind="ExternalOutput").ap()
    kind = "ExternalOutput" if debug else "Internal"
    xa = dt("xa", [T, D], F32, kind=kind).ap()
    xb_ = dt("xb", [T, D], F32, kind=kind).ap()
    oat = dt("oat", [D, T], BF16, kind=kind).ap()
    obt = dt("obt", [512, T], BF16, kind=kind).ap()
    qkv = dt("qkv", [T, 3, 1536], BF16, kind=kind).ap()
    with contextlib.ExitStack() as stack:
        cx, S = make_ctx(nc, stack)
        load_consts(cx, S, cst)
        cur = x
        for l in range(depth):
            last = (l == depth - 1)
            tg = "L%d" % l
            P = phases
            if P is None or "f1" in P:
                phase_ffn(cx, S, cur, xa, f1n[l:l + 1, :], f1i[l], f1o[l], T, tg + "f1")
            if P is None or "hg" in P:
                phase_hgrn(cx, S, xa, oat, mn[l:l + 1, :], win[l], lgT, gnT[l], T, l, tg + "hg")
            if P is None or "ap" in P:
                phase_attn_proj(cx, S, xa, qkv, mn[l:l + 1, :], win[l], aqn[l], akn[l], cos4, sin4, T, tg + "ap")
            if P is None or "ac" in P:
                phase_attn_core(cx, S, qkv, obt, T, tg + "ac")
            if P is None or "po" in P:
                phase_out(cx, S, xa, xb_, oat, obt, mn[l:l + 1, :], win[l], wa[l], wb[l], wo[l], T, tg + "po")
            if P is None or "f2" in P:
                phase_ffn(cx, S, xb_, y if last else xa, f2n[l:l + 1, :], f2i[l], f2o[l], T, tg + "f2")
            cur = xa
        barrier(cx, S)
        cx.ninst = S.ninst
        print("ninst", S.ninst, {k: v for k, v in S.cnt.items()})
    return nc


def rope_host(T):
    pos = np.arange(T, dtype=np.float32)
    inv = (np.float32(10000.0) ** (-np.arange(0, 128, 2, dtype=np.float32) / np.float32(128))).astype(np.float32)
    ang = pos[:, None] * inv[None, :]
    ang = np.concatenate([ang, ang], axis=-1).astype(np.float32)
    cos = np.cos(ang).astype(np.float32)
    sin = np.sin(ang).astype(np.float32)
    sin[:, :64] = -sin[:, :64]
    return np.tile(cos, (1, 4)), np.tile(sin, (1, 4))


def make_in_maps(inputs, T, depth=2):
    cos4, sin4 = rope_host(T)
    base = {k: np.ascontiguousarray(v, dtype=np.float32) for k, v in inputs.items() if k not in ("x", "hgrn_lb_logits", "hgrn_out_norm")}
    base["cst"] = host_consts()
    base["cos4"] = np.ascontiguousarray(cos4)
    base["sin4"] = np.ascontiguousarray(sin4)
    lg = np.asarray(inputs["hgrn_lb_logits"], np.float32)
    base["lgT"] = np.ascontiguousarray(lg.reshape(depth, 8, 128).transpose(2, 0, 1))
    gn = np.asarray(inputs["hgrn_out_norm"], np.float32)
    base["gnT"] = np.ascontiguousarray(gn.reshape(depth, 8, 128).transpose(0, 2, 1))
    x = np.asarray(inputs["x"], np.float32)
    maps = []
    for c in range(NCORES):
        m = dict(base)
        m["x"] = np.ascontiguousarray(x[c % x.shape[0], :T])
        maps.append(m)
    return maps


def kernel(**inputs):
    x = np.asarray(inputs["x"])
    B, T, _ = x.shape
    nc = build_nc(T, depth=2)
    maps = make_in_maps(inputs, T, depth=2)
    res = run_bass_kernel_spmd(nc, maps, core_ids=list(range(NCORES)))
    out = np.stack([np.asarray(res.results[b]["y"], dtype=np.float32) for b in range(B)], axis=0)
    return out


def build_ac_only(T):
    nc = bass.Bass("TRN2", target_bir_lowering=False)
    dt = nc.dram_tensor
    cst = dt("cst", [128, 1796], F32, kind="ExternalInput").ap()
    qkv = dt("qkv", [T, 3, 1536], BF16, kind="ExternalInput").ap()
    obt = dt("obt", [512, T], BF16, kind="ExternalOutput").ap()
    with contextlib.ExitStack() as stack:
        cx, S = make_ctx(nc, stack)
        load_consts(cx, S, cst)
        phase_attn_core(cx, S, qkv, obt, T, "ac")
        barrier(cx, S)
    return nc
```

```python
import contextlib
import numpy as np
import concourse.bass as bass
import concourse.mybir as mybir
from concourse.bass_utils import run_bass_kernel_spmd

F32 = mybir.dt.float32
BF16 = mybir.dt.bfloat16
AF = mybir.ActivationFunctionType
ALU = mybir.AluOpType
AX = mybir.AxisListType

D = 1024
DFF = 2816
P_IN = 10752
EPS = 1e-6
NCORES = 8


class Buf:
    __slots__ = ("w", "r", "name", "extra")

    def __init__(self, name=""):
        self.w = None
        self.r = {}
        self.name = name
        self.extra = []


class Sched:
    def __init__(self, nc, stack, ndma=6):
        self.nc = nc
        self.h = {"pe": nc.tensor, "act": nc.scalar, "dve": nc.vector,
                  "pool": nc.gpsimd, "sp": nc.sync}
        self.sems = {}
        self.cnt = {}
        for e in ("pe", "act", "dve", "pool"):
            self.sems[e] = stack.enter_context(nc.semaphore("s_" + e))
            self.cnt[e] = 0
        self.seen = {e: {} for e in self.h}
        self.dq = {}
        for q in ("sp", "pool", "act"):
            names = ["d_%s%d" % (q, i) for i in range(ndma)]
            for n in names:
                self.sems[n] = stack.enter_context(nc.semaphore(n))
                self.cnt[n] = 0
            self.dq[q] = [names, 0]
        self.ninst = 0

    def _wait(self, e, deps):
        need = {}
        for d in deps:
            if d is None:
                continue
            k, v, src = d
            if src == e and e == "pe":
                continue
            if self.seen[e].get(k, 0) >= v:
                continue
            if need.get(k, 0) < v:
                need[k] = v
        for k, v in need.items():
            self.h[e].wait_ge(self.sems[k], v)
            self.seen[e][k] = v
            self.ninst += 1

    def _deps(self, e, reads, writes):
        deps = []
        for b in reads:
            deps.append(b.w)
            if b.extra:
                deps.extend(b.extra)
        for b in writes:
            deps.append(b.w)
            for r in b.r.values():
                if r[2] == e and e != "pool":
                    continue
                deps.append(r)
        return deps

    def _record(self, ev, reads, writes):
        for b in reads:
            old = b.r.get(ev[0])
            if old is None or old[1] < ev[1]:
                b.r[ev[0]] = ev
        for b in writes:
            b.w = ev
            b.r = {}

    def op(self, e, fn, reads=(), writes=(), inc=True):
        self._wait(e, self._deps(e, reads, writes))
        ins = fn(self.h[e])
        if inc:
            self.cnt[e] += 1
            ins.then_inc(self.sems[e], 1)
            ev = (e, self.cnt[e], e)
        else:
            ev = (e, self.cnt[e] + 1, e)
        self.ninst += 1
        self._record(ev, reads, writes)
        return ev

    def dma(self, q, out, in_, reads=(), writes=(), **kw):
        names, idx = self.dq[q]
        k = names[idx % len(names)]
        self.dq[q][1] = idx + 1
        deps = self._deps("dma_issue", reads, writes)
        if self.cnt[k] > 0:
            deps.append((k, self.cnt[k], "dma"))
        self._wait(q, deps)
        ins = self.h[q].dma_start(out=out, in_=in_, **kw)
        self.cnt[k] += 16
        ins.then_inc(self.sems[k], 16)
        ev = (k, self.cnt[k], "dma")
        self.ninst += 1
        self._record(ev, reads, writes)
        return ev

    def finish(self, e="sp"):
        for k, c in self.cnt.items():
            if c > 0 and self.seen[e].get(k, 0) < c:
                self.h[e].wait_ge(self.sems[k], c)
                self.seen[e][k] = c


class Ctx:
    pass


def sb(cx, name, shape, dt):
    t = cx.stack.enter_context(cx.nc.sbuf_tensor("sb_" + name, list(shape), dt))
    return t


def ps(cx, name, shape, dt):
    t = cx.stack.enter_context(cx.nc.psum_tensor("ps_" + name, list(shape), dt))
    return t


def emit_norm_T(cx, S, xt, xb, gain, hT, hTb, col0, tag, fixed=None):
    nc = cx.nc
    i = cx.rr % 2 if fixed is None else fixed
    cx.rr += 1
    junk, junkb = cx.junk[i], cx.junkb[i]
    ss, ssb = cx.ss[i], cx.ssb[i]
    hb, hbb = cx.hbf[i], cx.hbfb[i]
    pT, pTb = cx.psT[i], cx.psTb[i]
    S.op("act", lambda e: e.activation(out=junk[:], in_=xt[:], func=AF.Square,
                                       accum_out=ss[:, 0:1]),
         reads=[xb], writes=[junkb, ssb])
    S.op("act", lambda e: e.activation(out=ss[:, 1:2], in_=ss[:, 0:1], func=AF.Sqrt,
                                       scale=1.0 / D, bias=EPS),
         reads=[ssb], writes=[ssb])
    S.op("dve", lambda e: e.reciprocal(out=ss[:, 2:3], in_=ss[:, 1:2]),
         reads=[ssb], writes=[ssb])
    S.op("dve", lambda e: e.scalar_tensor_tensor(out=hb[:], in0=xt[:], scalar=ss[:, 2:3],
                                                 in1=gain[:], op0=ALU.mult, op1=ALU.mult),
         reads=[xb, ssb, cx.constb], writes=[hbb])
    for k in range(8):
        S.op("pe", lambda e, k=k: e.transpose(pT[:, k * 128:(k + 1) * 128],
                                              hb[:, k * 128:(k + 1) * 128], cx.ident[:]),
             reads=[hbb, cx.constb], writes=[pTb], inc=(k == 7))
    S.op("act", lambda e: e.copy(out=hT[:, :, col0:col0 + 128],
                                 in_=pT[:].rearrange("p (k t) -> p k t", k=8)),
         reads=[pTb], writes=[hTb])


def load_weight_bf16(cx, S, dst, dstb, src_ap, nk, ncols, c0=0, c1=None, cstep=1408):
    c1 = ncols if c1 is None else c1
    v = src_ap.rearrange("(k p) c -> p k c", p=128)
    c = c0
    while c < c1:
        ce = min(c1, c + cstep)
        for k in range(nk):
            ev = S.dma("pool", dst[:, k, c - c0:ce - c0], v[:, k, c:ce])
            dstb.extra.append(ev)
        c = ce


def phase_ffn(cx, S, xin, xout, g_ap, w_in_ap, w_out_ap, T, tag):
    nc = cx.nc
    G = 256
    with contextlib.ExitStack() as st:
        old = cx.stack
        cx.stack = st
        w1 = sb(cx, tag + "w1", [128, 8, 2 * DFF], BF16)
        w2 = sb(cx, tag + "w2", [128, 22, D], BF16)
        gain = sb(cx, tag + "g", [128, D], F32)
        xts = [sb(cx, tag + "xt%d" % i, [128, D], F32) for i in range(4)]
        hT = [sb(cx, tag + "hT%d" % i, [128, 8, G], BF16) for i in range(2)]
        uT = sb(cx, tag + "uT", [128, 22, G], BF16)
        sa = [sb(cx, tag + "sa%d" % i, [128, G], F32) for i in range(2)]
        w1b, w2b, uTb = Buf(), Buf(), Buf()
        xtb = [Buf() for _ in range(4)]
        hTb = [Buf() for _ in range(2)]
        sab = [Buf() for _ in range(2)]
        S.dma("sp", gain[:], g_ap.partition_broadcast(128), writes=[cx.constb])
        load_weight_bf16(cx, S, w1, w1b, w_in_ap, 8, 2 * DFF)
        load_weight_bf16(cx, S, w2, w2b, w_out_ap, 22, D, cstep=1024)
        ng = T // G
        for gi in range(ng):
            hTg, hTgb = hT[gi % 2], hTb[gi % 2]
            for i in range(2):
                xt, xb = xts[(gi % 2) * 2 + i], xtb[(gi % 2) * 2 + i]
                r0 = gi * G + i * 128
                S.dma("sp", xt[:], xin[r0:r0 + 128, :], writes=[xb])
                emit_norm_T(cx, S, xt, xb, gain, hTg, hTgb, i * 128, tag)
            for j in range(22):
                pA, pAb = cx.psA[j % 2], cx.psAb[j % 2]
                pB, pBb = cx.psB[j % 2], cx.psBb[j % 2]
                for k in range(8):
                    S.op("pe", lambda e, k=k: e.matmul(pA[:, 0:G], w1[:, k, j * 128:(j + 1) * 128],
                                                       hTg[:, k, :], start=(k == 0), stop=(k == 7)),
                         reads=[w1b, hTgb], writes=[pAb], inc=(k == 7))
                for k in range(8):
                    S.op("pe", lambda e, k=k: e.matmul(pB[:, 0:G],
                                                       w1[:, k, DFF + j * 128:DFF + (j + 1) * 128],
                                                       hTg[:, k, :], start=(k == 0), stop=(k == 7)),
                         reads=[w1b, hTgb], writes=[pBb], inc=(k == 7))
                s_, s_b = sa[j % 2], sab[j % 2]
                S.op("act", lambda e: e.activation(out=s_[:], in_=pA[:, 0:G], func=AF.Silu),
                     reads=[pAb], writes=[s_b])
                S.op("dve", lambda e: e.tensor_tensor(out=uT[:, j, :], in0=s_[:], in1=pB[:, 0:G],
                                                      op=ALU.mult),
                     reads=[s_b, pBb], writes=[uTb])
            for i in range(2):
                xt, xb = xts[(gi % 2) * 2 + i], xtb[(gi % 2) * 2 + i]
                r0 = gi * G + i * 128
                for n in range(2):
                    pY, pYb = cx.psY[n], cx.psYb[n]
                    for j in range(22):
                        S.op("pe", lambda e, j=j: e.matmul(pY[:], uT[:, j, i * 128:(i + 1) * 128],
                                                           w2[:, j, n * 512:(n + 1) * 512],
                                                           start=(j == 0), stop=(j == 21)),
                             reads=[uTb, w2b], writes=[pYb], inc=(j == 21))
                    S.op("dve", lambda e: e.scalar_tensor_tensor(
                        out=xt[:, n * 512:(n + 1) * 512], in0=pY[:], scalar=0.5,
                        in1=xt[:, n * 512:(n + 1) * 512], op0=ALU.mult, op1=ALU.add),
                        reads=[pYb, xb], writes=[xb])
                S.dma("sp", xout[r0:r0 + 128, :], xt[:], reads=[xb])
        cx.stack = old
        S.finish("sp")
        barrier(cx, S)


def barrier(cx, S):
    for e in ("pe", "act", "dve", "pool", "sp"):
        S.finish(e)


def make_ctx(nc, stack):
    cx = Ctx()
    cx.nc = nc
    cx.stack = stack
    cx.rr = 0
    S = Sched(nc, stack)
    cx.ident = sb(cx, "ident", [128, 128], BF16)
    cx.constb = Buf()
    cx.junk = [sb(cx, "junk%d" % i, [128, D], BF16) for i in range(2)]
    cx.junkb = [Buf() for _ in range(2)]
    cx.ss = [sb(cx, "ss%d" % i, [128, 4], F32) for i in range(2)]
    cx.ssb = [Buf() for _ in range(2)]
    cx.hbf = [sb(cx, "hbf%d" % i, [128, D], BF16) for i in range(2)]
    cx.hbfb = [Buf() for _ in range(2)]
    cx.psT = [ps(cx, "psT%d" % i, [128, 1024], BF16) for i in range(2)]
    cx.psTb = [Buf() for _ in range(2)]
    cx.psA = [ps(cx, "psA%d" % i, [128, 512], F32) for i in range(2)]
    cx.psAb = [Buf() for _ in range(2)]
    cx.psB = [ps(cx, "psB%d" % i, [128, 512], F32) for i in range(2)]
    cx.psBb = [Buf() for _ in range(2)]
    cx.psY = [ps(cx, "psY%d" % i, [128, 512], F32) for i in range(2)]
    cx.psYb = [Buf() for _ in range(2)]
    return cx, S


def load_consts(cx, S, ident_ap):
    S.dma("pool", cx.ident[:], ident_ap, writes=[cx.constb])


def load_consts(cx, S, cst_ap):
    S.dma("pool", cx.cbf[:], cst_ap[:, 0:640], writes=[cx.constb])
    S.dma("sp", cx.rmask[:], cst_ap[:, 640:768], writes=[cx.constb])
    S.dma("sp", cx.cmask[:], cst_ap[:, 768:772], writes=[cx.constb])
    S.dma("pool", cx.bmask4[:], cst_ap[:, 772:1284], writes=[cx.constb])
    S.dma("sp", cx.rmask4[:], cst_ap[:, 1284:1796], writes=[cx.constb])


def host_consts():
    c = np.zeros((128, 1796), np.float32)
    i = np.arange(128)
    c[:, 0:128] = np.eye(128)
    c[:, 128:256] = 1.0
    s_, t_ = i[:, None], i[None, :]
    c[:, 256:384] = ((s_ // 32 == t_ // 32) & (s_ <= t_))
    c[:, 384:512] = (s_ >= t_)
    c[:, 512:640] = (s_ <= t_)
    c[:, 640:768] = (t_ % 32 != 0) * np.ones((128, 1))
    for j in range(4):
        c[:, 768 + j] = (i // 32 == j)
        c[:, 772 + j * 128:772 + (j + 1) * 128] = c[:, 256:384]
        c[:, 1284 + j * 128:1284 + (j + 1) * 128] = c[:, 640:768]
    return c


def phase_hgrn(cx, S, xin, oat, g_ap, w_in_ap, lgT_ap, gnT_ap, T, layer, tag):
    nc = cx.nc
    ident = cx.cbf[:, 0:128]
    ones = cx.cbf[:, 128:256]
    with contextlib.ExitStack() as st:
        old = cx.stack
        cx.stack = st
        W = sb(cx, tag + "W", [128, 8, 4096], BF16)
        Wb = Buf()
        gain = sb(cx, tag + "g", [128, D], F32)
        lg = sb(cx, tag + "lg", [128, lgT_ap.shape[1], 8], F32)
        lb = sb(cx, tag + "lb", [128, 8], F32)
        oml = sb(cx, tag + "oml", [128, 8], F32)
        gn = sb(cx, tag + "gn", [128, 8], F32)
        zer = sb(cx, tag + "zer", [128, 128], F32)
        lbf = sb(cx, tag + "lbf", [128, 8, 128], F32)
        omlf = sb(cx, tag + "omlf", [128, 8, 128], F32)
        gnf = sb(cx, tag + "gnf", [128, 8, 128], F32)
        parb = Buf()
        xts = [sb(cx, tag + "xt%d" % i, [128, D], F32) for i in range(2)]
        xtb = [Buf() for _ in range(2)]
        hT = [sb(cx, tag + "hT%d" % i, [128, 8, 128], BF16) for i in range(2)]
        hTb = [Buf() for _ in range(2)]
        v = [sb(cx, tag + "v%d" % i, [128, D], BF16) for i in range(2)]
        vb = [Buf() for _ in range(2)]
        S32 = [sb(cx, tag + "S32_%d" % i, [128, 512], F32) for i in range(2)]
        S32b = [Buf() for _ in range(2)]
        Sbf = [sb(cx, tag + "Sbf_%d" % i, [128, 8, 512], BF16) for i in range(2)]
        Sbfb = [[Buf() for _ in range(8)] for _ in range(2)]
        NB = 2

        def mk(name, dt, w=512, n=NB):
            return ([sb(cx, tag + name + str(i), [128, w], dt) for i in range(n)],
                    [Buf() for _ in range(n)])
        q32, q32b = mk("q", F32)
        f32t, f32b = mk("f", F32)
        gt, gtb = mk("gt", F32, n=4)
        lf, lfb = mk("lf", F32)
        kk, kkb = mk("kk", F32)
        gc, gcb = mk("gc", F32)
        eq, eqb = mk("eq", F32, n=4)
        ek, ekb = mk("ek", F32)
        qg, qgb = mk("qg", BF16, n=4)
        kg, kgb = mk("kg", BF16)
        kgT, kgTb = mk("kgT", BF16, w=2048)
        AT, ATb = mk("AT", BF16, n=4)
        sq, sqb = mk("sq", BF16)
        rs, rsb = mk("rs", F32)
        on, onb = mk("on", F32)
        tmp, tmpb = mk("tmp", F32)
        oa, oab = mk("oa", BF16, n=4)

        S.dma("sp", gain[:], g_ap.partition_broadcast(128), writes=[cx.constb])
        S.dma("sp", lg[:], lgT_ap, writes=[parb])
        S.dma("sp", gn[:], gnT_ap, writes=[parb])
        if layer == 0:
            S.op("dve", lambda e: e.memset(lb[:], 0.0), writes=[parb])
        else:
            S.op("dve", lambda e: e.tensor_tensor(out=lb[:], in0=lg[:, 1, :], in1=lg[:, 0, :],
                                                  op=ALU.subtract), reads=[parb], writes=[parb])
            S.op("act", lambda e: e.activation(out=lb[:], in_=lb[:], func=AF.Sigmoid),
                 reads=[parb], writes=[parb])
        S.op("dve", lambda e: e.tensor_scalar(out=oml[:], in0=lb[:], scalar1=-1.0, scalar2=1.0,
                                              op0=ALU.mult, op1=ALU.add), reads=[parb], writes=[parb])
        S.op("dve", lambda e: e.memset(zer[:], 0.0), writes=[parb])
        for hd in range(8):
            for (dst, src) in ((lbf, lb), (omlf, oml), (gnf, gn)):
                S.op("dve", lambda e: e.tensor_scalar(out=dst[:, hd, :], in0=zer[:], scalar1=src[:, hd:hd + 1],
                                                      scalar2=None, op0=ALU.add),
                     reads=[parb], writes=[parb])
        for gi in range(2):
            S.op("dve", lambda e: e.memset(S32[gi][:], 0.0), writes=[S32b[gi]])
            S.op("pool", lambda e: e.memset(Sbf[gi][:], 0.0), writes=Sbfb[gi])
        load_weight_bf16(cx, S, W, Wb, w_in_ap, 8, P_IN, c0=0, c1=4096, cstep=2048)

        pQ, pQb = cx.psA[0], cx.psAb[0]
        pF, pFb = cx.psA[1], cx.psAb[1]
        pG, pGb = cx.psB[0], cx.psBb[0]
        pAT, pATb = cx.psB[1], cx.psBb[1]
        pS, pSb = cx.psY[0], cx.psYb[0]
        pO, pOb = cx.psY[1], cx.psYb[1]
        pK, pKb = cx.psT[1], cx.psTb[1]
        oat_v = oat.rearrange("(h p) t -> p h t", p=128)
        nt = T // 128
        for ti in range(nt):
            t0 = ti * 128
            xt, xb = xts[ti % 2], xtb[ti % 2]
            hTt, hTtb = hT[ti % 2], hTb[ti % 2]
            vt, vtb = v[ti % 2], vb[ti % 2]
            S.dma("sp", xt[:], xin[t0:t0 + 128, :], writes=[xb])
            emit_norm_T(cx, S, xt, xb, gain, hTt, hTtb, 0, tag, fixed=0)
            for half in range(2):
                for k in range(8):
                    S.op("pe", lambda e, k=k: e.matmul(pS[:], hTt[:, k, :],
                                                       W[:, k, 2048 + half * 512:2048 + (half + 1) * 512],
                                                       start=(k == 0), stop=(k == 7)),
                         reads=[hTtb, Wb], writes=[pSb], inc=(k == 7))
                S.op("act", lambda e: e.copy(out=vt[:, half * 512:(half + 1) * 512], in_=pS[:]),
                     reads=[pSb], writes=[vtb])
            for gi in range(2):
                b = gi
                b4 = (2 * ti + gi) % 4
                c0 = gi * 512
                hsl = slice(gi * 4, gi * 4 + 4)
                for (pt, ptb, base) in ((pQ, pQb, 0), (pF, pFb, 1024), (pG, pGb, 3072)):
                    for hh in range(4):
                        for k in range(8):
                            S.op("pe", lambda e, k=k: e.matmul(
                                pt[:, hh * 128:(hh + 1) * 128],
                                W[:, k, base + c0 + hh * 128:base + c0 + (hh + 1) * 128],
                                hTt[:, k, :], start=(k == 0), stop=(k == 7)),
                                reads=[hTtb, Wb], writes=[ptb], inc=(k == 7 and hh == 3))
                S.op("act", lambda e: e.activation(out=q32[b][:], in_=pQ[:], func=AF.Silu),
                     reads=[pQb], writes=[q32b[b]])
                S.op("act", lambda e: e.activation(out=gt[b4][:], in_=pG[:], func=AF.Silu),
                     reads=[pGb], writes=[gtb[b4]])
                S.op("act", lambda e: e.activation(out=f32t[b][:], in_=pF[:], func=AF.Sigmoid),
                     reads=[pFb], writes=[f32b[b]])
                S.op("pool", lambda e: e.tensor_tensor(out=gt[b4][:], in0=gt[b4][:],
                                                       in1=gnf[:, hsl, :].rearrange("p h t -> p (h t)"), op=ALU.mult),
                     reads=[gtb[b4], parb], writes=[gtb[b4]])
                if layer != 0:
                    S.op("dve", lambda e: e.tensor_tensor(out=f32t[b][:], in0=f32t[b][:],
                                                          in1=omlf[:, hsl, :].rearrange("p h t -> p (h t)"),
                                                          op=ALU.mult),
                         reads=[f32b[b], parb], writes=[f32b[b]])
                    S.op("dve", lambda e: e.tensor_tensor(out=f32t[b][:], in0=f32t[b][:],
                                                          in1=lbf[:, hsl, :].rearrange("p h t -> p (h t)"),
                                                          op=ALU.add),
                         reads=[f32b[b], parb], writes=[f32b[b]])
                S.op("act", lambda e: e.activation(out=lf[b][:], in_=f32t[b][:], func=AF.Ln),
                     reads=[f32b[b]], writes=[lfb[b]])
                S.op("act", lambda e: e.activation(out=kk[b][:], in_=f32t[b][:], func=AF.Identity,
                                                   scale=-1.0, bias=1.0),
                     reads=[f32b[b]], writes=[kkb[b]])
                S.op("dve", lambda e: e.tensor_tensor_scan(out=gc[b][:], data0=cx.rmask4[:], data1=lf[b][:],
                                                           initial=0.0, op0=ALU.mult, op1=ALU.add),
                     reads=[lfb[b], cx.constb], writes=[gcb[b]])
                S.op("act", lambda e: e.activation(out=eq[b4][:], in_=gc[b][:], func=AF.Exp),
                     reads=[gcb[b]], writes=[eqb[b4]])
                S.op("act", lambda e: e.activation(out=ek[b][:], in_=gc[b][:], func=AF.Exp, scale=-1.0),
                     reads=[gcb[b]], writes=[ekb[b]])
                S.op("dve", lambda e: e.tensor_tensor(out=kg[b][:], in0=kk[b][:], in1=ek[b][:], op=ALU.mult),
                     reads=[kkb[b], ekb[b]], writes=[kgb[b]])
                S.op("dve", lambda e: e.tensor_tensor(out=qg[b4][:], in0=q32[b][:], in1=eq[b4][:], op=ALU.mult),
                     reads=[q32b[b], eqb[b4]], writes=[qgb[b4]])
            for gi in range(2):
                b = gi
                b4 = (2 * ti + gi) % 4
                c0 = gi * 512
                hsl = slice(gi * 4, gi * 4 + 4)
                for hh in range(4):
                    S.op("pe", lambda e: e.transpose(pK[:, hh * 128:(hh + 1) * 128],
                                                     kg[b][:, hh * 128:(hh + 1) * 128], ident),
                         reads=[kgb[b], cx.constb], writes=[pKb], inc=(hh == 3))
                for j in range(4):
                    S.op("act", lambda e: e.activation(out=kgT[b][:, j * 512:(j + 1) * 512], in_=pK[:, 0:512],
                                                       func=AF.Copy, scale=cx.cmask[:, j:j + 1]),
                         reads=[pKb, cx.constb], writes=[kgTb[b]])
                for hh in range(4):
                    hs = slice(hh * 128, (hh + 1) * 128)
                    S.op("pe", lambda e: e.matmul(pAT[:, hs], kg[b][:, hs], qg[b4][:, hs], start=True, stop=True),
                         reads=[kgb[b], qgb[b4]], writes=[pATb], inc=(hh == 3))
                S.op("dve", lambda e: e.tensor_tensor(out=AT[b4][:], in0=pAT[:], in1=cx.bmask4[:], op=ALU.mult),
                     reads=[pATb, cx.constb], writes=[ATb[b4]])
                eqv = eq[b4][:].rearrange("p (h t) -> p h t", h=4)
                for j in range(4):
                    for hh in range(4):
                        hs = slice(hh * 128, (hh + 1) * 128)
                        S.op("pe", lambda e: e.matmul(pS[:, hs], kgT[b][:, j * 512 + hh * 128:j * 512 + (hh + 1) * 128],
                                                      vt[:, c0 + hh * 128:c0 + (hh + 1) * 128],
                                                      start=True, stop=True),
                             reads=[kgTb[b], vtb], writes=[pSb], inc=(hh == 3))
                    eglb = eqv[:, :, 32 * j + 31:32 * j + 32].to_broadcast([128, 4, 128])
                    S.op("dve", lambda e: e.tensor_tensor(out=tmp[b][:], in0=S32[gi][:], in1=pS[:], op=ALU.add),
                         reads=[S32b[gi], pSb], writes=[tmpb[b]])
                    S.op("dve", lambda e: e.tensor_tensor(out=S32[gi][:].rearrange("p (h t) -> p h t", h=4),
                                                          in0=tmp[b][:].rearrange("p (h t) -> p h t", h=4),
                                                          in1=eglb, op=ALU.mult),
                         reads=[tmpb[b], eqb[b4]], writes=[S32b[gi]])
                    sl = ((ti % 2) * 4 + j + 1) % 8
                    S.op("act", lambda e: e.copy(out=Sbf[gi][:, sl, :], in_=S32[gi][:]),
                         reads=[S32b[gi]], writes=[Sbfb[gi][sl]])
            for gi in range(2):
                b = gi
                b4 = (2 * ti + gi) % 4
                c0 = gi * 512
                hsl = slice(gi * 4, gi * 4 + 4)
                for hh in range(4):
                    hs = slice(hh * 128, (hh + 1) * 128)
                    S.op("pe", lambda e: e.matmul(pO[:, hs], vt[:, c0 + hh * 128:c0 + (hh + 1) * 128], AT[b4][:, hs],
                                                  start=True, stop=False),
                         reads=[vtb, ATb[b4]], writes=[pOb], inc=False)
                    for j in range(4):
                        sl = ((ti % 2) * 4 + j) % 8
                        S.op("pe", lambda e: e.matmul(pO[:, hh * 128 + 32 * j:hh * 128 + 32 * j + 32],
                                                      Sbf[gi][:, sl, hs], qg[b4][:, hh * 128 + 32 * j:hh * 128 + 32 * j + 32],
                                                      start=False, stop=(j == 3)),
                             reads=[Sbfb[gi][sl], qgb[b4]], writes=[pOb], inc=(j == 3 and hh == 3))
                S.op("act", lambda e: e.activation(out=sq[b][:], in_=pO[:], func=AF.Square),
                     reads=[pOb], writes=[sqb[b]])
                S.op("pe", lambda e: e.matmul(pAT[:], ones, sq[b][:], start=True, stop=True),
                     reads=[sqb[b], cx.constb], writes=[pATb])
                S.op("act", lambda e: e.activation(out=rs[b][:], in_=pAT[:], func=AF.Sqrt,
                                                   scale=1.0 / 128, bias=EPS),
                     reads=[pATb], writes=[rsb[b]])
                S.op("dve", lambda e: e.reciprocal(out=rs[b][:], in_=rs[b][:]),
                     reads=[rsb[b]], writes=[rsb[b]])
                S.op("dve", lambda e: e.tensor_tensor(out=on[b][:], in0=pO[:], in1=rs[b][:], op=ALU.mult),
                     reads=[pOb, rsb[b]], writes=[onb[b]])
                ob = (2 * ti + gi) % 4
                S.op("dve", lambda e: e.tensor_tensor(out=oa[ob][:], in0=on[b][:], in1=gt[b4][:], op=ALU.mult),
                     reads=[onb[b], gtb[b4]], writes=[oab[ob]])
                S.dma("sp", oat_v[:, gi * 4:gi * 4 + 4, t0:t0 + 128],
                      oa[ob][:].rearrange("p (h t) -> p h t", h=4), reads=[oab[ob]])
        cx.stack = old
        barrier(cx, S)


def phase_attn_proj(cx, S, xin, qkv, g_ap, w_in_ap, qn_ap, kn_ap, cos_ap, sin_ap, T, tag):
    nc = cx.nc
    with contextlib.ExitStack() as st:
        old = cx.stack
        cx.stack = st
        W = sb(cx, tag + "W", [128, 8, 4608], BF16)
        Wb = Buf()
        gain = sb(cx, tag + "g", [128, D], F32)
        gq = sb(cx, tag + "gq", [128, 2, 3, 128], F32)
        parb = Buf()
        xts = [sb(cx, tag + "xt%d" % i, [128, D], F32) for i in range(2)]
        xtb = [Buf() for _ in range(2)]
        hT = [sb(cx, tag + "hT%d" % i, [128, 8, 128], BF16) for i in range(2)]
        hTb = [Buf() for _ in range(2)]
        cs = [sb(cx, tag + "cs%d" % i, [128, 2, 512], F32) for i in range(2)]
        csb = [Buf() for _ in range(2)]
        ob = [sb(cx, tag + "ob%d" % i, [128, 3, 1536], BF16) for i in range(2)]
        obb = [Buf() for _ in range(2)]
        NB = 2
        ssq = [sb(cx, tag + "ssq%d" % i, [128, 8], F32) for i in range(NB)]
        ssqb = [Buf() for _ in range(NB)]
        jk = [sb(cx, tag + "jk%d" % i, [128, 128], BF16) for i in range(NB)]
        jkb = [Buf() for _ in range(NB)]
        qn = [sb(cx, tag + "qn%d" % i, [128, 512], F32) for i in range(NB)]
        qnb = [Buf() for _ in range(NB)]
        t1 = [sb(cx, tag + "t1%d" % i, [128, 512], F32) for i in range(NB)]
        t1b = [Buf() for _ in range(NB)]
        t2 = [sb(cx, tag + "t2%d" % i, [128, 512], F32) for i in range(NB)]
        t2b = [Buf() for _ in range(NB)]
        S.dma("sp", gain[:], g_ap.partition_broadcast(128), writes=[cx.constb])
        for g in range(3):
            S.dma("sp", gq[:, 0, g, :], qn_ap[g:g + 1, :].partition_broadcast(128), writes=[parb])
            S.dma("sp", gq[:, 1, g, :], kn_ap[g:g + 1, :].partition_broadcast(128), writes=[parb])
        load_weight_bf16(cx, S, W, Wb, w_in_ap, 8, P_IN, c0=4096, c1=8704, cstep=1536)
        pss = [(cx.psA[0], cx.psAb[0]), (cx.psA[1], cx.psAb[1]), (cx.psB[0], cx.psBb[0]), (cx.psB[1], cx.psBb[1])]
        nt = T // 128
        cnt = 0
        for ti in range(nt):
            t0 = ti * 128
            xt, xb = xts[ti % 2], xtb[ti % 2]
            hTt, hTtb = hT[ti % 2], hTb[ti % 2]
            c_, c_b = cs[ti % 2], csb[ti % 2]
            o_, o_b = ob[ti % 2], obb[ti % 2]
            S.dma("sp", xt[:], xin[t0:t0 + 128, :], writes=[xb])
            S.dma("sp", c_[:, 0, :], cos_ap[t0:t0 + 128, :], writes=[c_b])
            S.dma("sp", c_[:, 1, :], sin_ap[t0:t0 + 128, :], writes=[c_b])
            emit_norm_T(cx, S, xt, xb, gain, hTt, hTtb, 0, tag)
            def unit_A(which, g, pt, ptb, b):
                cb = which * 1536 + g * 512
                for k in range(8):
                    S.op("pe", lambda e, k=k: e.matmul(pt[:], hTt[:, k, :], W[:, k, cb:cb + 512],
                                                       start=(k == 0), stop=(k == 7)),
                         reads=[hTtb, Wb], writes=[ptb], inc=(k == 7))
                if which == 2:
                    S.op("act", lambda e: e.copy(out=o_[:, 2, g * 512:(g + 1) * 512], in_=pt[:]),
                         reads=[ptb], writes=[o_b])
                    return
                for hh in range(4):
                    S.op("act", lambda e: e.activation(out=jk[b][:], in_=pt[:, hh * 128:(hh + 1) * 128],
                                                       func=AF.Square, accum_out=ssq[b][:, hh:hh + 1]),
                         reads=[ptb], writes=[jkb[b], ssqb[b]])
                S.op("act", lambda e: e.activation(out=ssq[b][:, 4:8], in_=ssq[b][:, 0:4], func=AF.Sqrt,
                                                   scale=1.0 / 128, bias=EPS),
                     reads=[ssqb[b]], writes=[ssqb[b]])
                S.op("dve", lambda e: e.reciprocal(out=ssq[b][:, 4:8], in_=ssq[b][:, 4:8]),
                     reads=[ssqb[b]], writes=[ssqb[b]])
                for hh in range(4):
                    S.op("dve", lambda e: e.scalar_tensor_tensor(
                        out=qn[b][:, hh * 128:(hh + 1) * 128], in0=pt[:, hh * 128:(hh + 1) * 128],
                        scalar=ssq[b][:, 4 + hh:5 + hh], in1=gq[:, which, g, :], op0=ALU.mult, op1=ALU.mult),
                        reads=[ptb, ssqb[b], parb], writes=[qnb[b]])
                S.op("dve", lambda e: e.tensor_tensor(out=t1[b][:], in0=qn[b][:], in1=c_[:, 0, :], op=ALU.mult),
                     reads=[qnb[b], c_b], writes=[t1b[b]])

            def unit_B(which, g, b):
                qv = qn[b][:].rearrange("p (h two d) -> p h two d", h=4, two=2)
                sv = c_[:, 1, :].rearrange("p (h two d) -> p h two d", h=4, two=2)
                tv = t2[b][:].rearrange("p (h two d) -> p h two d", h=4, two=2)
                S.op("pool", lambda e: e.tensor_tensor(out=tv[:, :, 0, :], in0=qv[:, :, 1, :], in1=sv[:, :, 0, :],
                                                       op=ALU.mult),
                     reads=[qnb[b], c_b], writes=[t2b[b]])
                S.op("pool", lambda e: e.tensor_tensor(out=tv[:, :, 1, :], in0=qv[:, :, 0, :], in1=sv[:, :, 1, :],
                                                       op=ALU.mult),
                     reads=[qnb[b], c_b], writes=[t2b[b]])

            def unit_C(which, g, b):
                S.op("dve", lambda e: e.tensor_tensor(out=o_[:, which, g * 512:(g + 1) * 512], in0=t1[b][:],
                                                      in1=t2[b][:], op=ALU.add),
                     reads=[t1b[b], t2b[b]], writes=[o_b])

            units = [(w_, g_) for w_ in range(2) for g_ in range(3)]
            slots = []
            for (w_, g_) in units:
                slots.append((pss[cnt % 4], cnt % NB))
                cnt += 1
            unit_A(units[0][0], units[0][1], slots[0][0][0], slots[0][0][1], slots[0][1])
            for u, (w_, g_) in enumerate(units):
                unit_B(w_, g_, slots[u][1])
                if u + 1 < len(units):
                    nw, ng = units[u + 1]
                    unit_A(nw, ng, slots[u + 1][0][0], slots[u + 1][0][1], slots[u + 1][1])
                else:
                    pt, ptb = pss[cnt % 4]
                    cnt += 1
                    unit_A(2, 0, pt, ptb, 0)
                unit_C(w_, g_, slots[u][1])
            for g_ in (1, 2):
                pt, ptb = pss[cnt % 4]
                cnt += 1
                unit_A(2, g_, pt, ptb, 0)
            S.dma("sp", qkv[t0:t0 + 128, :, :], o_[:], reads=[o_b])
        cx.stack = old
        barrier(cx, S)


ATT_PAT = ((128, 1), (512, 4), (2048, 16))
AC_STAGE = 4


def phase_attn_core(cx, S, qkv, obt, T, tag):
    nc = cx.nc
    ident = cx.cbf[:, 0:128]
    ones = cx.cbf[:, 128:256]
    amask = cx.cbf[:, 384:640]
    SB = 2048
    scale = 128 ** -0.5
    with contextlib.ExitStack() as st:
        old = cx.stack
        cx.stack = st
        num = sb(cx, tag + "num", [128, 4, SB], F32)
        den = sb(cx, tag + "den", [128, 4, SB], F32)
        numb = [Buf() for _ in range(4)]
        denb = [Buf() for _ in range(4)]
        NB = 2
        qt = [sb(cx, tag + "q%d" % i, [128, 512], BF16) for i in range(NB)]
        kp = [sb(cx, tag + "kp%d" % i, [128, 512], BF16) for i in range(NB)]
        ko = [sb(cx, tag + "ko%d" % i, [128, 512], BF16) for i in range(NB)]
        vp = [sb(cx, tag + "vp%d" % i, [128, 512], BF16) for i in range(NB)]
        vo = [sb(cx, tag + "vo%d" % i, [128, 512], BF16) for i in range(NB)]
        qtb, kpb, kob, vpb, vob = ([Buf() for _ in range(NB)] for _ in range(5))
        qT = [sb(cx, tag + "qT%d" % i, [128, 512], BF16) for i in range(NB)]
        kpT = [sb(cx, tag + "kpT%d" % i, [128, 512], BF16) for i in range(NB)]
        koT = [sb(cx, tag + "koT%d" % i, [128, 512], BF16) for i in range(NB)]
        qTb, kpTb, koTb = ([Buf() for _ in range(NB)] for _ in range(3))
        PT = [sb(cx, tag + "PT%d" % i, [128, 256], BF16) for i in range(4)]
        PTb = [Buf() for _ in range(4)]
        PM = [sb(cx, tag + "PM%d" % i, [128, 256], BF16) for i in range(4)]
        PMb = [Buf() for _ in range(4)]
        obf = [sb(cx, tag + "obf%d" % i, [128, SB], BF16) for i in range(2)]
        obfb = [Buf() for _ in range(2)]
        blk = 0
        hc = 0
        tc = 0
        oc = 0
        for sbi in range(T // SB):
            for g, (_, d) in enumerate(ATT_PAT):
                nper = SB // (128 * d)
                for r in range(d):
                    for nn in range(nper):
                        n = sbi * nper + nn
                        b = blk % NB
                        blk += 1
                        ts = n * 128 * d + r
                        te = ts + 127 * d + 1
                        off = ts - sbi * SB
                        has_prev = n > 0
                        c0, c1 = g * 512, (g + 1) * 512
                        S.dma("sp", qt[b][:], qkv[ts:te:d, 0, c0:c1], writes=[qtb[b]])
                        S.dma("sp", ko[b][:], qkv[ts:te:d, 1, c0:c1], writes=[kob[b]])
                        S.dma("sp", vo[b][:], qkv[ts:te:d, 2, c0:c1], writes=[vob[b]])
                        if has_prev:
                            ps_ = ts - 128 * d
                            S.dma("sp", kp[b][:], qkv[ps_:ps_ + 127 * d + 1:d, 1, c0:c1], writes=[kpb[b]])
                            S.dma("sp", vp[b][:], qkv[ps_:ps_ + 127 * d + 1:d, 2, c0:c1], writes=[vpb[b]])
                        todo = [(qt[b], qtb[b], qT[b], qTb[b]), (ko[b], kob[b], koT[b], koTb[b])]
                        if has_prev:
                            todo.append((kp[b], kpb[b], kpT[b], kpTb[b]))
                        for (src, srcb, dst, dstb) in todo:
                            pT, pTb = cx.psT[tc % 2], cx.psTb[tc % 2]
                            tc += 1
                            for hh in range(4):
                                S.op("pe", lambda e: e.transpose(pT[:, hh * 128:(hh + 1) * 128],
                                                                 src[:, hh * 128:(hh + 1) * 128], ident),
                                     reads=[srcb, cx.constb], writes=[pTb], inc=(hh == 3))
                            S.op("act", lambda e: e.copy(out=dst[:], in_=pT[:, 0:512]),
                                 reads=[pTb], writes=[dstb])
                        for hh in range(4 if AC_STAGE >= 2 else 0):
                            hs = slice(hh * 128, (hh + 1) * 128)
                            psS, psSb = cx.psA[hc % 2], cx.psAb[hc % 2]
                            psN, psNb = cx.psB[hc % 2], cx.psBb[hc % 2]
                            psD, psDb = cx.psY[hc % 2], cx.psYb[hc % 2]
                            P, Pb = PT[hc % 4], PTb[hc % 4]
                            hc += 1
                            lo = 0 if has_prev else 128
                            if has_prev:
                                S.op("pe", lambda e: e.matmul(psS[:, 0:128], kpT[b][:, hs], qT[b][:, hs],
                                                              start=True, stop=True),
                                     reads=[kpTb[b], qTb[b]], writes=[psSb], inc=False)
                            S.op("pe", lambda e: e.matmul(psS[:, 128:256], koT[b][:, hs], qT[b][:, hs],
                                                          start=True, stop=True),
                                 reads=[koTb[b], qTb[b]], writes=[psSb])
                            S.op("act", lambda e: e.activation(out=P[:, lo:256], in_=psS[:, lo:256], func=AF.Exp,
                                                               scale=scale),
                                 reads=[psSb], writes=[Pb])
                            Pe, Peb = P, Pb
                            P, Pb = PM[(hc - 1) % 4], PMb[(hc - 1) % 4]
                            S.op("dve", lambda e: e.tensor_tensor(out=P[:, lo:256], in0=Pe[:, lo:256],
                                                                  in1=amask[:, lo:256], op=ALU.mult),
                                 reads=[Peb, cx.constb], writes=[Pb])
                            if AC_STAGE < 3:
                                continue
                            if has_prev:
                                S.op("pe", lambda e: e.matmul(psN[:, 0:128], vp[b][:, hs], P[:, 0:128],
                                                              start=True, stop=False),
                                     reads=[vpb[b], Pb], writes=[psNb], inc=False)
                            S.op("pe", lambda e: e.matmul(psN[:, 0:128], vo[b][:, hs], P[:, 128:256],
                                                          start=(not has_prev), stop=True),
                                 reads=[vob[b], Pb], writes=[psNb], inc=False)
                            if has_prev:
                                S.op("pe", lambda e: e.matmul(psD[:, 0:128], ones, P[:, 0:128],
                                                              start=True, stop=False),
                                     reads=[cx.constb, Pb], writes=[psDb], inc=False)
                            S.op("pe", lambda e: e.matmul(psD[:, 0:128], ones, P[:, 128:256],
                                                          start=(not has_prev), stop=True),
                                 reads=[cx.constb, Pb], writes=[psDb])
                            if AC_STAGE < 3.2 or (AC_STAGE in (3.25, 3.27) and g > 0):
                                S.op("pe", lambda e: e.matmul(psN[:, 256:384], ones, P[:, 128:256], start=True, stop=True),
                                     reads=[cx.constb, Pb], writes=[psNb])
                                continue
                            if AC_STAGE < 3.4 and g > 0:
                                S.op("pe", lambda e: e.matmul(psN[:, 256:384], ones, P[:, 128:256], start=True, stop=True),
                                     reads=[cx.constb, Pb], writes=[psNb])
                                continue
                            nv = num[:, hh, off:off + 127 * d + 1:d]
                            dv = den[:, hh, off:off + 127 * d + 1:d]
                            if g == 0:
                                if AC_STAGE != 3.27:
                                    S.op("act", lambda e: e.copy(out=nv, in_=psN[:, 0:128]),
                                         reads=[psNb], writes=[numb[hh]])
                                if AC_STAGE != 3.25:
                                    S.op("act", lambda e: e.copy(out=dv, in_=psD[:, 0:128]),
                                         reads=[psDb], writes=[denb[hh]])
                            else:
                                S.op("dve", lambda e: e.tensor_tensor(out=nv, in0=nv, in1=psN[:, 0:128], op=ALU.add),
                                     reads=[psNb, numb[hh]], writes=[numb[hh]])
                                S.op("dve", lambda e: e.tensor_tensor(out=dv, in0=dv, in1=psD[:, 0:128], op=ALU.add),
                                     reads=[psDb, denb[hh]], writes=[denb[hh]])
            for hh in range(4 if AC_STAGE >= 4 else 0):
                o_, o_b = obf[oc % 2], obfb[oc % 2]
                oc += 1
                S.op("dve", lambda e: e.reciprocal(out=den[:, hh, :], in_=den[:, hh, :]),
                     reads=[denb[hh]], writes=[denb[hh]])
                S.op("pool", lambda e: e.tensor_tensor(out=o_[:], in0=num[:, hh, :], in1=den[:, hh, :], op=ALU.mult),
                     reads=[numb[hh], denb[hh]], writes=[o_b])
                S.dma("sp", obt[hh * 128:(hh + 1) * 128, sbi * SB:(sbi + 1) * SB], o_[:], reads=[o_b])
        cx.stack = old
        barrier(cx, S)


def phase_out(cx, S, xin, xout, oat, obt, g_ap, w_in_ap, wa_ap, wb_ap, wo_ap, T, tag):
    nc = cx.nc
    with contextlib.ExitStack() as st:
        old = cx.stack
        cx.stack = st
        WG = sb(cx, tag + "WG", [128, 8, 2048], BF16)
        WA = sb(cx, tag + "WA", [128, 8, D], BF16)
        WB = sb(cx, tag + "WB", [128, 4, D], BF16)
        WO = sb(cx, tag + "WO", [128, 8, D], BF16)
        Wb = Buf()
        gain = sb(cx, tag + "g", [128, D], F32)
        parb = Buf()
        xts = [sb(cx, tag + "xt%d" % i, [128, D], F32) for i in range(2)]
        xtb = [Buf() for _ in range(2)]
        hT = [sb(cx, tag + "hT%d" % i, [128, 8, 128], BF16) for i in range(2)]
        hTb = [Buf() for _ in range(2)]
        oaT = [sb(cx, tag + "oaT%d" % i, [128, 8, 128], BF16) for i in range(2)]
        oaTb = [Buf() for _ in range(2)]
        obT = [sb(cx, tag + "obT%d" % i, [128, 4, 128], BF16) for i in range(2)]
        obTb = [Buf() for _ in range(2)]
        sga = [sb(cx, tag + "sga%d" % i, [128, D], F32) for i in range(2)]
        sgb = [sb(cx, tag + "sgb%d" % i, [128, D], F32) for i in range(2)]
        sgab = [Buf() for _ in range(2)]
        sgbb = [Buf() for _ in range(2)]
        m1 = [sb(cx, tag + "m1%d" % i, [128, 512], F32) for i in range(2)]
        m2 = [sb(cx, tag + "m2%d" % i, [128, 512], F32) for i in range(2)]
        m1b = [Buf() for _ in range(2)]
        m2b = [Buf() for _ in range(2)]
        mg = [sb(cx, tag + "mg%d" % i, [128, D], BF16) for i in range(2)]
        mgb = [Buf() for _ in range(2)]
        mT = [sb(cx, tag + "mT%d" % i, [128, 8, 128], BF16) for i in range(2)]
        mTb = [Buf() for _ in range(2)]
        S.dma("sp", gain[:], g_ap.partition_broadcast(128), writes=[cx.constb])
        load_weight_bf16(cx, S, WG, Wb, w_in_ap, 8, P_IN, c0=8704, c1=10752, cstep=2048)
        load_weight_bf16(cx, S, WA, Wb, wa_ap, 8, D, cstep=1024)
        load_weight_bf16(cx, S, WB, Wb, wb_ap, 4, D, cstep=1024)
        load_weight_bf16(cx, S, WO, Wb, wo_ap, 8, D, cstep=1024)
        oat_v = oat.rearrange("(c p) t -> p c t", p=128)
        obt_v = obt.rearrange("(c p) t -> p c t", p=128)
        nt = T // 128
        def stage_X(ti):
            t0 = ti * 128
            i2 = ti % 2
            xt, xb = xts[i2], xtb[i2]
            S.dma("sp", xt[:], xin[t0:t0 + 128, :], writes=[xb])
            S.dma("sp", oaT[i2][:], oat_v[:, :, t0:t0 + 128], writes=[oaTb[i2]])
            S.dma("sp", obT[i2][:], obt_v[:, :, t0:t0 + 128], writes=[obTb[i2]])
            emit_norm_T(cx, S, xt, xb, gain, hT[i2], hTb[i2], 0, tag)
            for n in range(2):
                for (pt, ptb, base, dst, dstb) in ((cx.psA[n], cx.psAb[n], 0, sga[i2], sgab[i2]),
                                                   (cx.psB[n], cx.psBb[n], 1024, sgb[i2], sgbb[i2])):
                    for k in range(8):
                        S.op("pe", lambda e, k=k: e.matmul(pt[:], hT[i2][:, k, :],
                                                           WG[:, k, base + n * 512:base + (n + 1) * 512],
                                                           start=(k == 0), stop=(k == 7)),
                             reads=[hTb[i2], Wb], writes=[ptb], inc=(k == 7))
                    S.op("act", lambda e: e.activation(out=dst[:, n * 512:(n + 1) * 512], in_=pt[:],
                                                       func=AF.Sigmoid),
                         reads=[ptb], writes=[dstb])
            for n in range(2):
                ns = slice(n * 512, (n + 1) * 512)
                pa, pab = cx.psY[0], cx.psYb[0]
                pb_, pbb = cx.psY[1], cx.psYb[1]
                for c in range(8):
                    S.op("pe", lambda e, c=c: e.matmul(pa[:], oaT[i2][:, c, :], WA[:, c, ns],
                                                       start=(c == 0), stop=(c == 7)),
                         reads=[oaTb[i2], Wb], writes=[pab], inc=(c == 7))
                for c in range(4):
                    S.op("pe", lambda e, c=c: e.matmul(pb_[:], obT[i2][:, c, :], WB[:, c, ns],
                                                       start=(c == 0), stop=(c == 3)),
                         reads=[obTb[i2], Wb], writes=[pbb], inc=(c == 3))
                S.op("dve", lambda e: e.tensor_tensor(out=m1[n][:], in0=sga[i2][:, ns], in1=pa[:], op=ALU.mult),
                     reads=[sgab[i2], pab], writes=[m1b[n]])
                S.op("dve", lambda e: e.tensor_tensor(out=m2[n][:], in0=sgb[i2][:, ns], in1=pb_[:], op=ALU.mult),
                     reads=[sgbb[i2], pbb], writes=[m2b[n]])
                S.op("dve", lambda e: e.tensor_tensor(out=mg[i2][:, ns], in0=m1[n][:], in1=m2[n][:], op=ALU.add),
                     reads=[m1b[n], m2b[n]], writes=[mgb[i2]])

        def stage_Y(ti):
            t0 = ti * 128
            i2 = ti % 2
            xt, xb = xts[i2], xtb[i2]
            pT, pTb = cx.psT[ti % 2], cx.psTb[ti % 2]
            for c in range(8):
                S.op("pe", lambda e, c=c: e.transpose(pT[:, c * 128:(c + 1) * 128],
                                                      mg[i2][:, c * 128:(c + 1) * 128], cx.cbf[:, 0:128]),
                     reads=[mgb[i2], cx.constb], writes=[pTb], inc=(c == 7))
            S.op("act", lambda e: e.copy(out=mT[i2][:], in_=pT[:].rearrange("p (k t) -> p k t", k=8)),
                 reads=[pTb], writes=[mTb[i2]])
            for n in range(2):
                ns = slice(n * 512, (n + 1) * 512)
                pt, ptb = cx.psA[n], cx.psAb[n]
                for c in range(8):
                    S.op("pe", lambda e, c=c: e.matmul(pt[:], mT[i2][:, c, :], WO[:, c, ns],
                                                       start=(c == 0), stop=(c == 7)),
                         reads=[mTb[i2], Wb], writes=[ptb], inc=(c == 7))
                S.op("dve", lambda e: e.tensor_tensor(out=xt[:, ns], in0=xt[:, ns], in1=pt[:], op=ALU.add),
                     reads=[ptb, xb], writes=[xb])
            S.dma("sp", xout[t0:t0 + 128, :], xt[:], reads=[xb])
        stage_X(0)
        for ti in range(nt):
            if ti + 1 < nt:
                stage_X(ti + 1)
            stage_Y(ti)
        cx.stack = old
        barrier(cx, S)


def make_ctx(nc, stack):
    cx = Ctx()
    cx.nc = nc
    cx.stack = stack
    cx.rr = 0
    S = Sched(nc, stack)
    cx.cbf = sb(cx, "cbf", [128, 640], BF16)
    cx.rmask = sb(cx, "rmask", [128, 128], F32)
    cx.cmask = sb(cx, "cmask", [128, 4], F32)
    cx.bmask4 = sb(cx, "bmask4", [128, 512], BF16)
    cx.rmask4 = sb(cx, "rmask4", [128, 512], F32)
    cx.ident = cx.cbf[:, 0:128]
    cx.constb = Buf()
    cx.junk = [sb(cx, "junk%d" % i, [128, D], BF16) for i in range(2)]
    cx.junkb = [Buf() for _ in range(2)]
    cx.ss = [sb(cx, "ss%d" % i, [128, 4], F32) for i in range(2)]
    cx.ssb = [Buf() for _ in range(2)]
    cx.hbf = [sb(cx, "hbf%d" % i, [128, D], BF16) for i in range(2)]
    cx.hbfb = [Buf() for _ in range(2)]
    cx.psT = [ps(cx, "psT%d" % i, [128, 1024], BF16) for i in range(2)]
    cx.psTb = [Buf() for _ in range(2)]
    cx.psA = [ps(cx, "psA%d" % i, [128, 512], F32) for i in range(2)]
    cx.psAb = [Buf() for _ in range(2)]
    cx.psB = [ps(cx, "psB%d" % i, [128, 512], F32) for i in range(2)]
    cx.psBb = [Buf() for _ in range(2)]
    cx.psY = [ps(cx, "psY%d" % i, [128, 512], F32) for i in range(2)]
    cx.psYb = [Buf() for _ in range(2)]
    return cx, S


def build_nc(T, depth=2, phases=None, debug=False):
    nc = bass.Bass("TRN2", target_bir_lowering=False)
    dt = nc.dram_tensor
    x = dt("x", [T, D], F32, kind="ExternalInput").ap()
    cst = dt("cst", [128, 1796], F32, kind="ExternalInput").ap()
    cos4 = dt("cos4", [T, 512], F32, kind="ExternalInput").ap()
    sin4 = dt("sin4", [T, 512], F32, kind="ExternalInput").ap()
    f1n = dt("ffn1_norm", [depth, D], F32, kind="ExternalInput").ap()
    f1i = dt("ffn1_w_in", [depth, D, 2 * DFF], F32, kind="ExternalInput").ap()
    f1o = dt("ffn1_w_out", [depth, DFF, D], F32, kind="ExternalInput").ap()
    mn = dt("mix_norm", [depth, D], F32, kind="ExternalInput").ap()
    win = dt("w_in", [depth, D, P_IN], F32, kind="ExternalInput").ap()
    lgT = dt("lgT", [128, depth, 8], F32, kind="ExternalInput").ap()
    gnT = dt("gnT", [depth, 128, 8], F32, kind="ExternalInput").ap()
    aqn = dt("attn_q_norm", [depth, 3, 128], F32, kind="ExternalInput").ap()
    akn = dt("attn_k_norm", [depth, 3, 128], F32, kind="ExternalInput").ap()
    wa = dt("w_branch_a", [depth, D, D], F32, kind="ExternalInput").ap()
    wb = dt("w_branch_b", [depth, 512, D], F32, kind="ExternalInput").ap()
    wo = dt("w_out", [depth, D, D], F32, kind="ExternalInput").ap()
    f2n = dt("ffn2_norm", [depth, D], F32, kind="ExternalInput").ap()
    f2i = dt("ffn2_w_in", [depth, D, 2 * DFF], F32, kind="ExternalInput").ap()
    f2o = dt("ffn2_w_out", [depth, DFF, D], F32, kind="ExternalInput").ap()
    y = dt("y", [T, D], F32, kind="ExternalOutput").ap()
    kind = "ExternalOutput" if debug else "Internal"
    xa = dt("xa", [T, D], F32, kind=kind).ap()
    xb_ = dt("xb", [T, D], F32, kind=kind).ap()
    oat = dt("oat", [D, T], BF16, kind=kind).ap()
    obt = dt("obt", [512, T], BF16, kind=kind).ap()
    qkv = dt("qkv", [T, 3, 1536], BF16, kind=kind).ap()
    with contextlib.ExitStack() as stack:
        cx, S = make_ctx(nc, stack)
        load_consts(cx, S, cst)
        cur = x
        for l in range(depth):
            last = (l == depth - 1)
            tg = "L%d" % l
            P = phases
            if P is None or "f1" in P:
                phase_ffn(cx, S, cur, xa, f1n[l:l + 1, :], f1i[l], f1o[l], T, tg + "f1")
            if P is None or "hg" in P:
                phase_hgrn(cx, S, xa, oat, mn[l:l + 1, :], win[l], lgT, gnT[l], T, l, tg + "hg")
            if P is None or "ap" in P:
                phase_attn_proj(cx, S, xa, qkv, mn[l:l + 1, :], win[l], aqn[l], akn[l], cos4, sin4, T, tg + "ap")
            if P is None or "ac" in P:
                phase_attn_core(cx, S, qkv, obt, T, tg + "ac")
            if P is None or "po" in P:
                phase_out(cx, S, xa, xb_, oat, obt, mn[l:l + 1, :], win[l], wa[l], wb[l], wo[l], T, tg + "po")
            if P is None or "f2" in P:
                phase_ffn(cx, S, xb_, y if last else xa, f2n[l:l + 1, :], f2i[l], f2o[l], T, tg + "f2")
            cur = xa
        barrier(cx, S)
        cx.ninst = S.ninst
        print("ninst", S.ninst, {k: v for k, v in S.cnt.items()})
    return nc


def rope_host(T):
    pos = np.arange(T, dtype=np.float32)
    inv = (np.float32(10000.0) ** (-np.arange(0, 128, 2, dtype=np.float32) / np.float32(128))).astype(np.float32)
    ang = pos[:, None] * inv[None, :]
    ang = np.concatenate([ang, ang], axis=-1).astype(np.float32)
    cos = np.cos(ang).astype(np.float32)
    sin = np.sin(ang).astype(np.float32)
    sin[:, :64] = -sin[:, :64]
    return np.tile(cos, (1, 4)), np.tile(sin, (1, 4))


def make_in_maps(inputs, T, depth=2):
    cos4, sin4 = rope_host(T)
    base = {k: np.ascontiguousarray(v, dtype=np.float32) for k, v in inputs.items() if k not in ("x", "hgrn_lb_logits", "hgrn_out_norm")}
    base["cst"] = host_consts()
    base["cos4"] = np.ascontiguousarray(cos4)
    base["sin4"] = np.ascontiguousarray(sin4)
    lg = np.asarray(inputs["hgrn_lb_logits"], np.float32)
    base["lgT"] = np.ascontiguousarray(lg.reshape(depth, 8, 128).transpose(2, 0, 1))
    gn = np.asarray(inputs["hgrn_out_norm"], np.float32)
    base["gnT"] = np.ascontiguousarray(gn.reshape(depth, 8, 128).transpose(0, 2, 1))
    x = np.asarray(inputs["x"], np.float32)
    maps = []
    for c in range(NCORES):
        m = dict(base)
        m["x"] = np.ascontiguousarray(x[c % x.shape[0], :T])
        maps.append(m)
    return maps


def kernel(**inputs):
    x = np.asarray(inputs["x"])
    B, T, _ = x.shape
    nc = build_nc(T, depth=2)
    maps = make_in_maps(inputs, T, depth=2)
    res = run_bass_kernel_spmd(nc, maps, core_ids=list(range(NCORES)))
    out = np.stack([np.asarray(res.results[b]["y"], dtype=np.float32) for b in range(B)], axis=0)
    return out


def build_ac_only(T):
    nc = bass.Bass("TRN2", target_bir_lowering=False)
    dt = nc.dram_tensor
    cst = dt("cst", [128, 1796], F32, kind="ExternalInput").ap()
    qkv = dt("qkv", [T, 3, 1536], BF16, kind="ExternalInput").ap()
    obt = dt("obt", [512, T], BF16, kind="ExternalOutput").ap()
    with contextlib.ExitStack() as stack:
        cx, S = make_ctx(nc, stack)
        load_consts(cx, S, cst)
        phase_attn_core(cx, S, qkv, obt, T, "ac")
        barrier(cx, S)
    return nc
```
